# Optimizing a Trainium2 kernel written in Bass

```python
import jax, jax.numpy as jnp
from jax import lax
import numpy as np

D_MODEL = 1024
BATCH = 2
SEQ = 16384
DEPTH = 2

D_NSA = 512
D_LRU = 512
D_CONV = 512
D_MIX = D_NSA + D_LRU + D_CONV
HEAD_DIM = 64
N_HEADS = D_NSA // HEAD_DIM
N_KV = 2
N_REP = N_HEADS // N_KV
KV_W = N_KV * HEAD_DIM
L_CMP = 32
S_CMP = 16
CMP_HIDDEN = 256
L_SLC = 64
K_SEL = 16
WINDOW = 512
Q_BLOCK = 128
N_BRANCH = 3
ROPE_THETA = 10000.0
LRU_BLOCKS = 8
LRU_BLOCK_W = D_LRU // LRU_BLOCKS
LRU_CONV = 4
LRU_C = 8.0
CONV_K = 31
EPS = 1e-6
NEG = -1e30
FORCE = 1e4
COL_SIZES = (D_NSA, KV_W, KV_W, KV_W, KV_W, KV_W, KV_W, N_HEADS * N_BRANCH,
             D_NSA, D_LRU, D_LRU, D_CONV, D_CONV, D_CONV)
N_IN = D_NSA * 2 + KV_W * 6 + N_HEADS * N_BRANCH + D_LRU * 2 + D_CONV * 3

kernel_name = "hybrid_nsa_rglru_conformer_parallel"


def split_points():
    return [int(v) for v in np.cumsum(COL_SIZES)[:-1]]


def rms_norm(x, g):
    xf = x.astype(jnp.float32)
    y = xf * lax.rsqrt(jnp.mean(xf * xf, axis=-1, keepdims=True) + EPS)
    return (y * g.astype(jnp.float32)).astype(x.dtype)


def rope_tables(positions):
    inv_freq = ROPE_THETA ** (-jnp.arange(0, HEAD_DIM, 2, dtype=jnp.float32) / HEAD_DIM)
    ang = positions.astype(jnp.float32)[..., None] * inv_freq
    return jnp.cos(ang)[:, :, None, :], jnp.sin(ang)[:, :, None, :]


def apply_rope(u, cos, sin):
    uf = u.astype(jnp.float32)
    u1, u2 = jnp.split(uf, 2, axis=-1)
    out = jnp.concatenate([u1 * cos - u2 * sin, u2 * cos + u1 * sin], axis=-1)
    return out.astype(u.dtype)


def causal_dwconv(u, w, b):
    k = w.shape[0]
    c = u.shape[-1]
    y = lax.conv_general_dilated(u, w[:, None, :].astype(u.dtype), window_strides=(1,),
                                 padding=[(k - 1, 0)], dimension_numbers=('NWC', 'WIO', 'NWC'),
                                 feature_group_count=c)
    return y + b


def nsa_attention(q, k_c, v_c, k_s, v_s, k_w, v_w, gates, cmp_pos, cmp_w1, cmp_w2):
    B, T = q.shape[0], q.shape[1]
    qg = q.reshape(B, T, N_KV, N_REP, HEAD_DIM).transpose(0, 2, 3, 1, 4)
    tr = lambda a: a.transpose(0, 2, 1, 3)
    k_c, v_c, k_s, v_s, k_w, v_w = tr(k_c), tr(v_c), tr(k_s), tr(v_s), tr(k_w), tr(v_w)

    n_cmp = (T - L_CMP) // S_CMP + 1
    cmp_start = jnp.arange(n_cmp) * S_CMP
    idx = cmp_start[:, None] + jnp.arange(L_CMP)[None, :]

    def compress(kv, pos, w1, w2):
        blk = (kv[:, :, idx] + pos).reshape(B, N_KV, n_cmp, L_CMP * HEAD_DIM)
        return jax.nn.silu(blk @ w1) @ w2

    kc = compress(k_c, cmp_pos[0], cmp_w1[0], cmp_w2[0])
    vc = compress(v_c, cmp_pos[1], cmp_w1[1], cmp_w2[1])
    cmp_end = cmp_start + L_CMP - 1

    n_sel = T // L_SLC
    k_sel = min(K_SEL, n_sel)
    sel_start = jnp.arange(n_sel) * L_SLC
    overlap = ((cmp_start[:, None] < sel_start[None, :] + L_SLC) &
               (cmp_start[:, None] + L_CMP > sel_start[None, :])).astype(jnp.float32)
    ks_blk = k_s.reshape(B, N_KV, n_sel, L_SLC, HEAD_DIM)
    vs_blk = v_s.reshape(B, N_KV, n_sel, L_SLC, HEAD_DIM)
    bi = jnp.arange(B)[:, None, None, None]
    gi = jnp.arange(N_KV)[None, :, None, None]
    sel_j = jnp.arange(n_sel)

    pad = ((0, 0), (0, 0), (WINDOW, 0), (0, 0))
    kw_pad = jnp.pad(k_w, pad)
    vw_pad = jnp.pad(v_w, pad)
    scale = HEAD_DIM ** -0.5

    def block(i):
        s = i * Q_BLOCK
        qb = lax.dynamic_slice_in_dim(qg, s, Q_BLOCK, axis=3)
        t = s + jnp.arange(Q_BLOCK)
        sc = jnp.einsum('bgrqd,bgnd->bgrqn', qb, kc).astype(jnp.float32) * scale
        mc = cmp_end[None, :] <= t[:, None]
        pc = jnp.where(mc, jax.nn.softmax(jnp.where(mc, sc, NEG), axis=-1), 0.0)
        o_c = jnp.einsum('bgrqn,bgnd->bgrqd', pc.astype(vc.dtype), vc)
        imp = jnp.einsum('bgrqn,ns->bgqs', pc, overlap)
        cur = t // L_SLC
        forced = ((sel_j[None, :] == 0) | (sel_j[None, :] == cur[:, None]) |
                  (sel_j[None, :] == cur[:, None] - 1))
        valid = sel_start[None, :] <= t[:, None]
        score = jnp.where(forced, FORCE, jnp.where(valid, imp, -1.0))
        top_v, top_i = lax.top_k(score, k_sel)
        ok = top_v >= 0.0
        kg = ks_blk[bi, gi, top_i]
        vg = vs_blk[bi, gi, top_i]
        ss = jnp.einsum('bgrqd,bgqkld->bgrqkl', qb, kg).astype(jnp.float32) * scale
        tok = top_i[..., None] * L_SLC + jnp.arange(L_SLC)
        ms = (ok[..., None] & (tok <= t[None, None, :, None, None]))[:, :, None]
        ss = jnp.where(ms, ss, NEG).reshape(B, N_KV, N_REP, Q_BLOCK, k_sel * L_SLC)
        ps = jax.nn.softmax(ss, axis=-1).reshape(B, N_KV, N_REP, Q_BLOCK, k_sel, L_SLC)
        o_s = jnp.einsum('bgrqkl,bgqkld->bgrqd', ps.astype(vg.dtype), vg)
        kw = lax.dynamic_slice_in_dim(kw_pad, s, WINDOW + Q_BLOCK, axis=2)
        vw = lax.dynamic_slice_in_dim(vw_pad, s, WINDOW + Q_BLOCK, axis=2)
        pos = s - WINDOW + jnp.arange(WINDOW + Q_BLOCK)
        mw = ((pos[None, :] <= t[:, None]) & (pos[None, :] > t[:, None] - WINDOW) &
              (pos[None, :] >= 0))
        sw = jnp.einsum('bgrqd,bgkd->bgrqk', qb, kw).astype(jnp.float32) * scale
        pw = jax.nn.softmax(jnp.where(mw, sw, NEG), axis=-1)
        o_w = jnp.einsum('bgrqk,bgkd->bgrqd', pw.astype(vw.dtype), vw)
        return o_c, o_s, o_w

    o_c, o_s, o_w = lax.map(block, jnp.arange(T // Q_BLOCK))

    def unblock(o):
        return o.transpose(1, 0, 4, 2, 3, 5).reshape(B, T, N_KV, N_REP, HEAD_DIM)

    g = gates.reshape(B, T, N_KV, N_REP, N_BRANCH)
    out = g[..., 0:1] * unblock(o_c) + g[..., 1:2] * unblock(o_s) + g[..., 2:3] * unblock(o_w)
    return out.reshape(B, T, D_NSA)


def rg_lru(u, conv_w, conv_b, w_a, b_a, w_i, b_i, lam):
    B, T, C = u.shape
    xc = causal_dwconv(u, conv_w, conv_b)
    xh = xc.reshape(B, T, LRU_BLOCKS, LRU_BLOCK_W)
    r = jax.nn.sigmoid(jnp.einsum('bthi,hij->bthj', xh, w_a).reshape(B, T, C) + b_a)
    ig = jax.nn.sigmoid(jnp.einsum('bthi,hij->bthj', xh, w_i).reshape(B, T, C) + b_i)
    log_a = -LRU_C * r.astype(jnp.float32) * jax.nn.softplus(-lam.astype(jnp.float32))
    a = jnp.exp(log_a)
    bterm = jnp.sqrt(-jnp.expm1(2.0 * log_a)) * (ig * xc).astype(jnp.float32)

    def combine(left, right):
        a1, b1 = left
        a2, b2 = right
        return a1 * a2, a2 * b1 + b2

    _, h = lax.associative_scan(combine, (a, bterm), axis=1)
    return h.astype(u.dtype)


def conformer_conv(ga, gb, dw_w, dw_b, ln_g, ln_b, w_pw2):
    u = ga * jax.nn.sigmoid(gb)
    u = causal_dwconv(u, dw_w, dw_b)
    uf = u.astype(jnp.float32)
    mu = jnp.mean(uf, axis=-1, keepdims=True)
    var = jnp.mean(jnp.square(uf - mu), axis=-1, keepdims=True)
    un = (uf - mu) * lax.rsqrt(var + EPS) * ln_g.astype(jnp.float32) + ln_b.astype(jnp.float32)
    return jax.nn.silu(un).astype(u.dtype) @ w_pw2


def setup_inputs(seed: int = 0) -> dict:
    key = jax.random.key(seed)
    ks = jax.random.split(key, 24)
    f32 = jnp.float32
    nrm = lambda k, shape, s: jax.random.normal(k, shape, f32) * s
    u = jax.random.uniform(ks[13], (DEPTH, D_LRU), f32, 0.9, 0.999)
    sg = u ** (1.0 / LRU_C)
    lam = jnp.log(sg / (1.0 - sg))
    return {
        "x": jax.random.normal(ks[0], (BATCH, SEQ, D_MODEL), f32),
        "positions": jnp.broadcast_to(jnp.arange(SEQ, dtype=jnp.int32), (BATCH, SEQ)),
        "norm_g": 1.0 + nrm(ks[1], (DEPTH, D_MODEL), 0.02),
        "w_in": nrm(ks[2], (DEPTH, D_MODEL, N_IN), D_MODEL ** -0.5),
        "b_gate": nrm(ks[3], (DEPTH, N_HEADS * N_BRANCH), 0.02),
        "cmp_pos": nrm(ks[4], (DEPTH, 2, L_CMP, HEAD_DIM), 0.02),
        "cmp_w1": nrm(ks[5], (DEPTH, 2, L_CMP * HEAD_DIM, CMP_HIDDEN), (L_CMP * HEAD_DIM) ** -0.5),
        "cmp_w2": nrm(ks[6], (DEPTH, 2, CMP_HIDDEN, HEAD_DIM), CMP_HIDDEN ** -0.5),
        "lru_conv_w": nrm(ks[7], (DEPTH, LRU_CONV, D_LRU), LRU_CONV ** -0.5),
        "lru_conv_b": nrm(ks[8], (DEPTH, D_LRU), 0.02),
        "lru_w_a": nrm(ks[9], (DEPTH, LRU_BLOCKS, LRU_BLOCK_W, LRU_BLOCK_W), LRU_BLOCK_W ** -0.5),
        "lru_b_a": nrm(ks[10], (DEPTH, D_LRU), 0.02),
        "lru_w_i": nrm(ks[11], (DEPTH, LRU_BLOCKS, LRU_BLOCK_W, LRU_BLOCK_W), LRU_BLOCK_W ** -0.5),
        "lru_b_i": nrm(ks[12], (DEPTH, D_LRU), 0.02),
        "lru_lam": lam,
        "conv_dw_w": nrm(ks[14], (DEPTH, CONV_K, D_CONV), CONV_K ** -0.5),
        "conv_dw_b": nrm(ks[15], (DEPTH, D_CONV), 0.02),
        "conv_ln_g": 1.0 + nrm(ks[16], (DEPTH, D_CONV), 0.02),
        "conv_ln_b": nrm(ks[17], (DEPTH, D_CONV), 0.02),
        "conv_w_pw2": nrm(ks[18], (DEPTH, D_CONV, D_CONV), D_CONV ** -0.5),
        "w_out": nrm(ks[19], (DEPTH, D_MIX, D_MODEL), D_MIX ** -0.5),
        "final_g": 1.0 + nrm(ks[20], (D_MODEL,), 0.02),
    }


def reference(x, positions, norm_g, w_in, b_gate, cmp_pos, cmp_w1, cmp_w2, lru_conv_w, lru_conv_b,
              lru_w_a, lru_b_a, lru_w_i, lru_b_i, lru_lam, conv_dw_w, conv_dw_b, conv_ln_g, conv_ln_b,
              conv_w_pw2, w_out, final_g):
    B, T = x.shape[0], x.shape[1]
    cos, sin = rope_tables(positions)
    for l in range(DEPTH):
        h = rms_norm(x, norm_g[l])
        proj = h @ w_in[l]
        (q, k_c, v_c, k_s, v_s, k_w, v_w, g_lin, z_nsa, u_lru, z_lru, glu_a, glu_b,
         z_conv) = jnp.split(proj, split_points(), axis=-1)
        kvr = lambda a: a.reshape(B, T, N_KV, HEAD_DIM)
        q = apply_rope(q.reshape(B, T, N_HEADS, HEAD_DIM), cos, sin)
        k_c = apply_rope(kvr(k_c), cos, sin)
        k_s = apply_rope(kvr(k_s), cos, sin)
        k_w = apply_rope(kvr(k_w), cos, sin)
        gates = jax.nn.sigmoid(g_lin + b_gate[l]).reshape(B, T, N_HEADS, N_BRANCH)
        y_nsa = nsa_attention(q, k_c, kvr(v_c), k_s, kvr(v_s), k_w, kvr(v_w), gates,
                              cmp_pos[l], cmp_w1[l], cmp_w2[l])
        y_lru = rg_lru(u_lru, lru_conv_w[l], lru_conv_b[l], lru_w_a[l], lru_b_a[l],
                       lru_w_i[l], lru_b_i[l], lru_lam[l])
        y_conv = conformer_conv(glu_a, glu_b, conv_dw_w[l], conv_dw_b[l], conv_ln_g[l],
                                conv_ln_b[l], conv_w_pw2[l])
        mixed = jnp.concatenate([y_nsa * jax.nn.silu(z_nsa), y_lru * jax.nn.silu(z_lru),
                                 y_conv * jax.nn.silu(z_conv)], axis=-1)
        x = x + mixed @ w_out[l]
    return rms_norm(x, final_g)
```

```python
import numpy as np
import ml_dtypes
from contextlib import ExitStack
import concourse.bass as bass
import concourse.mybir as mybir
from concourse.bass_utils import run_bass_kernel_spmd

F32 = mybir.dt.float32
BF16 = mybir.dt.bfloat16
I32 = mybir.dt.int32
AF = mybir.ActivationFunctionType
ALU = mybir.AluOpType
AX = mybir.AxisListType
NPBF = ml_dtypes.bfloat16

D_MODEL = 1024
BATCH = 2
SEQ = 16384
DEPTH = 2
N_IN = 4376
EPS = 1e-6
NCORES = 8
TWO_PI = 6.283185307179586


class TT:
    def __init__(self, h):
        self.h = h
        self.lw = None
        self.rd = {}

    def __getitem__(self, idx):
        return self.h[idx]


class K:
    NDS = 12

    def __init__(self):
        self.nc = bass.Bass("TRN2", target_bir_lowering=False)
        self.es = ExitStack()
        nc = self.nc
        self.eng = {"pe": nc.tensor, "act": nc.scalar, "dve": nc.vector,
                    "pool": nc.gpsimd, "sp": nc.sync}
        self.sem = {}
        self.cnt = {}
        for e in self.eng:
            self.sem[e] = self.es.enter_context(nc.semaphore("s_" + e))
            self.cnt[e] = 0
        for i in range(self.NDS):
            k = ("d", i)
            self.sem[k] = self.es.enter_context(nc.semaphore("s_d%d" % i))
            self.cnt[k] = 0
        self.waited = {e: {} for e in self.eng}
        self.dma_rr = 0
        self.n_alloc = 0
        self.inherit = {}
        self.scope_tts = None

    def sb(self, shape, dt, name=None):
        self.n_alloc += 1
        name = name or ("t%d" % self.n_alloc)
        t = TT(self.es.enter_context(self.nc.sbuf_tensor(name, list(shape), dt)))
        t.rd = dict(self.inherit)
        if self.scope_tts is not None:
            self.scope_tts.append(t)
        return t

    def scope_begin(self):
        self.outer_es = self.es
        self.es = ExitStack()
        self.scope_tts = []

    def scope_end(self):
        for t in self.scope_tts:
            for (kk, v) in ([t.lw] if t.lw else []) + list(t.rd.items()):
                if self.inherit.get(kk, 0) < v:
                    self.inherit[kk] = v
        self.scope_tts = None
        self.es.close()
        self.es = self.outer_es

    def ps(self, shape, dt=F32, name=None):
        self.n_alloc += 1
        name = name or ("p%d" % self.n_alloc)
        return TT(self.es.enter_context(self.nc.psum_tensor(name, list(shape), dt)))

    def din(self, name, shape, dt):
        return self.nc.dram_tensor(name, list(shape), dt, kind="ExternalInput").ap()

    def dout(self, name, shape, dt):
        return self.nc.dram_tensor(name, list(shape), dt, kind="ExternalOutput").ap()

    def _wait(self, e, deps):
        for (k, v) in deps:
            if k == e and e == "pe":
                continue
            if self.waited[e].get(k, 0) >= v:
                continue
            self.eng[e].wait_ge(self.sem[k], v)
            self.waited[e][k] = v

    def _deps(self, r, w):
        deps = []
        for t in r:
            if t.lw:
                deps.append(t.lw)
        for t in w:
            if t.lw:
                deps.append(t.lw)
            deps += list(t.rd.items())
        return deps

    def op(self, e, fn, r=(), w=()):
        self._wait(e, self._deps(r, w))
        inst = fn(self.eng[e])
        self.cnt[e] += 1
        inst.then_inc(self.sem[e], 1)
        c = self.cnt[e]
        for t in r:
            t.rd[e] = c
        for t in w:
            t.lw = (e, c)
            t.rd = {}

    def dma(self, out_ap, in_ap, r=(), w=(), q="sp"):
        k = ("d", self.dma_rr % self.NDS)
        self.dma_rr += 1
        deps = self._deps(r, w)
        if self.cnt[k]:
            deps.append((k, self.cnt[k]))
        self._wait(q, deps)
        self.eng[q].dma_start(out=out_ap, in_=in_ap).then_inc(self.sem[k], 16)
        self.cnt[k] += 16
        c = self.cnt[k]
        for t in r:
            t.rd[k] = c
        for t in w:
            t.lw = (k, c)
            t.rd = {}

    def finish(self):
        for i in range(self.NDS):
            k = ("d", i)
            if self.cnt[k]:
                self.eng["sp"].wait_ge(self.sem[k], self.cnt[k])
        self.es.close()


TA = 4096


def build_A():
    k = K()
    xT = k.din("xT", [D_MODEL, TA], F32)
    w_in = k.din("w_in", [D_MODEL, N_IN], F32)
    g8 = k.din("g8", [128, 8], F32)
    pos = k.din("pos", [128, TA // 128], I32)
    invf = k.din("invf", [128, 32], F32)
    o_qkv = k.dout("o_qkv", [TA, 1280], BF16)
    o_g = k.dout("o_g", [TA, 24], F32)
    o_r = k.dout("o_r", [TA, 3072], F32)
    NT = TA // 128

    Wb = k.sb([128, 8, N_IN], BF16)
    S = [k.sb([128, N_IN], F32) for _ in range(2)]
    xg = k.sb([128, 8, 512], F32)
    xsq = k.sb([128, 8, 512], F32)
    hT = [k.sb([128, 8, 512], BF16) for _ in range(2)]
    Q = [k.sb([128, 1280], BF16) for _ in range(2)]
    g_sb = k.sb([128, 8], F32)
    pos_i = k.sb([128, NT], I32)
    pos_f = k.sb([128, NT], F32)
    invf_sb = k.sb([128, 32], F32)
    ang = k.sb([128, NT, 32], F32)
    ki = k.sb([128, NT, 32], I32)
    kf = k.sb([128, NT, 32], F32)
    red = k.sb([128, NT, 32], F32)
    cos_t = k.sb([128, NT, 32], F32)
    sin_t = k.sb([128, NT, 32], F32)
    ones = k.sb([128, 1], F32)
    rstd = k.sb([128, 4], F32)
    lnv = k.sb([128, 4], F32)
    tmp = [k.sb([128, 8, 32], F32) for _ in range(4)]
    pss = k.ps([128, 4], F32)
    PB = [k.ps([128, 512], F32) for _ in range(6)]

    k.dma(g_sb[:], g8[:, :], w=[g_sb])
    k.dma(pos_i[:], pos[:, :], w=[pos_i])
    k.dma(invf_sb[:], invf[:, :], w=[invf_sb])
    k.op("pool", lambda e: e.memset(ones[:], 1.0), w=[ones])
    for c in range(8):
        st = S[c % 2]
        k.dma(st[:], w_in[c * 128:(c + 1) * 128, :], w=[st])
        if c % 2 == 0:
            k.op("act", lambda e: e.copy(out=Wb[:, c, :], in_=st[:]), r=[st], w=[Wb])
        else:
            k.op("dve", lambda e: e.tensor_copy(out=Wb[:, c, :], in_=st[:]), r=[st], w=[Wb])
    k.op("dve", lambda e: e.tensor_copy(out=pos_f[:], in_=pos_i[:]), r=[pos_i], w=[pos_f])
    k.op("dve", lambda e: e.tensor_tensor(
        out=ang[:], in0=pos_f[:].unsqueeze(2).to_broadcast([128, NT, 32]),
        in1=invf_sb[:].unsqueeze(1).to_broadcast([128, NT, 32]), op=ALU.mult),
        r=[pos_f, invf_sb], w=[ang])
    for which, dst in ((0, sin_t), (1, cos_t)):
        if which == 1:
            k.op("dve", lambda e: e.tensor_scalar(out=ang[:], in0=ang[:], scalar1=float(np.pi / 2),
                                                  scalar2=None, op0=ALU.add), r=[ang], w=[ang])
        k.op("dve", lambda e: e.tensor_scalar(out=ki[:], in0=ang[:], scalar1=float(1.0 / TWO_PI),
                                              scalar2=None, op0=ALU.mult), r=[ang], w=[ki])
        k.op("dve", lambda e: e.tensor_copy(out=kf[:], in_=ki[:]), r=[ki], w=[kf])
        k.op("dve", lambda e: e.scalar_tensor_tensor(out=red[:], in0=kf[:], scalar=float(-TWO_PI),
                                                     in1=ang[:], op0=ALU.mult, op1=ALU.add),
             r=[kf, ang], w=[red])
        k.op("dve", lambda e: e.tensor_scalar(out=red[:], in0=red[:], scalar1=-3.1415925,
                                              scalar2=3.1415925, op0=ALU.max, op1=ALU.min),
             r=[red], w=[red])
        k.op("act", lambda e: e.activation(out=dst[:], in_=red[:], func=AF.Sin), r=[red], w=[dst])

    pb_i = 0
    segs = [(0, 8), (512, 2), (768, 2), (1024, 2)]
    for gi in range(TA // 512):
        h = hT[gi % 2]
        k.dma(xg[:], xT[:, gi * 512:(gi + 1) * 512].rearrange("(c p) t -> p c t", p=128), w=[xg])
        k.op("pool", lambda e: e.tensor_tensor(out=xsq[:], in0=xg[:], in1=xg[:], op=ALU.mult),
             r=[xg], w=[xsq])
        for c in range(8):
            k.op("dve", lambda e: e.tensor_scalar(out=h[:, c, :], in0=xg[:, c, :],
                                                  scalar1=g_sb[:, c:c + 1], scalar2=None,
                                                  op0=ALU.mult), r=[xg, g_sb], w=[h])
        for sub in range(4):
            for c in range(8):
                k.op("pe", lambda e: e.matmul(pss[:, sub:sub + 1],
                                              lhsT=xsq[:, c, sub * 128:(sub + 1) * 128],
                                              rhs=ones[:, 0:1], start=(c == 0), stop=(c == 7)),
                     r=[xsq, ones], w=[pss])
        k.op("dve", lambda e: e.tensor_scalar(out=lnv[:], in0=pss[:], scalar1=1.0 / D_MODEL,
                                              scalar2=EPS, op0=ALU.mult, op1=ALU.add),
             r=[pss], w=[lnv])
        k.op("act", lambda e: e.activation(out=lnv[:], in_=lnv[:], func=AF.Ln), r=[lnv], w=[lnv])
        k.op("act", lambda e: e.activation(out=rstd[:], in_=lnv[:], func=AF.Exp, scale=-0.5),
             r=[lnv], w=[rstd])
        for sub in range(4):
            ti = gi * 4 + sub
            st = S[ti % 2]
            qb = Q[ti % 2]
            for nt in range(9):
                n0 = nt * 512
                nw = min(512, N_IN - n0)
                pb = PB[pb_i % 6]
                pb_i += 1
                for c in range(8):
                    k.op("pe", lambda e: e.matmul(pb[:, 0:nw],
                                                  lhsT=h[:, c, sub * 128:(sub + 1) * 128],
                                                  rhs=Wb[:, c, n0:n0 + nw],
                                                  start=(c == 0), stop=(c == 7)),
                         r=[h, Wb], w=[pb])
                if nt % 2 == 0:
                    k.op("act", lambda e: e.activation(out=st[:, n0:n0 + nw], in_=pb[:, 0:nw],
                                                       func=AF.Copy, scale=rstd[:, sub:sub + 1]),
                         r=[pb, rstd], w=[st])
                else:
                    k.op("dve", lambda e: e.tensor_scalar(out=st[:, n0:n0 + nw], in0=pb[:, 0:nw],
                                                          scalar1=rstd[:, sub:sub + 1], scalar2=None,
                                                          op0=ALU.mult), r=[pb, rstd], w=[st])
            for si, (o, H) in enumerate(segs):
                eng = "dve" if si % 2 == 0 else "pool"
                u = st[:, o:o + 64 * H].rearrange("p (h two d) -> p h two d", h=H, two=2)
                qv = qb[:, o:o + 64 * H].rearrange("p (h two d) -> p h two d", h=H, two=2)
                u1 = u[:, :, 0, :]
                u2 = u[:, :, 1, :]
                cb = cos_t[:, ti, :].unsqueeze(1).to_broadcast([128, H, 32])
                sb_ = sin_t[:, ti, :].unsqueeze(1).to_broadcast([128, H, 32])
                t = [tt_[:, 0:H, :] for tt_ in tmp]
                k.op(eng, lambda e: e.tensor_tensor(out=t[0], in0=u1, in1=cb, op=ALU.mult),
                     r=[st, cos_t], w=[tmp[0]])
                k.op(eng, lambda e: e.tensor_tensor(out=t[1], in0=u2, in1=sb_, op=ALU.mult),
                     r=[st, sin_t], w=[tmp[1]])
                k.op(eng, lambda e: e.tensor_tensor(out=qv[:, :, 0, :], in0=t[0], in1=t[1],
                                                    op=ALU.subtract), r=[tmp[0], tmp[1]], w=[qb])
                k.op(eng, lambda e: e.tensor_tensor(out=t[2], in0=u2, in1=cb, op=ALU.mult),
                     r=[st, cos_t], w=[tmp[2]])
                k.op(eng, lambda e: e.tensor_tensor(out=t[3], in0=u1, in1=sb_, op=ALU.mult),
                     r=[st, sin_t], w=[tmp[3]])
                k.op(eng, lambda e: e.tensor_tensor(out=qv[:, :, 1, :], in0=t[2], in1=t[3],
                                                    op=ALU.add), r=[tmp[2], tmp[3]], w=[qb])
            vsrc = st[:, 512:1280].rearrange("p (a two d) -> p a two d", a=3, two=2)[:, :, 1, :]
            vdst = qb[:, 512:1280].rearrange("p (a two d) -> p a two d", a=3, two=2)[:, :, 1, :]
            k.op("pool", lambda e: e.tensor_copy(out=vdst, in_=vsrc), r=[st], w=[qb])
            r0 = ti * 128
            k.dma(o_qkv[r0:r0 + 128, :], qb[:], r=[qb])
            k.dma(o_g[r0:r0 + 128, :], st[:, 1280:1304], r=[st])
            k.dma(o_r[r0:r0 + 128, :], st[:, 1304:N_IN], r=[st])
    k.finish()
    return k.nc


def run_A(x_l, positions, norm_g_l, w_in_l):
    nc = build_A()
    xf = x_l.reshape(BATCH * SEQ, D_MODEL)
    pf = positions.reshape(BATCH * SEQ)
    invf = (10000.0 ** (-np.arange(0, 64, 2, dtype=np.float32) / 64)).astype(np.float32)
    invf = np.ascontiguousarray(np.broadcast_to(invf[None, :], (128, 32)))
    g8 = np.ascontiguousarray(norm_g_l.reshape(8, 128).T)
    in_maps = []
    for c in range(NCORES):
        sl = slice(c * TA, (c + 1) * TA)
        in_maps.append({
            "xT": np.ascontiguousarray(xf[sl].T),
            "w_in": np.ascontiguousarray(w_in_l),
            "g8": g8,
            "pos": np.ascontiguousarray(pf[sl].reshape(TA // 128, 128).T),
            "invf": invf,
        })
    res = run_bass_kernel_spmd(nc, in_maps, core_ids=list(range(NCORES)))
    qkv = np.concatenate([r["o_qkv"] for r in res.results], axis=0)
    g = np.concatenate([r["o_g"] for r in res.results], axis=0)
    rr = np.concatenate([r["o_r"] for r in res.results], axis=0)
    return qkv, g, rr


NJ = 16
NEGM = -30000.0
SCALE = 0.125


def build_B(debug=False):
    k = K()
    qT_d = k.din("qT", [NJ, 64, 2048], BF16)
    ksT_d = k.din("ksT", [64, SEQ], BF16)
    vs1_d = k.din("vs1", [128, 128 * 65], BF16)
    kwT_d = k.din("kwT", [NJ, 64, 1024], BF16)
    vw1_d = k.din("vw1", [NJ, 128, 8 * 65], BF16)
    kcr_d = k.din("kcr", [64, SEQ + 16], BF16)
    vcr_d = k.din("vcr", [64, SEQ + 16], BF16)
    posT_d = k.din("posT", [64, 64], F32)
    w1_d = k.din("w1", [2, 64, 32 * 256], F32)
    w2_d = k.din("w2", [128, 256], F32)
    glin_d = k.din("glin", [NJ, 128, 48], F32)
    bg_d = k.din("bg", [128, 12], F32)
    G_d = k.din("G", [128, 8192], BF16)
    id_d = k.din("ident", [128, 128], BF16)
    diag_d = k.din("diagm", [128, 8 * 512], BF16)
    win_d = k.din("winm", [128, 8 * 512], BF16)
    cmT_d = k.din("cmT", [NJ, 128, 1024], BF16)
    cmtok_d = k.din("cmtok", [NJ, 128, 1024], BF16)
    sA_d = k.din("sA", [NJ, 128, 1024], F32)
    sB_d = k.din("sB", [NJ, 128, 1024], F32)
    y_d = k.dout("y", [NJ, 128, 1024], F32)
    ydbg_d = k.dout("ydbg", [NJ, 128, 3072], F32) if debug else None

    ksT = k.sb([64, SEQ], BF16)
    vs1 = k.sb([128, 128, 65], BF16)
    G = k.sb([128, 8192], BF16)
    ident = k.sb([128, 128], BF16)
    diagm = k.sb([128, 8, 512], BF16)
    winm = k.sb([128, 8, 512], BF16)
    kcT = k.sb([64, 1024], BF16)
    vc1 = k.sb([128, 8, 65], BF16)
    bg = k.sb([128, 12], F32)
    k.dma(ksT[:], ksT_d[:, :], w=[ksT])
    k.dma(vs1[:].rearrange("p a c -> p (a c)"), vs1_d[:, :], w=[vs1])
    k.dma(G[:], G_d[:, :], w=[G])
    k.dma(ident[:], id_d[:, :], w=[ident])
    k.dma(diagm[:].rearrange("p a c -> p (a c)"), diag_d[:, :], w=[diagm])
    k.dma(winm[:].rearrange("p a c -> p (a c)"), win_d[:, :], w=[winm])
    k.dma(bg[:], bg_d[:, :], w=[bg])

    ST = [k.ps([128, 512], F32) for _ in range(3)]
    ACC = [k.ps([128, 512], F32) for _ in range(3)]
    MS = k.ps([128, 1024], F32)

    k.scope_begin()
    raw = k.sb([64, SEQ + 16], BF16)
    w1f = k.sb([64, 32 * 256], F32)
    w1b = k.sb([64, 32, 256], BF16)
    w2f = k.sb([128, 256], F32)
    w2b = k.sb([128, 2, 2, 64], BF16)
    posT = k.sb([64, 64], F32)
    tl = [k.sb([64, 512], BF16) for _ in range(3)]
    h1s = k.sb([128, 2, 512], BF16)
    k.dma(w2f[:], w2_d[:, :], w=[w2f])
    k.dma(posT[:], posT_d[:, :], w=[posT])
    k.op("dve", lambda e: e.tensor_copy(out=w2b[:].rearrange("p a b c -> p (a b c)"), in_=w2f[:]),
         r=[w2f], w=[w2b])
    k.op("pool", lambda e: e.memset(vc1[:, :, 64:65], 1.0), w=[vc1])
    for kv in range(2):
        k.dma(raw[:], (kcr_d if kv == 0 else vcr_d)[:, :], w=[raw])
        k.dma(w1f[:], w1_d[kv, :, :], w=[w1f])
        k.op("act", lambda e: e.copy(out=w1b[:].rearrange("p a b -> p (a b)"), in_=w1f[:]),
             r=[w1f], w=[w1b])
        for nt in range(2):
            for l in range(32):
                t = tl[l % 3]
                src = raw[:, l + 16 * 512 * nt: l + 16 * 512 * nt + 16 * 511 + 1: 16]
                k.op("dve" if l % 2 == 0 else "pool",
                     lambda e: e.tensor_scalar(out=t[:], in0=src,
                                               scalar1=posT[:, kv * 32 + l: kv * 32 + l + 1],
                                               scalar2=None, op0=ALU.add),
                     r=[raw, posT], w=[t])
                for hh in range(2):
                    k.op("pe", lambda e: e.matmul(MS[:, hh * 512:(hh + 1) * 512],
                                                  lhsT=w1b[:, l, hh * 128:(hh + 1) * 128], rhs=t[:],
                                                  start=(l == 0), stop=(l == 31)),
                         r=[w1b, t], w=[MS])
            for hh in range(2):
                k.op("act", lambda e: e.activation(out=h1s[:, hh, :], in_=MS[:, hh * 512:(hh + 1) * 512],
                                                   func=AF.Silu), r=[MS], w=[h1s])
            if kv == 0:
                for hh in range(2):
                    k.op("pe", lambda e: e.matmul(ST[0][0:64, :], lhsT=w2b[:, 0, hh, :], rhs=h1s[:, hh, :],
                                                  start=(hh == 0), stop=(hh == 1)),
                         r=[w2b, h1s], w=[ST[0]])
                k.op("dve", lambda e: e.tensor_copy(out=kcT[:, nt * 512:(nt + 1) * 512], in_=ST[0][0:64, :]),
                     r=[ST[0]], w=[kcT])
            else:
                for ns in range(4):
                    for hh in range(2):
                        k.op("pe", lambda e: e.matmul(ST[0][:, ns * 64:(ns + 1) * 64],
                                                      lhsT=h1s[:, hh, ns * 128:(ns + 1) * 128],
                                                      rhs=w2b[:, 1, hh, :],
                                                      start=(hh == 0 and ns == 0), stop=(hh == 1),
                                                      skip_group_check=True),
                             r=[h1s, w2b], w=[ST[0]])
                k.op("dve", lambda e: e.tensor_copy(
                    out=vc1[:, nt * 4:(nt + 1) * 4, 0:64],
                    in_=ST[0][:, 0:256].rearrange("p (a c) -> p a c", a=4)), r=[ST[0]], w=[vc1])
    k.scope_end()

    q_j = [k.sb([64, 4, 512], BF16) for _ in range(2)]
    kw_j = [k.sb([64, 1024], BF16) for _ in range(2)]
    vw_j = [k.sb([128, 8, 65], BF16) for _ in range(2)]
    cmT_j = [k.sb([128, 2, 512], BF16) for _ in range(2)]
    cmtok_j = [k.sb([128, 4, 256], BF16) for _ in range(2)]
    sA_j = [k.sb([128, 4, 256], F32) for _ in range(2)]
    sB_j = [k.sb([128, 4, 256], F32) for _ in range(2)]
    gl_j = [k.sb([128, 4, 12], F32) for _ in range(2)]
    y_j = [k.sb([128, 4, 4, 64], F32) for _ in range(2)]
    gate = k.sb([128, 4, 12], F32)
    ydbg = k.sb([128, 3, 4, 4, 64], F32) if debug else None
    biasT = k.sb([128, 2, 512], BF16)
    PT = [k.sb([128, 512], BF16) for _ in range(3)]
    pcs = k.sb([128, 1032], F32)
    ee = k.sb([128, 1024], F32)
    Zc = k.sb([128, 4], F32)
    imp = k.sb([128, 256], F32)
    score = k.sb([128, 256], F32)
    sc2 = k.sb([128, 256], F32)
    m8 = k.sb([128, 8], F32)
    thr = k.sb([128, 1], F32)
    bq = k.sb([128, 256], BF16)
    coef = k.sb([128, 16], F32)
    zt = k.sb([128, 16], F32)
    cnt = {"st": 0, "pt": 0}

    def acc_slot(r, sub):
        idx = r * 4 + sub
        return ACC[idx // 6], (idx % 6) * 65

    def load_j(j):
        b = j % 2
        k.dma(q_j[b][:].rearrange("p a c -> p (a c)"), qT_d[j, :, :], w=[q_j[b]])
        k.dma(kw_j[b][:], kwT_d[j, :, :], w=[kw_j[b]])
        k.dma(vw_j[b][:].rearrange("p a c -> p (a c)"), vw1_d[j, :, :], w=[vw_j[b]])
        k.dma(cmT_j[b][:].rearrange("p a c -> p (a c)"), cmT_d[j, :, :], w=[cmT_j[b]])
        k.dma(cmtok_j[b][:].rearrange("p a c -> p (a c)"), cmtok_d[j, :, :], w=[cmtok_j[b]])
        k.dma(sA_j[b][:].rearrange("p a c -> p (a c)"), sA_d[j, :, :], w=[sA_j[b]])
        k.dma(sB_j[b][:].rearrange("p a c -> p (a c)"), sB_d[j, :, :], w=[sB_j[b]])
        k.dma(gl_j[b][:].rearrange("p a c -> p (a c)"), glin_d[j, :, :], w=[gl_j[b]])

    def branch(j, br, items):
        b = j % 2
        yj = y_j[b]
        for a in ACC:
            k.op("dve", lambda e: e.memset(a[:], 0.0), w=[a])
        for (keys_ap, keys_tt, v_ap, v_tt, slo, shi, masks) in items:
            c0, c1 = slo * 128, (shi + 1) * 128
            for r in range(4):
                st = ST[cnt["st"] % 3]
                cnt["st"] += 1
                pt = PT[cnt["pt"] % 3]
                cnt["pt"] += 1
                k.op("pe", lambda e: e.matmul(st[:, c0:c1], lhsT=keys_ap, rhs=q_j[b][:, r, c0:c1],
                                              start=True, stop=(len(masks) == 0)),
                     r=[keys_tt, q_j[b]], w=[st])
                for mi, (ml, mr, mtts) in enumerate(masks):
                    k.op("pe", lambda e: e.matmul(st[:, c0:c1], lhsT=ml, rhs=mr[:, c0:c1],
                                                  start=False, stop=(mi == len(masks) - 1)),
                         r=mtts, w=[st])
                k.op("act", lambda e: e.activation(out=pt[:, c0:c1], in_=st[:, c0:c1], func=AF.Exp,
                                                   scale=SCALE), r=[st], w=[pt])
                for sub in range(slo, shi + 1):
                    a, col = acc_slot(r, sub)
                    k.op("pe", lambda e: e.matmul(a[:, col:col + 65], lhsT=pt[:, sub * 128:(sub + 1) * 128],
                                                  rhs=v_ap, start=False, stop=False, skip_group_check=True),
                         r=[pt, v_tt], w=[a])
        for bi, a in enumerate(ACC):
            ns = 6 if bi < 2 else 4
            k.op("dve", lambda e: e.tensor_scalar(
                out=zt[:, bi * 6:bi * 6 + ns],
                in0=a[:, 0:ns * 65].rearrange("p (s c) -> p s c", c=65)[:, :, 64],
                scalar1=1e-30, scalar2=None, op0=ALU.max), r=[a], w=[zt])
        k.op("dve", lambda e: e.reciprocal(out=zt[:], in_=zt[:]), r=[zt], w=[zt])
        for r in range(4):
            k.op("dve", lambda e: e.tensor_tensor(out=coef[:, r * 4:(r + 1) * 4], in0=zt[:, r * 4:(r + 1) * 4],
                                                  in1=gate[:, :, r * 3 + br], op=ALU.mult),
                 r=[zt, gate], w=[coef])
        for r in range(4):
            for sub in range(4):
                a, col = acc_slot(r, sub)
                idx = r * 4 + sub
                if debug:
                    k.op("dve", lambda e: e.tensor_scalar(out=ydbg[:, br, sub, r, :], in0=a[:, col:col + 64],
                                                          scalar1=zt[:, idx:idx + 1], scalar2=None,
                                                          op0=ALU.mult), r=[a, zt], w=[ydbg])
                if br == 0:
                    k.op("dve", lambda e: e.tensor_scalar(out=yj[:, sub, r, :], in0=a[:, col:col + 64],
                                                          scalar1=coef[:, idx:idx + 1], scalar2=None,
                                                          op0=ALU.mult), r=[a, coef], w=[yj])
                else:
                    k.op("dve", lambda e: e.scalar_tensor_tensor(out=yj[:, sub, r, :], in0=a[:, col:col + 64],
                                                                 scalar=coef[:, idx:idx + 1],
                                                                 in1=yj[:, sub, r, :], op0=ALU.mult,
                                                                 op1=ALU.add), r=[a, coef, yj], w=[yj])

    load_j(0)
    for j in range(NJ):
        b = j % 2
        if j + 1 < NJ:
            load_j(j + 1)
        qj = q_j[b]
        k.op("dve", lambda e: e.tensor_tensor(out=gate[:], in0=gl_j[b][:],
                                              in1=bg[:].unsqueeze(1).to_broadcast([128, 4, 12]), op=ALU.add),
             r=[gl_j[b], bg], w=[gate])
        k.op("act", lambda e: e.activation(out=gate[:], in_=gate[:], func=AF.Exp, scale=-1.0),
             r=[gate], w=[gate])
        k.op("dve", lambda e: e.tensor_scalar(out=gate[:], in0=gate[:], scalar1=1.0, scalar2=None,
                                              op0=ALU.add), r=[gate], w=[gate])
        k.op("dve", lambda e: e.reciprocal(out=gate[:], in_=gate[:]), r=[gate], w=[gate])

        W = 128 * (j // 2 + 1)
        mw = min(256, W)
        for sub in range(4):
            k.op("pool", lambda e: e.memset(pcs[:], 0.0), w=[pcs])
            for r in range(4):
                for c0 in range(0, W, 512):
                    c1 = min(W, c0 + 512)
                    m0 = max(c0, W - mw)
                    has_mask = m0 < c1
                    k.op("pe", lambda e: e.matmul(MS[:, c0:c1], lhsT=qj[:, r, sub * 128:(sub + 1) * 128],
                                                  rhs=kcT[:, c0:c1], start=True, stop=not has_mask),
                         r=[qj, kcT], w=[MS])
                    if has_mask:
                        k.op("pe", lambda e: e.matmul(MS[:, m0:c1], lhsT=ident[:],
                                                      rhs=cmtok_j[b][:, sub, m0 - (W - mw):c1 - (W - mw)],
                                                      start=False, stop=True),
                             r=[ident, cmtok_j[b]], w=[MS])
                k.op("act", lambda e: e.activation(out=ee[:, 0:W], in_=MS[:, 0:W], func=AF.Exp, scale=SCALE,
                                                   accum_out=Zc[:, r:r + 1]), r=[MS], w=[ee, Zc])
                k.op("dve", lambda e: e.tensor_scalar(out=Zc[:, r:r + 1], in0=Zc[:, r:r + 1], scalar1=1e-30,
                                                      scalar2=None, op0=ALU.max), r=[Zc], w=[Zc])
                k.op("dve", lambda e: e.reciprocal(out=Zc[:, r:r + 1], in_=Zc[:, r:r + 1]), r=[Zc], w=[Zc])
                k.op("dve", lambda e: e.scalar_tensor_tensor(out=pcs[:, 1:1 + W], in0=ee[:, 0:W],
                                                             scalar=Zc[:, r:r + 1], in1=pcs[:, 1:1 + W],
                                                             op0=ALU.mult, op1=ALU.add),
                     r=[ee, Zc, pcs], w=[pcs])
            k.op("dve", lambda e: e.tensor_reduce(out=imp[:], in_=pcs[:, 0:1024].rearrange("p (s c) -> p s c", c=4),
                                                  axis=AX.X, op=ALU.add), r=[pcs], w=[imp])
            k.op("dve", lambda e: e.tensor_tensor(out=imp[:], in0=imp[:], in1=pcs[:, 4:1028:4], op=ALU.add),
                 r=[imp, pcs], w=[imp])
            k.op("dve", lambda e: e.tensor_tensor(out=score[:], in0=imp[:], in1=sA_j[b][:, sub, :], op=ALU.mult),
                 r=[imp, sA_j[b]], w=[score])
            k.op("dve", lambda e: e.tensor_tensor(out=score[:], in0=score[:], in1=sB_j[b][:, sub, :], op=ALU.add),
                 r=[score, sB_j[b]], w=[score])
            k.op("dve", lambda e: e.max(out=m8[:], in_=score[:]), r=[score], w=[m8])
            k.op("dve", lambda e: e.match_replace(out=sc2[:], in_to_replace=m8[:], in_values=score[:],
                                                  imm_value=-2.0), r=[m8, score], w=[sc2])
            k.op("dve", lambda e: e.max(out=m8[:], in_=sc2[:]), r=[sc2], w=[m8])
            k.op("dve", lambda e: e.tensor_scalar(out=thr[:], in0=m8[:, 7:8], scalar1=0.0, scalar2=None,
                                                  op0=ALU.max), r=[m8], w=[thr])
            k.op("dve", lambda e: e.tensor_scalar(out=bq[:], in0=score[:], scalar1=thr[:, 0:1], scalar2=1.0,
                                                  op0=ALU.is_ge, op1=ALU.subtract), r=[score, thr], w=[bq])
            MSb = MS[:].bitcast(BF16)
            for ch in range(2):
                k.op("pe", lambda e: e.transpose(MSb[:, ch * 128:(ch + 1) * 128], bq[:, ch * 128:(ch + 1) * 128],
                                                 ident[:]), r=[bq, ident], w=[MS])
            k.op("act", lambda e: e.activation(
                out=biasT[:, :, sub * 128:(sub + 1) * 128],
                in_=MSb[:, 0:256].rearrange("p (a c) -> p a c", a=2), func=AF.Copy, scale=30000.0),
                r=[MS], w=[biasT])

        items = []
        for m in range(j // 2 + 1):
            masks = []
            if m >= j // 2 - 1:
                masks.append((ident[:], cmT_j[b][:, m - (j // 2 - 1), :], [ident, cmT_j[b]]))
            items.append((kcT[:, m * 128:(m + 1) * 128], kcT, vc1[:, m, :], vc1, 0, 3, masks))
        branch(j, 0, items)
        items = []
        for kt in range(8 * j + 8):
            masks = [(G[:, (kt % 64) * 128:(kt % 64 + 1) * 128], biasT[:, kt // 64, :], [G, biasT])]
            if kt >= 8 * j:
                masks.append((ident[:], diagm[:, kt - 8 * j, :], [ident, diagm]))
            items.append((ksT[:, kt * 128:(kt + 1) * 128], ksT, vs1[:, kt, :], vs1, 0, 3, masks))
        branch(j, 1, items)
        items = []
        for kt in range(8):
            masks = [(ident[:], winm[:, kt, :], [ident, winm])]
            items.append((kw_j[b][:, kt * 128:(kt + 1) * 128], kw_j[b], vw_j[b][:, kt, :], vw_j[b],
                          max(0, kt - 4), min(3, kt), masks))
        branch(j, 2, items)
        k.dma(y_d[j, :, :], y_j[b][:].rearrange("p a b c -> p (a b c)"), r=[y_j[b]])
        if debug:
            k.dma(ydbg_d[j, :, :], ydbg[:].rearrange("p e a b c -> p (e a b c)"), r=[ydbg])
    k.finish()
    return k.nc


def _b_consts(p):
    c = {}
    c["G"] = (np.arange(8192)[None, :] // 64 == np.arange(128)[:, None]).astype(NPBF)
    c["ident"] = np.eye(128, dtype=np.float32).astype(NPBF)
    kk = np.arange(128)[:, None, None]
    ii = np.arange(8)[None, :, None]
    qq = np.arange(512)[None, None, :]
    c["diagm"] = np.where(128 * ii + kk <= 512 * p + qq, 0.0, NEGM).astype(NPBF).reshape(128, 4096)
    kp = 128 * ii + kk
    qp = 512 + qq
    c["winm"] = np.where((kp <= qp) & (kp > qp - 512), 0.0, NEGM).astype(NPBF).reshape(128, 4096)
    cmT = np.zeros((NJ, 128, 2, 512), np.float32)
    cmtok = np.zeros((NJ, 128, 4, 256), np.float32)
    sA = np.zeros((NJ, 128, 4, 256), np.float32)
    sB = np.zeros((NJ, 128, 4, 256), np.float32)
    for j in range(NJ):
        gq = 2 * j + p
        for i in range(2):
            mt = j // 2 - 1 + i
            if mt < 0:
                continue
            n = 128 * mt + np.arange(128)[:, None]
            t = 512 * gq + np.arange(512)[None, :]
            cmT[j, :, i, :] = np.where((16 * n + 31 <= t) & (n <= 1022), 0.0, NEGM)
        W = 128 * (j // 2 + 1)
        mw = min(256, W)
        t = 512 * gq + 128 * np.arange(4)[None, :, None] + np.arange(128)[:, None, None]
        n = (W - mw) + np.arange(mw)[None, None, :]
        cmtok[j, :, :, 0:mw] = np.where((16 * n + 31 <= t) & (n <= 1022), 0.0, NEGM)
        s = np.arange(256)[None, None, :]
        cur = t // 64
        forced = (s == 0) | (s == cur) | (s == cur - 1)
        valid = 64 * s <= t
        sA[j] = (valid & ~forced).astype(np.float32)
        sB[j] = np.where(forced, 1e4, np.where(valid, 0.0, -1.0))
    c["cmT"] = cmT.astype(NPBF).reshape(NJ, 128, 1024)
    c["cmtok"] = cmtok.astype(NPBF).reshape(NJ, 128, 1024)
    c["sA"] = sA.reshape(NJ, 128, 1024)
    c["sB"] = sB.reshape(NJ, 128, 1024)
    return c


def run_B(qkv, glin, b_gate_l, cmp_pos_l, cmp_w1_l, cmp_w2_l, debug=False):
    nc = build_B(debug)
    qkv = qkv.reshape(BATCH, SEQ, 1280)
    glin = glin.reshape(BATCH, SEQ, 8, 3)
    consts = [_b_consts(0), _b_consts(1)]
    posT = np.ascontiguousarray(cmp_pos_l.transpose(2, 0, 1).reshape(64, 64)).astype(np.float32)
    w1 = np.ascontiguousarray(cmp_w1_l.reshape(2, 32, 64, 256).transpose(0, 2, 1, 3).reshape(2, 64, 32 * 256))
    w2 = np.ascontiguousarray(cmp_w2_l.reshape(2, 2, 128, 64).transpose(2, 0, 1, 3).reshape(128, 256))
    ones = np.ones((SEQ, 1), NPBF)
    in_maps = []
    for core in range(NCORES):
        b, g, p = core // 4, (core // 2) % 2, core % 2
        qk = qkv[b]
        tiles = [2 * j + p for j in range(NJ)]
        q = qk[:, g * 256:(g + 1) * 256].reshape(32, 512, 4, 64)[tiles]
        qT = np.ascontiguousarray(q.transpose(0, 3, 2, 1)).reshape(NJ, 64, 2048)
        sl = lambda o: qk[:, o + g * 64: o + (g + 1) * 64]
        kc, vc, ks, vs, kw, vw = sl(512), sl(640), sl(768), sl(896), sl(1024), sl(1152)
        pad16 = np.zeros((64, 16), NPBF)
        kcr = np.concatenate([np.ascontiguousarray(kc.T), pad16], axis=1)
        vcr = np.concatenate([np.ascontiguousarray(vc.T), pad16], axis=1)
        ksT = np.ascontiguousarray(ks.T)
        vs1 = np.concatenate([vs, ones], axis=1).reshape(128, 128, 65).transpose(1, 0, 2)
        vs1 = np.ascontiguousarray(vs1).reshape(128, 128 * 65)
        kwp = np.concatenate([np.zeros((512, 64), NPBF), kw], axis=0)
        vwp = np.concatenate([np.zeros((512, 65), NPBF), np.concatenate([vw, ones], axis=1)], axis=0)
        kwT = np.stack([kwp[512 * gq: 512 * gq + 1024].T for gq in tiles])
        vw1 = np.stack([vwp[512 * gq: 512 * gq + 1024].reshape(8, 128, 65).transpose(1, 0, 2).reshape(128, 520)
                        for gq in tiles])
        gl = glin[b][:, g * 4:(g + 1) * 4, :].reshape(32, 4, 128, 12)[tiles]
        gl = np.ascontiguousarray(gl.transpose(0, 2, 1, 3)).reshape(NJ, 128, 48)
        bg = np.ascontiguousarray(np.broadcast_to(b_gate_l.reshape(8, 3)[g * 4:(g + 1) * 4].reshape(1, 12), (128, 12)))
        m = {"qT": qT, "ksT": ksT, "vs1": vs1, "kwT": np.ascontiguousarray(kwT), "vw1": np.ascontiguousarray(vw1),
             "kcr": np.ascontiguousarray(kcr), "vcr": np.ascontiguousarray(vcr), "posT": posT, "w1": w1, "w2": w2,
             "glin": gl.astype(np.float32), "bg": bg.astype(np.float32)}
        m.update(consts[p])
        in_maps.append(m)
    res = run_bass_kernel_spmd(nc, in_maps, core_ids=list(range(NCORES)))
    y = np.zeros((BATCH, 32, 512, 2, 256), np.float32)
    for core in range(NCORES):
        b, g, p = core // 4, (core // 2) % 2, core % 2
        yc = res.results[core]["y"].reshape(NJ, 128, 4, 256).transpose(0, 2, 1, 3).reshape(NJ, 512, 256)
        y[b, p::2, :, g, :] = yc
    if debug:
        yd = np.zeros((BATCH, 32, 512, 3, 2, 256), np.float32)
        for core in range(NCORES):
            b, g, p = core // 4, (core // 2) % 2, core % 2
            yc = res.results[core]["ydbg"].reshape(NJ, 128, 3, 4, 256).transpose(0, 3, 1, 2, 4).reshape(NJ, 512, 3, 256)
            yd[b, p::2, :, :, g, :] = yc
        return y.reshape(BATCH * SEQ, 512), yd.reshape(BATCH * SEQ, 3, 512)
    return y.reshape(BATCH * SEQ, 512)


def build_C():
    k = K()
    uT_d = k.din("uT", [128, SEQ + 3], F32)
    cw_d = k.din("cw", [128, 4], F32)
    vec_d = k.din("vec", [128, 4], F32)
    wa_d = k.din("wa", [128, 128], F32)
    wi_d = k.din("wi", [128, 128], F32)
    y_d = k.dout("y", [128, SEQ], F32)
    cw = k.sb([128, 4], F32)
    vec = k.sb([128, 4], F32)
    wf = k.sb([128, 256], F32)
    wb = k.sb([128, 256], BF16)
    cs = k.sb([128, 4], F32)
    t1 = k.sb([128, 1], F32)
    k.dma(cw[:], cw_d[:, :], w=[cw])
    k.dma(vec[:], vec_d[:, :], w=[vec])
    k.dma(wf[:, 0:128], wa_d[:, :], w=[wf])
    k.dma(wf[:, 128:256], wi_d[:, :], w=[wf])
    k.op("dve", lambda e: e.tensor_copy(out=wb[:], in_=wf[:]), r=[wf], w=[wb])
    k.op("act", lambda e: e.activation(out=t1[:], in_=vec[:, 3:4], func=AF.Exp, scale=-1.0), r=[vec], w=[t1])
    k.op("dve", lambda e: e.tensor_scalar(out=t1[:], in0=t1[:], scalar1=1.0, scalar2=None, op0=ALU.add),
         r=[t1], w=[t1])
    k.op("act", lambda e: e.activation(out=t1[:], in_=t1[:], func=AF.Ln), r=[t1], w=[t1])
    k.op("dve", lambda e: e.tensor_scalar(out=cs[:, 0:1], in0=t1[:], scalar1=-8.0, scalar2=None, op0=ALU.mult),
         r=[t1], w=[cs])
    k.op("dve", lambda e: e.tensor_scalar(out=cs[:, 1:2], in0=t1[:], scalar1=-16.0, scalar2=None, op0=ALU.mult),
         r=[t1], w=[cs])
    k.op("dve", lambda e: e.tensor_scalar(out=cs[:, 2:4], in0=vec[:, 1:3], scalar1=-1.0, scalar2=None,
                                          op0=ALU.mult), r=[vec], w=[cs])
    NTL = SEQ // 512
    u = [k.sb([128, 515], F32) for _ in range(2)]
    xc = k.sb([128, 512], F32)
    xcb = k.sb([128, 512], BF16)
    rr = k.sb([128, 512], F32)
    ig = k.sb([128, 512], F32)
    aa = k.sb([128, 512], F32)
    om = k.sb([128, 512], F32)
    bt = k.sb([128, 512], F32)
    hh = [k.sb([128, 512], F32) for _ in range(2)]
    P = [k.ps([128, 512], F32) for _ in range(2)]
    for ti in range(NTL):
        ub = u[ti % 2]
        h = hh[ti % 2]
        k.dma(ub[:], uT_d[:, ti * 512: ti * 512 + 515], w=[ub])
        k.op("dve", lambda e: e.tensor_scalar(out=xc[:], in0=ub[:, 0:512], scalar1=cw[:, 0:1], scalar2=vec[:, 0:1],
                                              op0=ALU.mult, op1=ALU.add), r=[ub, cw, vec], w=[xc])
        for kk in range(1, 4):
            k.op("dve", lambda e: e.scalar_tensor_tensor(out=xc[:], in0=ub[:, kk:kk + 512], scalar=cw[:, kk:kk + 1],
                                                         in1=xc[:], op0=ALU.mult, op1=ALU.add),
                 r=[ub, cw, xc], w=[xc])
        k.op("pool", lambda e: e.tensor_copy(out=xcb[:], in_=xc[:]), r=[xc], w=[xcb])
        for gi, dst in enumerate((rr, ig)):
            k.op("pe", lambda e: e.matmul(P[gi][:], lhsT=wb[:, gi * 128:(gi + 1) * 128], rhs=xcb[:],
                                          start=True, stop=True), r=[wb, xcb], w=[P[gi]])
            k.op("act", lambda e: e.activation(out=dst[:], in_=P[gi][:], func=AF.Exp, scale=-1.0,
                                               bias=cs[:, 2 + gi:3 + gi]), r=[P[gi], cs], w=[dst])
            k.op("pool", lambda e: e.tensor_scalar(out=dst[:], in0=dst[:], scalar1=1.0, scalar2=None,
                                                   op0=ALU.add), r=[dst], w=[dst])
            k.op("dve", lambda e: e.reciprocal(out=dst[:], in_=dst[:]), r=[dst], w=[dst])
        k.op("act", lambda e: e.activation(out=aa[:], in_=rr[:], func=AF.Exp, scale=cs[:, 0:1]),
             r=[rr, cs], w=[aa])
        k.op("act", lambda e: e.activation(out=om[:], in_=rr[:], func=AF.Exp, scale=cs[:, 1:2]),
             r=[rr, cs], w=[om])
        k.op("pool", lambda e: e.tensor_scalar(out=om[:], in0=om[:], scalar1=-1.0, scalar2=1.0,
                                               op0=ALU.mult, op1=ALU.add), r=[om], w=[om])
        k.op("pool", lambda e: e.tensor_scalar(out=om[:], in0=om[:], scalar1=1e-18, scalar2=None,
                                               op0=ALU.max), r=[om], w=[om])
        k.op("act", lambda e: e.activation(out=om[:], in_=om[:], func=AF.Ln), r=[om], w=[om])
        k.op("act", lambda e: e.activation(out=om[:], in_=om[:], func=AF.Exp, scale=0.5), r=[om], w=[om])
        k.op("dve", lambda e: e.tensor_tensor(out=bt[:], in0=ig[:], in1=xc[:], op=ALU.mult), r=[ig, xc], w=[bt])
        k.op("dve", lambda e: e.tensor_tensor(out=bt[:], in0=bt[:], in1=om[:], op=ALU.mult), r=[bt, om], w=[bt])
        if ti == 0:
            k.op("dve", lambda e: e.tensor_tensor_scan(out=h[:], data0=aa[:], data1=bt[:], initial=0.0,
                                                       op0=ALU.mult, op1=ALU.add), r=[aa, bt], w=[h])
        else:
            hp = hh[(ti - 1) % 2]
            k.op("dve", lambda e: e.tensor_tensor_scan(out=h[:], data0=aa[:], data1=bt[:], initial=hp[:, 511:512],
                                                       op0=ALU.mult, op1=ALU.add), r=[aa, bt, hp], w=[h])
        k.dma(y_d[:, ti * 512:(ti + 1) * 512], h[:], r=[h])
    k.finish()
    return k.nc


def run_C(u_lru, conv_w, conv_b, w_a, b_a, w_i, b_i, lam):
    nc = build_C()
    u = u_lru.reshape(BATCH, SEQ, 512)
    in_maps = []
    for core in range(NCORES):
        b, cq = core // 4, core % 4
        cs_ = slice(cq * 128, (cq + 1) * 128)
        uT = np.concatenate([np.zeros((128, 3), np.float32), np.ascontiguousarray(u[b][:, cs_].T)], axis=1)
        wa = np.zeros((128, 128), np.float32)
        wi = np.zeros((128, 128), np.float32)
        for hb in range(2):
            wa[hb * 64:(hb + 1) * 64, hb * 64:(hb + 1) * 64] = w_a[cq * 2 + hb]
            wi[hb * 64:(hb + 1) * 64, hb * 64:(hb + 1) * 64] = w_i[cq * 2 + hb]
        vec = np.stack([conv_b[cs_], b_a[cs_], b_i[cs_], lam[cs_]], axis=1).astype(np.float32)
        in_maps.append({"uT": np.ascontiguousarray(uT), "cw": np.ascontiguousarray(conv_w[:, cs_].T),
                        "vec": np.ascontiguousarray(vec), "wa": wa, "wi": wi})
    res = run_bass_kernel_spmd(nc, in_maps, core_ids=list(range(NCORES)))
    y = np.zeros((BATCH, SEQ, 512), np.float32)
    for core in range(NCORES):
        b, cq = core // 4, core % 4
        y[b][:, cq * 128:(cq + 1) * 128] = res.results[core]["y"].T
    return y.reshape(BATCH * SEQ, 512)


def build_E(final):
    k = K()
    glu_d = k.din("gluT", [1024, TA + 30], F32)
    dww_d = k.din("dww", [128, 4 * 31], F32)
    vec_d = k.din("vec", [128, 12], F32)
    wp_d = k.din("wp", [512, 512], F32)
    yT_d = k.din("yT", [1024, TA], F32)
    zT_d = k.din("zT", [1536, TA], F32)
    wo_d = k.din("wo", [1536, 1024], F32)
    x_d = k.din("x", [TA, 1024], F32)
    fg_d = k.din("fg", [128, 1024], F32)
    o_d = k.dout("o", [TA, 1024], F32)

    Wo = k.sb([128, 12, 1024], BF16)
    Wp = k.sb([128, 4, 512], BF16)
    dww = k.sb([128, 4, 31], F32)
    vec = k.sb([128, 12], F32)
    fg = k.sb([128, 1024], F32)
    onesM = k.sb([128, 128], F32)
    xt = [k.sb([128, 1024], F32) for _ in range(2)]
    xo = [k.sb([128, 1024], F32) for _ in range(2)]
    k.dma(dww[:].rearrange("p a c -> p (a c)"), dww_d[:, :], w=[dww])
    k.dma(vec[:], vec_d[:, :], w=[vec])
    k.dma(fg[:], fg_d[:, :], w=[fg])
    k.op("pool", lambda e: e.memset(onesM[:], 1.0 / 512.0), w=[onesM])
    for c in range(12):
        st = xt[c % 2]
        k.dma(st[:], wo_d[c * 128:(c + 1) * 128, :], w=[st])
        k.op("act" if c % 2 == 0 else "dve",
             (lambda e: e.copy(out=Wo[:, c, :], in_=st[:])) if c % 2 == 0 else
             (lambda e: e.tensor_copy(out=Wo[:, c, :], in_=st[:])), r=[st], w=[Wo])
    for c in range(4):
        st = xt[c % 2]
        k.dma(st[:, 0:512], wp_d[c * 128:(c + 1) * 128, :], w=[st])
        k.op("dve", lambda e: e.tensor_copy(out=Wp[:, c, :], in_=st[:, 0:512]), r=[st], w=[Wp])

    ga = k.sb([128, 542], F32)
    gb = k.sb([128, 542], F32)
    uu = k.sb([128, 542], F32)
    cv = k.sb([128, 4, 512], F32)
    cvsq = k.sb([128, 4, 512], F32)
    mean = k.sb([128, 512], F32)
    m2 = k.sb([128, 512], F32)
    rstd = k.sb([128, 512], F32)
    un = k.sb([128, 512], F32)
    sT = k.sb([128, 4, 512], BF16)
    zt = k.sb([128, 12, 512], F32)
    yt = k.sb([128, 8, 512], F32)
    mixT = k.sb([128, 12, 512], BF16)
    sq = k.sb([128, 1024], F32)
    ss = k.sb([128, 1], F32)
    PM = k.ps([128, 512], F32)
    PV = k.ps([128, 512], F32)
    PW = [k.ps([128, 512], F32) for _ in range(2)]
    PO = [k.ps([128, 512], F32) for _ in range(2)]
    cnt = 0
    for ti in range(TA // 512):
        t0 = ti * 512
        k.dma(zt[:], zT_d[:, t0:t0 + 512].rearrange("(c p) t -> p c t", p=128), w=[zt])
        k.dma(yt[:], yT_d[:, t0:t0 + 512].rearrange("(c p) t -> p c t", p=128), w=[yt])
        for ch in range(4):
            k.dma(ga[:], glu_d[ch * 128:(ch + 1) * 128, t0:t0 + 542], w=[ga])
            k.dma(gb[:], glu_d[512 + ch * 128:512 + (ch + 1) * 128, t0:t0 + 542], w=[gb])
            k.op("act", lambda e: e.activation(out=gb[:], in_=gb[:], func=AF.Sigmoid), r=[gb], w=[gb])
            k.op("pool", lambda e: e.tensor_tensor(out=uu[:], in0=ga[:], in1=gb[:], op=ALU.mult),
                 r=[ga, gb], w=[uu])
            k.op("dve", lambda e: e.tensor_scalar(out=cv[:, ch, :], in0=uu[:, 0:512], scalar1=dww[:, ch, 0:1],
                                                  scalar2=vec[:, ch:ch + 1], op0=ALU.mult, op1=ALU.add),
                 r=[uu, dww, vec], w=[cv])
            for kk in range(1, 31):
                k.op("dve", lambda e: e.scalar_tensor_tensor(out=cv[:, ch, :], in0=uu[:, kk:kk + 512],
                                                             scalar=dww[:, ch, kk:kk + 1], in1=cv[:, ch, :],
                                                             op0=ALU.mult, op1=ALU.add), r=[uu, dww, cv], w=[cv])
        k.op("pool", lambda e: e.tensor_tensor(out=cvsq[:], in0=cv[:], in1=cv[:], op=ALU.mult), r=[cv], w=[cvsq])
        for ch in range(4):
            k.op("pe", lambda e: e.matmul(PM[:], lhsT=onesM[:], rhs=cv[:, ch, :], start=(ch == 0), stop=(ch == 3)),
                 r=[onesM, cv], w=[PM])
        for ch in range(4):
            k.op("pe", lambda e: e.matmul(PV[:], lhsT=onesM[:], rhs=cvsq[:, ch, :], start=(ch == 0), stop=(ch == 3)),
                 r=[onesM, cvsq], w=[PV])
        k.op("act", lambda e: e.copy(out=mean[:], in_=PM[:]), r=[PM], w=[mean])
        k.op("pool", lambda e: e.tensor_tensor(out=m2[:], in0=mean[:], in1=mean[:], op=ALU.mult), r=[mean], w=[m2])
        k.op("dve", lambda e: e.tensor_tensor(out=rstd[:], in0=PV[:], in1=m2[:], op=ALU.subtract),
             r=[PV, m2], w=[rstd])
        k.op("dve", lambda e: e.tensor_scalar(out=rstd[:], in0=rstd[:], scalar1=0.0, scalar2=EPS, op0=ALU.max,
                                              op1=ALU.add), r=[rstd], w=[rstd])
        k.op("act", lambda e: e.activation(out=rstd[:], in_=rstd[:], func=AF.Ln), r=[rstd], w=[rstd])
        k.op("act", lambda e: e.activation(out=rstd[:], in_=rstd[:], func=AF.Exp, scale=-0.5), r=[rstd], w=[rstd])
        for ch in range(4):
            k.op("pool", lambda e: e.tensor_tensor(out=un[:], in0=cv[:, ch, :], in1=mean[:], op=ALU.subtract),
                 r=[cv, mean], w=[un])
            k.op("dve", lambda e: e.tensor_tensor(out=un[:], in0=un[:], in1=rstd[:], op=ALU.mult),
                 r=[un, rstd], w=[un])
            k.op("act", lambda e: e.activation(out=sT[:, ch, :], in_=un[:], func=AF.Silu,
                                               scale=vec[:, 4 + ch:5 + ch], bias=vec[:, 8 + ch:9 + ch]),
                 r=[un, vec], w=[sT])
        k.op("act", lambda e: e.activation(out=zt[:], in_=zt[:], func=AF.Silu), r=[zt], w=[zt])
        k.op("dve", lambda e: e.tensor_tensor(out=mixT[:, 0:8, :], in0=yt[:], in1=zt[:, 0:8, :], op=ALU.mult),
             r=[yt, zt], w=[mixT])
        for jc in range(4):
            pw = PW[jc % 2]
            for ic in range(4):
                k.op("pe", lambda e: e.matmul(pw[:], lhsT=Wp[:, ic, jc * 128:(jc + 1) * 128], rhs=sT[:, ic, :],
                                              start=(ic == 0), stop=(ic == 3)), r=[Wp, sT], w=[pw])
            k.op("dve", lambda e: e.tensor_tensor(out=mixT[:, 8 + jc, :], in0=pw[:], in1=zt[:, 8 + jc, :],
                                                  op=ALU.mult), r=[pw, zt], w=[mixT])
        for sub in range(4):
            r0 = t0 + sub * 128
            xb = xt[cnt % 2]
            ob = xo[cnt % 2]
            cnt += 1
            k.dma(xb[:], x_d[r0:r0 + 128, :], w=[xb])
            for nh in range(2):
                po = PO[nh]
                for kc in range(12):
                    k.op("pe", lambda e: e.matmul(po[:], lhsT=mixT[:, kc, sub * 128:(sub + 1) * 128],
                                                  rhs=Wo[:, kc, nh * 512:(nh + 1) * 512],
                                                  start=(kc == 0), stop=(kc == 11)), r=[mixT, Wo], w=[po])
                k.op("dve", lambda e: e.tensor_tensor(out=ob[:, nh * 512:(nh + 1) * 512], in0=po[:],
                                                      in1=xb[:, nh * 512:(nh + 1) * 512], op=ALU.add),
                     r=[po, xb], w=[ob])
            if final:
                k.op("pool", lambda e: e.tensor_tensor(out=sq[:], in0=ob[:], in1=ob[:], op=ALU.mult), r=[ob], w=[sq])
                k.op("dve", lambda e: e.tensor_reduce(out=ss[:], in_=sq[:], axis=AX.X, op=ALU.add), r=[sq], w=[ss])
                k.op("dve", lambda e: e.tensor_scalar(out=ss[:], in0=ss[:], scalar1=1.0 / D_MODEL, scalar2=EPS,
                                                      op0=ALU.mult, op1=ALU.add), r=[ss], w=[ss])
                k.op("act", lambda e: e.activation(out=ss[:], in_=ss[:], func=AF.Ln), r=[ss], w=[ss])
                k.op("act", lambda e: e.activation(out=ss[:], in_=ss[:], func=AF.Exp, scale=-0.5), r=[ss], w=[ss])
                k.op("dve", lambda e: e.scalar_tensor_tensor(out=ob[:], in0=ob[:], scalar=ss[:, 0:1], in1=fg[:],
                                                             op0=ALU.mult, op1=ALU.mult), r=[ob, ss, fg], w=[ob])
            k.dma(o_d[r0:r0 + 128, :], ob[:], r=[ob])
    k.finish()
    return k.nc


def run_E(final, rr, y_nsa, y_lru, x_l, dw_w, dw_b, ln_g, ln_b, w_pw2, w_out, final_g):
    nc = build_E(final)
    rr = rr.reshape(BATCH, SEQ, 3072)
    y_nsa = y_nsa.reshape(BATCH, SEQ, 512)
    y_lru = y_lru.reshape(BATCH, SEQ, 512)
    xf = x_l.reshape(BATCH, SEQ, D_MODEL)
    dww = np.ascontiguousarray(dw_w.T.reshape(4, 128, 31).transpose(1, 0, 2)).reshape(128, 124)
    v3 = lambda a: a.reshape(4, 128).T
    vec = np.ascontiguousarray(np.concatenate([v3(dw_b), v3(ln_g), v3(ln_b)], axis=1)).astype(np.float32)
    fg = np.ascontiguousarray(np.broadcast_to(final_g[None, :], (128, D_MODEL))).astype(np.float32)
    in_maps = []
    for core in range(NCORES):
        b, c = core // 4, core % 4
        sl = slice(c * TA, (c + 1) * TA)
        glu = rr[b][:, 1536:2560]
        gp = np.concatenate([np.zeros((30, 1024), np.float32), glu], axis=0)[c * TA: c * TA + TA + 30]
        z = np.concatenate([rr[b][sl, 0:512], rr[b][sl, 1024:1536], rr[b][sl, 2560:3072]], axis=1)
        yy = np.concatenate([y_nsa[b][sl], y_lru[b][sl]], axis=1)
        in_maps.append({"gluT": np.ascontiguousarray(gp.T), "dww": dww, "vec": vec,
                        "wp": np.ascontiguousarray(w_pw2), "yT": np.ascontiguousarray(yy.T),
                        "zT": np.ascontiguousarray(z.T), "wo": np.ascontiguousarray(w_out),
                        "x": np.ascontiguousarray(xf[b][sl]), "fg": fg})
    res = run_bass_kernel_spmd(nc, in_maps, core_ids=list(range(NCORES)))
    out = np.concatenate([r["o"] for r in res.results], axis=0)
    return out.reshape(BATCH, SEQ, D_MODEL)


def kernel(x, positions, norm_g, w_in, b_gate, cmp_pos, cmp_w1, cmp_w2, lru_conv_w, lru_conv_b,
           lru_w_a, lru_b_a, lru_w_i, lru_b_i, lru_lam, conv_dw_w, conv_dw_b, conv_ln_g, conv_ln_b,
           conv_w_pw2, w_out, final_g):
    f = lambda a: np.asarray(a)
    x = f(x).astype(np.float32, copy=False)
    positions = f(positions)
    for l in range(DEPTH):
        qkv, glin, rr = run_A(x, positions, f(norm_g)[l], f(w_in)[l])
        y_nsa = run_B(qkv, glin, f(b_gate)[l], f(cmp_pos)[l], f(cmp_w1)[l], f(cmp_w2)[l])
        y_lru = run_C(np.ascontiguousarray(rr[:, 512:1024]), f(lru_conv_w)[l], f(lru_conv_b)[l], f(lru_w_a)[l],
                      f(lru_b_a)[l], f(lru_w_i)[l], f(lru_b_i)[l], f(lru_lam)[l])
        x = run_E(l == DEPTH - 1, rr, y_nsa, y_lru, x, f(conv_dw_w)[l], f(conv_dw_b)[l], f(conv_ln_g)[l],
                  f(conv_ln_b)[l], f(conv_w_pw2)[l], f(w_out)[l], f(final_g))
    return x
```

```python
import numpy as np
import ml_dtypes
from contextlib import ExitStack
import concourse.bass as bass
import concourse.mybir as mybir
from concourse.bass_utils import run_bass_kernel_spmd

F32 = mybir.dt.float32
BF16 = mybir.dt.bfloat16
I32 = mybir.dt.int32
AF = mybir.ActivationFunctionType
ALU = mybir.AluOpType
AX = mybir.AxisListType
NPBF = ml_dtypes.bfloat16

D_MODEL = 1024
BATCH = 2
SEQ = 16384
DEPTH = 2
N_IN = 4376
EPS = 1e-6
NCORES = 8
TRACE = False


def _run(nc, in_maps, tag):
    res = run_bass_kernel_spmd(nc, in_maps, core_ids=list(range(NCORES)), trace=TRACE)
    if TRACE:
        print("EXEC", tag, res.exec_time_ns)
        global LAST_RES
        LAST_RES = res
    return res
TWO_PI = 6.283185307179586


class TT:
    def __init__(self, h):
        self.h = h
        self.lw = None
        self.rd = {}

    def __getitem__(self, idx):
        return self.h[idx]


class K:
    NDS = 12

    def __init__(self):
        self.nc = bass.Bass("TRN2", target_bir_lowering=False)
        self.es = ExitStack()
        nc = self.nc
        self.eng = {"pe": nc.tensor, "act": nc.scalar, "dve": nc.vector,
                    "pool": nc.gpsimd, "sp": nc.sync}
        self.sem = {}
        self.cnt = {}
        for e in self.eng:
            self.sem[e] = self.es.enter_context(nc.semaphore("s_" + e))
            self.cnt[e] = 0
        for i in range(self.NDS):
            k = ("d", i)
            self.sem[k] = self.es.enter_context(nc.semaphore("s_d%d" % i))
            self.cnt[k] = 0
        self.waited = {e: {} for e in self.eng}
        self.dma_rr = 0
        self.n_alloc = 0
        self.inherit = {}
        self.scope_tts = None

    def sb(self, shape, dt, name=None):
        self.n_alloc += 1
        name = name or ("t%d" % self.n_alloc)
        t = TT(self.es.enter_context(self.nc.sbuf_tensor(name, list(shape), dt)))
        t.rd = dict(self.inherit)
        if self.scope_tts is not None:
            self.scope_tts.append(t)
        return t

    def scope_begin(self):
        self.outer_es = self.es
        self.es = ExitStack()
        self.scope_tts = []

    def scope_end(self):
        for t in self.scope_tts:
            for (kk, v) in ([t.lw] if t.lw else []) + list(t.rd.items()):
                if self.inherit.get(kk, 0) < v:
                    self.inherit[kk] = v
        self.scope_tts = None
        self.es.close()
        self.es = self.outer_es

    def ps(self, shape, dt=F32, name=None):
        self.n_alloc += 1
        name = name or ("p%d" % self.n_alloc)
        return TT(self.es.enter_context(self.nc.psum_tensor(name, list(shape), dt)))

    def din(self, name, shape, dt):
        return self.nc.dram_tensor(name, list(shape), dt, kind="ExternalInput").ap()

    def dout(self, name, shape, dt):
        return self.nc.dram_tensor(name, list(shape), dt, kind="ExternalOutput").ap()

    def _wait(self, e, deps):
        for (k, v) in deps:
            if k == e and e == "pe":
                continue
            if self.waited[e].get(k, 0) >= v:
                continue
            self.eng[e].wait_ge(self.sem[k], v)
            self.waited[e][k] = v

    def _deps(self, r, w):
        deps = []
        for t in r:
            if t.lw:
                deps.append(t.lw)
        for t in w:
            if t.lw:
                deps.append(t.lw)
            deps += list(t.rd.items())
        return deps

    def op(self, e, fn, r=(), w=()):
        self._wait(e, self._deps(r, w))
        inst = fn(self.eng[e])
        self.cnt[e] += 1
        inst.then_inc(self.sem[e], 1)
        c = self.cnt[e]
        for t in r:
            t.rd[e] = c
        for t in w:
            t.lw = (e, c)
            t.rd = {}

    def dma(self, out_ap, in_ap, r=(), w=(), q="sp"):
        k = ("d", self.dma_rr % self.NDS)
        self.dma_rr += 1
        deps = self._deps(r, w)
        if self.cnt[k]:
            deps.append((k, self.cnt[k]))
        self._wait(q, deps)
        self.eng[q].dma_start(out=out_ap, in_=in_ap).then_inc(self.sem[k], 16)
        self.cnt[k] += 16
        c = self.cnt[k]
        for t in r:
            t.rd[k] = c
        for t in w:
            t.lw = (k, c)
            t.rd = {}

    def finish(self):
        for i in range(self.NDS):
            k = ("d", i)
            if self.cnt[k]:
                self.eng["sp"].wait_ge(self.sem[k], self.cnt[k])
        self.es.close()


TA = 4096


def build_A():
    k = K()
    xT = k.din("xT", [D_MODEL, TA], F32)
    w_in = k.din("w_in", [D_MODEL, N_IN], F32)
    g8 = k.din("g8", [128, 8], F32)
    pos = k.din("pos", [128, TA // 128], I32)
    invf = k.din("invf", [128, 32], F32)
    o_qkv = k.dout("o_qkv", [TA, 1280], BF16)
    o_g = k.dout("o_g", [TA, 24], F32)
    o_r = k.dout("o_r", [TA, 3072], F32)
    NT = TA // 128

    Wb = k.sb([128, 8, N_IN], BF16)
    S = [k.sb([128, N_IN], F32) for _ in range(2)]
    xg = k.sb([128, 8, 512], F32)
    xsq = k.sb([128, 8, 512], F32)
    hT = [k.sb([128, 8, 512], BF16) for _ in range(2)]
    Q = [k.sb([128, 1280], BF16) for _ in range(2)]
    g_sb = k.sb([128, 8], F32)
    pos_i = k.sb([128, NT], I32)
    pos_f = k.sb([128, NT], F32)
    invf_sb = k.sb([128, 32], F32)
    ang = k.sb([128, NT, 32], F32)
    ki = k.sb([128, NT, 32], I32)
    kf = k.sb([128, NT, 32], F32)
    red = k.sb([128, NT, 32], F32)
    cos_t = k.sb([128, NT, 32], F32)
    sin_t = k.sb([128, NT, 32], F32)
    ones = k.sb([128, 1], F32)
    rstd = k.sb([128, 4], F32)
    lnv = k.sb([128, 4], F32)
    tmp = [k.sb([128, 8, 32], F32) for _ in range(4)]
    pss = k.ps([128, 4], F32)
    PB = [k.ps([128, 512], F32) for _ in range(6)]

    k.dma(g_sb[:], g8[:, :], w=[g_sb])
    k.dma(pos_i[:], pos[:, :], w=[pos_i])
    k.dma(invf_sb[:], invf[:, :], w=[invf_sb])
    k.op("pool", lambda e: e.memset(ones[:], 1.0), w=[ones])
    for c in range(8):
        st = S[c % 2]
        k.dma(st[:], w_in[c * 128:(c + 1) * 128, :], w=[st])
        if c % 2 == 0:
            k.op("act", lambda e: e.copy(out=Wb[:, c, :], in_=st[:]), r=[st], w=[Wb])
        else:
            k.op("dve", lambda e: e.tensor_copy(out=Wb[:, c, :], in_=st[:]), r=[st], w=[Wb])
    k.op("dve", lambda e: e.tensor_copy(out=pos_f[:], in_=pos_i[:]), r=[pos_i], w=[pos_f])
    k.op("dve", lambda e: e.tensor_tensor(
        out=ang[:], in0=pos_f[:].unsqueeze(2).to_broadcast([128, NT, 32]),
        in1=invf_sb[:].unsqueeze(1).to_broadcast([128, NT, 32]), op=ALU.mult),
        r=[pos_f, invf_sb], w=[ang])
    for which, dst in ((0, sin_t), (1, cos_t)):
        if which == 1:
            k.op("dve", lambda e: e.tensor_scalar(out=ang[:], in0=ang[:], scalar1=float(np.pi / 2),
                                                  scalar2=None, op0=ALU.add), r=[ang], w=[ang])
        k.op("dve", lambda e: e.tensor_scalar(out=ki[:], in0=ang[:], scalar1=float(1.0 / TWO_PI),
                                              scalar2=None, op0=ALU.mult), r=[ang], w=[ki])
        k.op("dve", lambda e: e.tensor_copy(out=kf[:], in_=ki[:]), r=[ki], w=[kf])
        k.op("dve", lambda e: e.scalar_tensor_tensor(out=red[:], in0=kf[:], scalar=float(-TWO_PI),
                                                     in1=ang[:], op0=ALU.mult, op1=ALU.add),
             r=[kf, ang], w=[red])
        k.op("dve", lambda e: e.tensor_scalar(out=red[:], in0=red[:], scalar1=-3.1415925,
                                              scalar2=3.1415925, op0=ALU.max, op1=ALU.min),
             r=[red], w=[red])
        k.op("act", lambda e: e.activation(out=dst[:], in_=red[:], func=AF.Sin), r=[red], w=[dst])

    pb_i = 0
    segs = [(0, 8), (512, 2), (768, 2), (1024, 2)]
    for gi in range(TA // 512):
        h = hT[gi % 2]
        k.dma(xg[:], xT[:, gi * 512:(gi + 1) * 512].rearrange("(c p) t -> p c t", p=128), w=[xg])
        k.op("pool", lambda e: e.tensor_tensor(out=xsq[:], in0=xg[:], in1=xg[:], op=ALU.mult),
             r=[xg], w=[xsq])
        for c in range(8):
            k.op("dve", lambda e: e.tensor_scalar(out=h[:, c, :], in0=xg[:, c, :],
                                                  scalar1=g_sb[:, c:c + 1], scalar2=None,
                                                  op0=ALU.mult), r=[xg, g_sb], w=[h])
        for sub in range(4):
            for c in range(8):
                k.op("pe", lambda e: e.matmul(pss[:, sub:sub + 1],
                                              lhsT=xsq[:, c, sub * 128:(sub + 1) * 128],
                                              rhs=ones[:, 0:1], start=(c == 0), stop=(c == 7)),
                     r=[xsq, ones], w=[pss])
        k.op("dve", lambda e: e.tensor_scalar(out=lnv[:], in0=pss[:], scalar1=1.0 / D_MODEL,
                                              scalar2=EPS, op0=ALU.mult, op1=ALU.add),
             r=[pss], w=[lnv])
        k.op("act", lambda e: e.activation(out=lnv[:], in_=lnv[:], func=AF.Ln), r=[lnv], w=[lnv])
        k.op("act", lambda e: e.activation(out=rstd[:], in_=lnv[:], func=AF.Exp, scale=-0.5),
             r=[lnv], w=[rstd])
        for sub in range(4):
            ti = gi * 4 + sub
            st = S[ti % 2]
            qb = Q[ti % 2]
            for nt in range(9):
                n0 = nt * 512
                nw = min(512, N_IN - n0)
                pb = PB[pb_i % 6]
                pb_i += 1
                for c in range(8):
                    k.op("pe", lambda e: e.matmul(pb[:, 0:nw],
                                                  lhsT=h[:, c, sub * 128:(sub + 1) * 128],
                                                  rhs=Wb[:, c, n0:n0 + nw],
                                                  start=(c == 0), stop=(c == 7)),
                         r=[h, Wb], w=[pb])
                if nt % 2 == 0:
                    k.op("act", lambda e: e.activation(out=st[:, n0:n0 + nw], in_=pb[:, 0:nw],
                                                       func=AF.Copy, scale=rstd[:, sub:sub + 1]),
                         r=[pb, rstd], w=[st])
                else:
                    k.op("dve", lambda e: e.tensor_scalar(out=st[:, n0:n0 + nw], in0=pb[:, 0:nw],
                                                          scalar1=rstd[:, sub:sub + 1], scalar2=None,
                                                          op0=ALU.mult), r=[pb, rstd], w=[st])
            for si, (o, H) in enumerate(segs):
                eng = "dve" if si % 2 == 0 else "pool"
                u = st[:, o:o + 64 * H].rearrange("p (h two d) -> p h two d", h=H, two=2)
                qv = qb[:, o:o + 64 * H].rearrange("p (h two d) -> p h two d", h=H, two=2)
                u1 = u[:, :, 0, :]
                u2 = u[:, :, 1, :]
                cb = cos_t[:, ti, :].unsqueeze(1).to_broadcast([128, H, 32])
                sb_ = sin_t[:, ti, :].unsqueeze(1).to_broadcast([128, H, 32])
                t = [tt_[:, 0:H, :] for tt_ in tmp]
                k.op(eng, lambda e: e.tensor_tensor(out=t[0], in0=u1, in1=cb, op=ALU.mult),
                     r=[st, cos_t], w=[tmp[0]])
                k.op(eng, lambda e: e.tensor_tensor(out=t[1], in0=u2, in1=sb_, op=ALU.mult),
                     r=[st, sin_t], w=[tmp[1]])
                k.op(eng, lambda e: e.tensor_tensor(out=qv[:, :, 0, :], in0=t[0], in1=t[1],
                                                    op=ALU.subtract), r=[tmp[0], tmp[1]], w=[qb])
                k.op(eng, lambda e: e.tensor_tensor(out=t[2], in0=u2, in1=cb, op=ALU.mult),
                     r=[st, cos_t], w=[tmp[2]])
                k.op(eng, lambda e: e.tensor_tensor(out=t[3], in0=u1, in1=sb_, op=ALU.mult),
                     r=[st, sin_t], w=[tmp[3]])
                k.op(eng, lambda e: e.tensor_tensor(out=qv[:, :, 1, :], in0=t[2], in1=t[3],
                                                    op=ALU.add), r=[tmp[2], tmp[3]], w=[qb])
            vsrc = st[:, 512:1280].rearrange("p (a two d) -> p a two d", a=3, two=2)[:, :, 1, :]
            vdst = qb[:, 512:1280].rearrange("p (a two d) -> p a two d", a=3, two=2)[:, :, 1, :]
            k.op("pool", lambda e: e.tensor_copy(out=vdst, in_=vsrc), r=[st], w=[qb])
            r0 = ti * 128
            k.dma(o_qkv[r0:r0 + 128, :], qb[:], r=[qb])
            k.dma(o_g[r0:r0 + 128, :], st[:, 1280:1304], r=[st])
            k.dma(o_r[r0:r0 + 128, :], st[:, 1304:N_IN], r=[st])
    k.finish()
    return k.nc


def run_A(x_l, positions, norm_g_l, w_in_l):
    nc = build_A()
    xf = x_l.reshape(BATCH * SEQ, D_MODEL)
    pf = positions.reshape(BATCH * SEQ)
    invf = (10000.0 ** (-np.arange(0, 64, 2, dtype=np.float32) / 64)).astype(np.float32)
    invf = np.ascontiguousarray(np.broadcast_to(invf[None, :], (128, 32)))
    g8 = np.ascontiguousarray(norm_g_l.reshape(8, 128).T)
    in_maps = []
    for c in range(NCORES):
        sl = slice(c * TA, (c + 1) * TA)
        in_maps.append({
            "xT": np.ascontiguousarray(xf[sl].T),
            "w_in": np.ascontiguousarray(w_in_l),
            "g8": g8,
            "pos": np.ascontiguousarray(pf[sl].reshape(TA // 128, 128).T),
            "invf": invf,
        })
    res = _run(nc, in_maps, 'A')
    qkv = np.concatenate([r["o_qkv"] for r in res.results], axis=0)
    g = np.concatenate([r["o_g"] for r in res.results], axis=0)
    rr = np.concatenate([r["o_r"] for r in res.results], axis=0)
    return qkv, g, rr


NJ = 16
NEGM = -30000.0
SCALE = 0.125


def build_B(debug=False):
    k = K()
    qT_d = k.din("qT", [NJ, 64, 2048], BF16)
    ksT_d = k.din("ksT", [64, SEQ], BF16)
    vs1_d = k.din("vs1", [128, 128 * 65], BF16)
    kwT_d = k.din("kwT", [NJ, 64, 1024], BF16)
    vw1_d = k.din("vw1", [NJ, 128, 8 * 65], BF16)
    kcr_d = k.din("kcr", [64, SEQ + 16], BF16)
    vcr_d = k.din("vcr", [64, SEQ + 16], BF16)
    posT_d = k.din("posT", [64, 64], F32)
    w1_d = k.din("w1", [2, 64, 32 * 256], F32)
    w2_d = k.din("w2", [128, 256], F32)
    glin_d = k.din("glin", [NJ, 128, 48], F32)
    bg_d = k.din("bg", [128, 12], F32)
    G_d = k.din("G", [128, 8192], BF16)
    id_d = k.din("ident", [128, 128], BF16)
    idf_d = k.din("identf", [128, 128], F32)
    diag_d = k.din("diagm", [128, 8 * 512], BF16)
    win_d = k.din("winm", [128, 8 * 512], BF16)
    cmT_d = k.din("cmT", [NJ, 128, 1024], BF16)
    cmtok_d = k.din("cmtok", [NJ, 128, 1024], BF16)
    sA_d = k.din("sA", [NJ, 128, 1024], F32)
    sB_d = k.din("sB", [NJ, 128, 1024], F32)
    y_d = k.dout("y", [NJ, 128, 1024], F32)
    ydbg_d = k.dout("ydbg", [NJ, 128, 3072], F32) if debug else None

    ksT = k.sb([64, SEQ], BF16)
    vs1 = k.sb([128, 128, 65], BF16)
    G = k.sb([128, 8192], BF16)
    ident = k.sb([128, 128], BF16)
    diagm = k.sb([128, 8, 512], BF16)
    winm = k.sb([128, 8, 512], BF16)
    kcT = k.sb([64, 1024], BF16)
    vc1 = k.sb([128, 8, 65], BF16)
    bg = k.sb([128, 12], F32)
    k.dma(ksT[:], ksT_d[:, :], w=[ksT])
    k.dma(vs1[:].rearrange("p a c -> p (a c)"), vs1_d[:, :], w=[vs1])
    k.dma(G[:], G_d[:, :], w=[G])
    k.dma(ident[:], id_d[:, :], w=[ident])
    identf = k.sb([128, 128], F32)
    k.dma(identf[:], idf_d[:, :], w=[identf])
    k.dma(diagm[:].rearrange("p a c -> p (a c)"), diag_d[:, :], w=[diagm])
    k.dma(winm[:].rearrange("p a c -> p (a c)"), win_d[:, :], w=[winm])
    k.dma(bg[:], bg_d[:, :], w=[bg])

    ST = [k.ps([128, 512], F32) for _ in range(3)]
    MS = k.ps([128, 1024], F32)
    MS2 = k.ps([128, 1024], F32)
    TP = k.ps([128, 512], F32)

    k.scope_begin()
    raw = k.sb([64, SEQ + 16], BF16)
    w1f = k.sb([64, 32 * 256], F32)
    w1b = k.sb([64, 32, 256], BF16)
    w2f = k.sb([128, 256], F32)
    w2b = k.sb([128, 2, 2, 64], BF16)
    posT = k.sb([64, 64], F32)
    tl = [k.sb([64, 512], BF16) for _ in range(3)]
    h1s = k.sb([128, 2, 512], BF16)
    k.dma(w2f[:], w2_d[:, :], w=[w2f])
    k.dma(posT[:], posT_d[:, :], w=[posT])
    k.op("dve", lambda e: e.tensor_copy(out=w2b[:].rearrange("p a b c -> p (a b c)"), in_=w2f[:]),
         r=[w2f], w=[w2b])
    k.op("pool", lambda e: e.memset(vc1[:, :, 64:65], 1.0), w=[vc1])
    for kv in range(2):
        k.dma(raw[:], (kcr_d if kv == 0 else vcr_d)[:, :], w=[raw])
        k.dma(w1f[:], w1_d[kv, :, :], w=[w1f])
        k.op("act", lambda e: e.copy(out=w1b[:].rearrange("p a b -> p (a b)"), in_=w1f[:]),
             r=[w1f], w=[w1b])
        for nt in range(2):
            for l in range(32):
                t = tl[l % 3]
                src = raw[:, l + 16 * 512 * nt: l + 16 * 512 * nt + 16 * 511 + 1: 16]
                k.op("dve" if l % 2 == 0 else "pool",
                     lambda e: e.tensor_scalar(out=t[:], in0=src,
                                               scalar1=posT[:, kv * 32 + l: kv * 32 + l + 1],
                                               scalar2=None, op0=ALU.add),
                     r=[raw, posT], w=[t])
                for hh in range(2):
                    k.op("pe", lambda e: e.matmul(MS[:, hh * 512:(hh + 1) * 512],
                                                  lhsT=w1b[:, l, hh * 128:(hh + 1) * 128], rhs=t[:],
                                                  start=(l == 0), stop=(l == 31)),
                         r=[w1b, t], w=[MS])
            for hh in range(2):
                k.op("act", lambda e: e.activation(out=h1s[:, hh, :], in_=MS[:, hh * 512:(hh + 1) * 512],
                                                   func=AF.Silu), r=[MS], w=[h1s])
            if kv == 0:
                for hh in range(2):
                    k.op("pe", lambda e: e.matmul(ST[0][0:64, :], lhsT=w2b[:, 0, hh, :], rhs=h1s[:, hh, :],
                                                  start=(hh == 0), stop=(hh == 1)),
                         r=[w2b, h1s], w=[ST[0]])
                k.op("dve", lambda e: e.tensor_copy(out=kcT[:, nt * 512:(nt + 1) * 512], in_=ST[0][0:64, :]),
                     r=[ST[0]], w=[kcT])
            else:
                for ns in range(4):
                    for hh in range(2):
                        k.op("pe", lambda e: e.matmul(ST[0][:, ns * 64:(ns + 1) * 64],
                                                      lhsT=h1s[:, hh, ns * 128:(ns + 1) * 128],
                                                      rhs=w2b[:, 1, hh, :],
                                                      start=(hh == 0 and ns == 0), stop=(hh == 1),
                                                      skip_group_check=True),
                             r=[h1s, w2b], w=[ST[0]])
                k.op("dve", lambda e: e.tensor_copy(
                    out=vc1[:, nt * 4:(nt + 1) * 4, 0:64],
                    in_=ST[0][:, 0:256].rearrange("p (a c) -> p a c", a=4)), r=[ST[0]], w=[vc1])
    k.scope_end()

    q_j = [k.sb([64, 4, 512], BF16) for _ in range(2)]
    kw_j = [k.sb([64, 1024], BF16) for _ in range(2)]
    vw_j = [k.sb([128, 8, 65], BF16) for _ in range(2)]
    cmT_j = [k.sb([128, 2, 512], BF16) for _ in range(2)]
    cmtok_j = [k.sb([128, 4, 256], BF16) for _ in range(2)]
    sA_j = [k.sb([128, 4, 256], F32) for _ in range(2)]
    sB_j = [k.sb([128, 4, 256], F32) for _ in range(2)]
    gl_j = [k.sb([128, 4, 12], F32) for _ in range(2)]
    y_j = [k.sb([128, 4, 4, 64], F32) for _ in range(2)]
    gate = k.sb([128, 4, 12], F32)
    ydbg = k.sb([128, 3, 4, 4, 64], F32) if debug else None
    biasT = k.sb([128, 2, 512], BF16)
    PT = [k.sb([128, 512], BF16) for _ in range(4)]
    MSB = [k.sb([128, 512], BF16) for _ in range(2)]
    pcs = k.sb([128, 1032], F32)
    ee = k.sb([128, 1024], F32)
    Zc = k.sb([128, 4], F32)
    imp = k.sb([128, 256], F32)
    score = k.sb([128, 256], F32)
    sc2 = k.sb([128, 256], F32)
    m8 = k.sb([128, 8], F32)
    thr = k.sb([128, 1], F32)
    bq = k.sb([128, 256], BF16)
    coef = k.sb([128, 16], F32)
    zt = k.sb([128, 16], F32)
    cnt = {"st": 0, "pt": 0, "mk": 0}

    oT = k.sb([65, 512], F32)

    def acc_bank(r):
        return (MS if r < 2 else MS2), (r % 2) * 512

    def load_j(j):
        b = j % 2
        k.dma(q_j[b][:].rearrange("p a c -> p (a c)"), qT_d[j, :, :], w=[q_j[b]])
        k.dma(kw_j[b][:], kwT_d[j, :, :], w=[kw_j[b]])
        k.dma(vw_j[b][:].rearrange("p a c -> p (a c)"), vw1_d[j, :, :], w=[vw_j[b]])
        k.dma(cmT_j[b][:].rearrange("p a c -> p (a c)"), cmT_d[j, :, :], w=[cmT_j[b]])
        k.dma(cmtok_j[b][:].rearrange("p a c -> p (a c)"), cmtok_d[j, :, :], w=[cmtok_j[b]])
        k.dma(sA_j[b][:].rearrange("p a c -> p (a c)"), sA_d[j, :, :], w=[sA_j[b]])
        k.dma(sB_j[b][:].rearrange("p a c -> p (a c)"), sB_d[j, :, :], w=[sB_j[b]])
        k.dma(gl_j[b][:].rearrange("p a c -> p (a c)"), glin_d[j, :, :], w=[gl_j[b]])

    def branch(j, br, items):
        b = j % 2
        yj = y_j[b]
        units = [(it, r) for it in items for r in range(4)]
        bufs = {}
        mkb = {}
        first = {}
        last = {}
        for ui, (it, r) in enumerate(units):
            first.setdefault(r, ui)
            last[r] = ui

        def score(ui):
            (keys_ap, keys_tt, v_ap, v_tt, slo, shi, masks, mm), r = units[ui]
            c0, c1 = slo * 128, (shi + 1) * 128
            if mm is not None and r == 0:
                (g_ap, s_ap, mtts) = mm
                mb = MSB[cnt["mk"] % 2]
                cnt["mk"] += 1
                k.op("pe", lambda e: e.matmul(TP[:, 0:512], lhsT=g_ap, rhs=s_ap, start=True, stop=True),
                     r=mtts, w=[TP])
                k.op("dve", lambda e: e.tensor_copy(out=mb[:], in_=TP[:, 0:512]), r=[TP], w=[mb])
                mkb[id(units[ui][0])] = mb
            st = ST[cnt["st"] % 3]
            cnt["st"] += 1
            pt = PT[cnt["pt"] % 4]
            cnt["pt"] += 1
            bufs[ui] = pt
            k.op("pe", lambda e: e.matmul(st[:, c0:c1], lhsT=keys_ap, rhs=q_j[b][:, r, c0:c1],
                                          start=True, stop=(len(masks) == 0)),
                 r=[keys_tt, q_j[b]], w=[st])
            for mi, (ml, mr, mtts) in enumerate(masks):
                k.op("pe", lambda e: e.matmul(st[:, c0:c1], lhsT=ml, rhs=mr[:, c0:c1],
                                              start=False, stop=(mi == len(masks) - 1)),
                     r=mtts, w=[st])
            k.op("act", lambda e: e.activation(out=pt[:, c0:c1], in_=st[:, c0:c1], func=AF.Exp,
                                               scale=SCALE), r=[st], w=[pt])

        def mul(ui):
            (keys_ap, keys_tt, v_ap, v_tt, slo, shi, masks, mm), r = units[ui]
            if mm is None:
                return
            pt = bufs[ui]
            mb = mkb[id(units[ui][0])]
            k.op("dve" if r % 2 == 0 else "pool",
                 lambda e: e.tensor_tensor(out=pt[:], in0=pt[:], in1=mb[:], op=ALU.mult), r=[pt, mb], w=[pt])

        def pv(ui):
            (keys_ap, keys_tt, v_ap, v_tt, slo, shi, masks, mm), r = units[ui]
            c0, c1 = slo * 128, (shi + 1) * 128
            pt = bufs.pop(ui)
            a, off = acc_bank(r)
            k.op("pe", lambda e: e.matmul(a[0:65, off + c0:off + c1], lhsT=v_ap, rhs=pt[:, c0:c1],
                                          start=(ui == first[r]), stop=(ui == last[r])),
                 r=[pt, v_tt], w=[a])

        LAG = 3
        for ui in range(len(units) + LAG):
            if ui < len(units):
                score(ui)
            if 0 <= ui - 1 < len(units):
                mul(ui - 1)
            if ui - LAG >= 0:
                pv(ui - LAG)

        for r in range(4):
            a, off = acc_bank(r)
            k.op("act", lambda e: e.copy(out=oT[:], in_=a[0:65, off:off + 512]), r=[a], w=[oT])
            for sub in range(4):
                k.op("pe", lambda e: e.transpose(TP[:, sub * 65:(sub + 1) * 65], oT[:, sub * 128:(sub + 1) * 128],
                                                 identf[0:65, 0:65]), r=[oT, identf], w=[TP])
            k.op("dve", lambda e: e.tensor_scalar(
                out=zt[:, 0:4], in0=TP[:, 0:260].rearrange("p (s c) -> p s c", c=65)[:, :, 64],
                scalar1=1e-30, scalar2=None, op0=ALU.max), r=[TP], w=[zt])
            k.op("dve", lambda e: e.reciprocal(out=zt[:, 0:4], in_=zt[:, 0:4]), r=[zt], w=[zt])
            k.op("dve", lambda e: e.tensor_tensor(out=coef[:, 0:4], in0=zt[:, 0:4],
                                                  in1=gate[:, :, r * 3 + br], op=ALU.mult),
                 r=[zt, gate], w=[coef])
            for sub in range(4):
                src = TP[:, sub * 65:sub * 65 + 64]
                if debug:
                    k.op("dve", lambda e: e.tensor_scalar(out=ydbg[:, br, sub, r, :], in0=src,
                                                          scalar1=zt[:, sub:sub + 1], scalar2=None,
                                                          op0=ALU.mult), r=[TP, zt], w=[ydbg])
                if br == 0:
                    k.op("dve", lambda e: e.tensor_scalar(out=yj[:, sub, r, :], in0=src,
                                                          scalar1=coef[:, sub:sub + 1], scalar2=None,
                                                          op0=ALU.mult), r=[TP, coef], w=[yj])
                else:
                    k.op("dve", lambda e: e.scalar_tensor_tensor(out=yj[:, sub, r, :], in0=src,
                                                                 scalar=coef[:, sub:sub + 1],
                                                                 in1=yj[:, sub, r, :], op0=ALU.mult,
                                                                 op1=ALU.add), r=[TP, coef, yj], w=[yj])

    load_j(0)
    for j in range(NJ):
        b = j % 2
        if j + 1 < NJ:
            load_j(j + 1)
        qj = q_j[b]
        k.op("dve", lambda e: e.tensor_tensor(out=gate[:], in0=gl_j[b][:],
                                              in1=bg[:].unsqueeze(1).to_broadcast([128, 4, 12]), op=ALU.add),
             r=[gl_j[b], bg], w=[gate])
        k.op("act", lambda e: e.activation(out=gate[:], in_=gate[:], func=AF.Exp, scale=-1.0),
             r=[gate], w=[gate])
        k.op("dve", lambda e: e.tensor_scalar(out=gate[:], in0=gate[:], scalar1=1.0, scalar2=None,
                                              op0=ALU.add), r=[gate], w=[gate])
        k.op("dve", lambda e: e.reciprocal(out=gate[:], in_=gate[:]), r=[gate], w=[gate])

        W = 128 * (j // 2 + 1)
        mw = min(256, W)
        for sub in range(4):
            k.op("pool", lambda e: e.memset(pcs[:], 0.0), w=[pcs])
            for r in range(4):
                for c0 in range(0, W, 512):
                    c1 = min(W, c0 + 512)
                    m0 = max(c0, W - mw)
                    has_mask = m0 < c1
                    k.op("pe", lambda e: e.matmul(MS[:, c0:c1], lhsT=qj[:, r, sub * 128:(sub + 1) * 128],
                                                  rhs=kcT[:, c0:c1], start=True, stop=not has_mask),
                         r=[qj, kcT], w=[MS])
                    if has_mask:
                        k.op("pe", lambda e: e.matmul(MS[:, m0:c1], lhsT=ident[:],
                                                      rhs=cmtok_j[b][:, sub, m0 - (W - mw):c1 - (W - mw)],
                                                      start=False, stop=True),
                             r=[ident, cmtok_j[b]], w=[MS])
                k.op("act", lambda e: e.activation(out=ee[:, 0:W], in_=MS[:, 0:W], func=AF.Exp, scale=SCALE,
                                                   accum_out=Zc[:, r:r + 1]), r=[MS], w=[ee, Zc])
                k.op("dve", lambda e: e.tensor_scalar(out=Zc[:, r:r + 1], in0=Zc[:, r:r + 1], scalar1=1e-30,
                                                      scalar2=None, op0=ALU.max), r=[Zc], w=[Zc])
                k.op("dve", lambda e: e.reciprocal(out=Zc[:, r:r + 1], in_=Zc[:, r:r + 1]), r=[Zc], w=[Zc])
                k.op("dve", lambda e: e.scalar_tensor_tensor(out=pcs[:, 1:1 + W], in0=ee[:, 0:W],
                                                             scalar=Zc[:, r:r + 1], in1=pcs[:, 1:1 + W],
                                                             op0=ALU.mult, op1=ALU.add),
                     r=[ee, Zc, pcs], w=[pcs])
            k.op("dve", lambda e: e.tensor_reduce(out=imp[:], in_=pcs[:, 0:1024].rearrange("p (s c) -> p s c", c=4),
                                                  axis=AX.X, op=ALU.add), r=[pcs], w=[imp])
            k.op("dve", lambda e: e.tensor_tensor(out=imp[:], in0=imp[:], in1=pcs[:, 4:1028:4], op=ALU.add),
                 r=[imp, pcs], w=[imp])
            k.op("dve", lambda e: e.tensor_tensor(out=score[:], in0=imp[:], in1=sA_j[b][:, sub, :], op=ALU.mult),
                 r=[imp, sA_j[b]], w=[score])
            k.op("dve", lambda e: e.tensor_tensor(out=score[:], in0=score[:], in1=sB_j[b][:, sub, :], op=ALU.add),
                 r=[score, sB_j[b]], w=[score])
            k.op("dve", lambda e: e.max(out=m8[:], in_=score[:]), r=[score], w=[m8])
            k.op("dve", lambda e: e.match_replace(out=sc2[:], in_to_replace=m8[:], in_values=score[:],
                                                  imm_value=-2.0), r=[m8, score], w=[sc2])
            k.op("dve", lambda e: e.max(out=m8[:], in_=sc2[:]), r=[sc2], w=[m8])
            k.op("dve", lambda e: e.tensor_scalar(out=thr[:], in0=m8[:, 7:8], scalar1=0.0, scalar2=None,
                                                  op0=ALU.max), r=[m8], w=[thr])
            k.op("dve", lambda e: e.tensor_scalar(out=bq[:], in0=score[:], scalar1=thr[:, 0:1], scalar2=None,
                                                  op0=ALU.is_ge), r=[score, thr], w=[bq])
            MSb = MS2[:].bitcast(BF16)
            for ch in range(2):
                k.op("pe", lambda e: e.transpose(MSb[:, ch * 128:(ch + 1) * 128], bq[:, ch * 128:(ch + 1) * 128],
                                                 ident[:]), r=[bq, ident], w=[MS2])
            k.op("act", lambda e: e.activation(
                out=biasT[:, :, sub * 128:(sub + 1) * 128],
                in_=MSb[:, 0:256].rearrange("p (a c) -> p a c", a=2), func=AF.Copy),
                r=[MS2], w=[biasT])

        items = []
        for m in range(j // 2 + 1):
            masks = []
            if m >= j // 2 - 1:
                masks.append((ident[:], cmT_j[b][:, m - (j // 2 - 1), :], [ident, cmT_j[b]]))
            items.append((kcT[:, m * 128:(m + 1) * 128], kcT, vc1[:, m, :], vc1, 0, 3, masks, None))
        branch(j, 0, items)
        items = []
        for kt in range(8 * j + 8):
            masks = []
            if kt >= 8 * j:
                masks.append((ident[:], diagm[:, kt - 8 * j, :], [ident, diagm]))
            mm = (G[:, (kt % 64) * 128:(kt % 64 + 1) * 128], biasT[:, kt // 64, :], [G, biasT])
            items.append((ksT[:, kt * 128:(kt + 1) * 128], ksT, vs1[:, kt, :], vs1, 0, 3, masks, mm))
        branch(j, 1, items)
        items = []
        for kt in range(8):
            masks = [(ident[:], winm[:, kt, :], [ident, winm])]
            items.append((kw_j[b][:, kt * 128:(kt + 1) * 128], kw_j[b], vw_j[b][:, kt, :], vw_j[b],
                          max(0, kt - 4), min(3, kt), masks, None))
        branch(j, 2, items)
        k.dma(y_d[j, :, :], y_j[b][:].rearrange("p a b c -> p (a b c)"), r=[y_j[b]])
        if debug:
            k.dma(ydbg_d[j, :, :], ydbg[:].rearrange("p e a b c -> p (e a b c)"), r=[ydbg])
    k.finish()
    return k.nc


def _b_consts(p):
    c = {}
    c["G"] = (np.arange(8192)[None, :] // 64 == np.arange(128)[:, None]).astype(NPBF)
    c["ident"] = np.eye(128, dtype=np.float32).astype(NPBF)
    c["identf"] = np.eye(128, dtype=np.float32)
    kk = np.arange(128)[:, None, None]
    ii = np.arange(8)[None, :, None]
    qq = np.arange(512)[None, None, :]
    c["diagm"] = np.where(128 * ii + kk <= 512 * p + qq, 0.0, NEGM).astype(NPBF).reshape(128, 4096)
    kp = 128 * ii + kk
    qp = 512 + qq
    c["winm"] = np.where((kp <= qp) & (kp > qp - 512), 0.0, NEGM).astype(NPBF).reshape(128, 4096)
    cmT = np.zeros((NJ, 128, 2, 512), np.float32)
    cmtok = np.zeros((NJ, 128, 4, 256), np.float32)
    sA = np.zeros((NJ, 128, 4, 256), np.float32)
    sB = np.zeros((NJ, 128, 4, 256), np.float32)
    for j in range(NJ):
        gq = 2 * j + p
        for i in range(2):
            mt = j // 2 - 1 + i
            if mt < 0:
                continue
            n = 128 * mt + np.arange(128)[:, None]
            t = 512 * gq + np.arange(512)[None, :]
            cmT[j, :, i, :] = np.where((16 * n + 31 <= t) & (n <= 1022), 0.0, NEGM)
        W = 128 * (j // 2 + 1)
        mw = min(256, W)
        t = 512 * gq + 128 * np.arange(4)[None, :, None] + np.arange(128)[:, None, None]
        n = (W - mw) + np.arange(mw)[None, None, :]
        cmtok[j, :, :, 0:mw] = np.where((16 * n + 31 <= t) & (n <= 1022), 0.0, NEGM)
        s = np.arange(256)[None, None, :]
        cur = t // 64
        forced = (s == 0) | (s == cur) | (s == cur - 1)
        valid = 64 * s <= t
        sA[j] = (valid & ~forced).astype(np.float32)
        sB[j] = np.where(forced, 1e4, np.where(valid, 0.0, -1.0))
    c["cmT"] = cmT.astype(NPBF).reshape(NJ, 128, 1024)
    c["cmtok"] = cmtok.astype(NPBF).reshape(NJ, 128, 1024)
    c["sA"] = sA.reshape(NJ, 128, 1024)
    c["sB"] = sB.reshape(NJ, 128, 1024)
    return c


def run_B(qkv, glin, b_gate_l, cmp_pos_l, cmp_w1_l, cmp_w2_l, debug=False):
    nc = build_B(debug)
    qkv = qkv.reshape(BATCH, SEQ, 1280)
    glin = glin.reshape(BATCH, SEQ, 8, 3)
    consts = [_b_consts(0), _b_consts(1)]
    posT = np.ascontiguousarray(cmp_pos_l.transpose(2, 0, 1).reshape(64, 64)).astype(np.float32)
    w1 = np.ascontiguousarray(cmp_w1_l.reshape(2, 32, 64, 256).transpose(0, 2, 1, 3).reshape(2, 64, 32 * 256))
    w2 = np.ascontiguousarray(cmp_w2_l.reshape(2, 2, 128, 64).transpose(2, 0, 1, 3).reshape(128, 256))
    ones = np.ones((SEQ, 1), NPBF)
    in_maps = []
    for core in range(NCORES):
        b, g, p = core // 4, (core // 2) % 2, core % 2
        qk = qkv[b]
        tiles = [2 * j + p for j in range(NJ)]
        q = qk[:, g * 256:(g + 1) * 256].reshape(32, 512, 4, 64)[tiles]
        qT = np.ascontiguousarray(q.transpose(0, 3, 2, 1)).reshape(NJ, 64, 2048)
        sl = lambda o: qk[:, o + g * 64: o + (g + 1) * 64]
        kc, vc, ks, vs, kw, vw = sl(512), sl(640), sl(768), sl(896), sl(1024), sl(1152)
        pad16 = np.zeros((64, 16), NPBF)
        kcr = np.concatenate([np.ascontiguousarray(kc.T), pad16], axis=1)
        vcr = np.concatenate([np.ascontiguousarray(vc.T), pad16], axis=1)
        ksT = np.ascontiguousarray(ks.T)
        vs1 = np.concatenate([vs, ones], axis=1).reshape(128, 128, 65).transpose(1, 0, 2)
        vs1 = np.ascontiguousarray(vs1).reshape(128, 128 * 65)
        kwp = np.concatenate([np.zeros((512, 64), NPBF), kw], axis=0)
        vwp = np.concatenate([np.zeros((512, 65), NPBF), np.concatenate([vw, ones], axis=1)], axis=0)
        kwT = np.stack([kwp[512 * gq: 512 * gq + 1024].T for gq in tiles])
        vw1 = np.stack([vwp[512 * gq: 512 * gq + 1024].reshape(8, 128, 65).transpose(1, 0, 2).reshape(128, 520)
                        for gq in tiles])
        gl = glin[b][:, g * 4:(g + 1) * 4, :].reshape(32, 4, 128, 12)[tiles]
        gl = np.ascontiguousarray(gl.transpose(0, 2, 1, 3)).reshape(NJ, 128, 48)
        bg = np.ascontiguousarray(np.broadcast_to(b_gate_l.reshape(8, 3)[g * 4:(g + 1) * 4].reshape(1, 12), (128, 12)))
        m = {"qT": qT, "ksT": ksT, "vs1": vs1, "kwT": np.ascontiguousarray(kwT), "vw1": np.ascontiguousarray(vw1),
             "kcr": np.ascontiguousarray(kcr), "vcr": np.ascontiguousarray(vcr), "posT": posT, "w1": w1, "w2": w2,
             "glin": gl.astype(np.float32), "bg": bg.astype(np.float32)}
        m.update(consts[p])
        in_maps.append(m)
    res = _run(nc, in_maps, 'B')
    y = np.zeros((BATCH, 32, 512, 2, 256), np.float32)
    for core in range(NCORES):
        b, g, p = core // 4, (core // 2) % 2, core % 2
        yc = res.results[core]["y"].reshape(NJ, 128, 4, 256).transpose(0, 2, 1, 3).reshape(NJ, 512, 256)
        y[b, p::2, :, g, :] = yc
    if debug:
        yd = np.zeros((BATCH, 32, 512, 3, 2, 256), np.float32)
        for core in range(NCORES):
            b, g, p = core // 4, (core // 2) % 2, core % 2
            yc = res.results[core]["ydbg"].reshape(NJ, 128, 3, 4, 256).transpose(0, 3, 1, 2, 4).reshape(NJ, 512, 3, 256)
            yd[b, p::2, :, :, g, :] = yc
        return y.reshape(BATCH * SEQ, 512), yd.reshape(BATCH * SEQ, 3, 512)
    return y.reshape(BATCH * SEQ, 512)


def build_C():
    k = K()
    uT_d = k.din("uT", [128, SEQ + 3], F32)
    cw_d = k.din("cw", [128, 4], F32)
    vec_d = k.din("vec", [128, 4], F32)
    wa_d = k.din("wa", [128, 128], F32)
    wi_d = k.din("wi", [128, 128], F32)
    y_d = k.dout("y", [128, SEQ], F32)
    cw = k.sb([128, 4], F32)
    vec = k.sb([128, 4], F32)
    wf = k.sb([128, 256], F32)
    wb = k.sb([128, 256], BF16)
    cs = k.sb([128, 4], F32)
    t1 = k.sb([128, 1], F32)
    k.dma(cw[:], cw_d[:, :], w=[cw])
    k.dma(vec[:], vec_d[:, :], w=[vec])
    k.dma(wf[:, 0:128], wa_d[:, :], w=[wf])
    k.dma(wf[:, 128:256], wi_d[:, :], w=[wf])
    k.op("dve", lambda e: e.tensor_copy(out=wb[:], in_=wf[:]), r=[wf], w=[wb])
    k.op("act", lambda e: e.activation(out=t1[:], in_=vec[:, 3:4], func=AF.Exp, scale=-1.0), r=[vec], w=[t1])
    k.op("dve", lambda e: e.tensor_scalar(out=t1[:], in0=t1[:], scalar1=1.0, scalar2=None, op0=ALU.add),
         r=[t1], w=[t1])
    k.op("act", lambda e: e.activation(out=t1[:], in_=t1[:], func=AF.Ln), r=[t1], w=[t1])
    k.op("dve", lambda e: e.tensor_scalar(out=cs[:, 0:1], in0=t1[:], scalar1=-8.0, scalar2=None, op0=ALU.mult),
         r=[t1], w=[cs])
    k.op("dve", lambda e: e.tensor_scalar(out=cs[:, 1:2], in0=t1[:], scalar1=-16.0, scalar2=None, op0=ALU.mult),
         r=[t1], w=[cs])
    k.op("dve", lambda e: e.tensor_scalar(out=cs[:, 2:4], in0=vec[:, 1:3], scalar1=-1.0, scalar2=None,
                                          op0=ALU.mult), r=[vec], w=[cs])
    NTL = SEQ // 512
    u = [k.sb([128, 515], F32) for _ in range(2)]
    xc = k.sb([128, 512], F32)
    xcb = k.sb([128, 512], BF16)
    rr = k.sb([128, 512], F32)
    ig = k.sb([128, 512], F32)
    aa = k.sb([128, 512], F32)
    om = k.sb([128, 512], F32)
    bt = k.sb([128, 512], F32)
    hh = [k.sb([128, 512], F32) for _ in range(2)]
    P = [k.ps([128, 512], F32) for _ in range(2)]
    for ti in range(NTL):
        ub = u[ti % 2]
        h = hh[ti % 2]
        k.dma(ub[:], uT_d[:, ti * 512: ti * 512 + 515], w=[ub])
        k.op("dve", lambda e: e.tensor_scalar(out=xc[:], in0=ub[:, 0:512], scalar1=cw[:, 0:1], scalar2=vec[:, 0:1],
                                              op0=ALU.mult, op1=ALU.add), r=[ub, cw, vec], w=[xc])
        for kk in range(1, 4):
            k.op("dve", lambda e: e.scalar_tensor_tensor(out=xc[:], in0=ub[:, kk:kk + 512], scalar=cw[:, kk:kk + 1],
                                                         in1=xc[:], op0=ALU.mult, op1=ALU.add),
                 r=[ub, cw, xc], w=[xc])
        k.op("pool", lambda e: e.tensor_copy(out=xcb[:], in_=xc[:]), r=[xc], w=[xcb])
        for gi, dst in enumerate((rr, ig)):
            k.op("pe", lambda e: e.matmul(P[gi][:], lhsT=wb[:, gi * 128:(gi + 1) * 128], rhs=xcb[:],
                                          start=True, stop=True), r=[wb, xcb], w=[P[gi]])
            k.op("act", lambda e: e.activation(out=dst[:], in_=P[gi][:], func=AF.Exp, scale=-1.0,
                                               bias=cs[:, 2 + gi:3 + gi]), r=[P[gi], cs], w=[dst])
            k.op("pool", lambda e: e.tensor_scalar(out=dst[:], in0=dst[:], scalar1=1.0, scalar2=None,
                                                   op0=ALU.add), r=[dst], w=[dst])
            k.op("dve", lambda e: e.reciprocal(out=dst[:], in_=dst[:]), r=[dst], w=[dst])
        k.op("act", lambda e: e.activation(out=aa[:], in_=rr[:], func=AF.Exp, scale=cs[:, 0:1]),
             r=[rr, cs], w=[aa])
        k.op("act", lambda e: e.activation(out=om[:], in_=rr[:], func=AF.Exp, scale=cs[:, 1:2]),
             r=[rr, cs], w=[om])
        k.op("pool", lambda e: e.tensor_scalar(out=om[:], in0=om[:], scalar1=-1.0, scalar2=1.0,
                                               op0=ALU.mult, op1=ALU.add), r=[om], w=[om])
        k.op("pool", lambda e: e.tensor_scalar(out=om[:], in0=om[:], scalar1=1e-18, scalar2=None,
                                               op0=ALU.max), r=[om], w=[om])
        k.op("act", lambda e: e.activation(out=om[:], in_=om[:], func=AF.Ln), r=[om], w=[om])
        k.op("act", lambda e: e.activation(out=om[:], in_=om[:], func=AF.Exp, scale=0.5), r=[om], w=[om])
        k.op("dve", lambda e: e.tensor_tensor(out=bt[:], in0=ig[:], in1=xc[:], op=ALU.mult), r=[ig, xc], w=[bt])
        k.op("dve", lambda e: e.tensor_tensor(out=bt[:], in0=bt[:], in1=om[:], op=ALU.mult), r=[bt, om], w=[bt])
        if ti == 0:
            k.op("dve", lambda e: e.tensor_tensor_scan(out=h[:], data0=aa[:], data1=bt[:], initial=0.0,
                                                       op0=ALU.mult, op1=ALU.add), r=[aa, bt], w=[h])
        else:
            hp = hh[(ti - 1) % 2]
            k.op("dve", lambda e: e.tensor_tensor_scan(out=h[:], data0=aa[:], data1=bt[:], initial=hp[:, 511:512],
                                                       op0=ALU.mult, op1=ALU.add), r=[aa, bt, hp], w=[h])
        k.dma(y_d[:, ti * 512:(ti + 1) * 512], h[:], r=[h])
    k.finish()
    return k.nc


def run_C(u_lru, conv_w, conv_b, w_a, b_a, w_i, b_i, lam):
    nc = build_C()
    u = u_lru.reshape(BATCH, SEQ, 512)
    in_maps = []
    for core in range(NCORES):
        b, cq = core // 4, core % 4
        cs_ = slice(cq * 128, (cq + 1) * 128)
        uT = np.concatenate([np.zeros((128, 3), np.float32), np.ascontiguousarray(u[b][:, cs_].T)], axis=1)
        wa = np.zeros((128, 128), np.float32)
        wi = np.zeros((128, 128), np.float32)
        for hb in range(2):
            wa[hb * 64:(hb + 1) * 64, hb * 64:(hb + 1) * 64] = w_a[cq * 2 + hb]
            wi[hb * 64:(hb + 1) * 64, hb * 64:(hb + 1) * 64] = w_i[cq * 2 + hb]
        vec = np.stack([conv_b[cs_], b_a[cs_], b_i[cs_], lam[cs_]], axis=1).astype(np.float32)
        in_maps.append({"uT": np.ascontiguousarray(uT), "cw": np.ascontiguousarray(conv_w[:, cs_].T),
                        "vec": np.ascontiguousarray(vec), "wa": wa, "wi": wi})
    res = _run(nc, in_maps, 'C')
    y = np.zeros((BATCH, SEQ, 512), np.float32)
    for core in range(NCORES):
        b, cq = core // 4, core % 4
        y[b][:, cq * 128:(cq + 1) * 128] = res.results[core]["y"].T
    return y.reshape(BATCH * SEQ, 512)


def build_E(final):
    k = K()
    glu_d = k.din("gluT", [1024, TA + 30], F32)
    dww_d = k.din("dww", [128, 4 * 31], F32)
    vec_d = k.din("vec", [128, 12], F32)
    wp_d = k.din("wp", [512, 512], F32)
    yT_d = k.din("yT", [1024, TA], F32)
    zT_d = k.din("zT", [1536, TA], F32)
    wo_d = k.din("wo", [1536, 1024], F32)
    x_d = k.din("x", [TA, 1024], F32)
    fg_d = k.din("fg", [128, 1024], F32)
    o_d = k.dout("o", [TA, 1024], F32)

    Wo = k.sb([128, 12, 1024], BF16)
    Wp = k.sb([128, 4, 512], BF16)
    dww = k.sb([128, 4, 31], F32)
    vec = k.sb([128, 12], F32)
    fg = k.sb([128, 1024], F32)
    onesM = k.sb([128, 128], F32)
    xt = [k.sb([128, 1024], F32) for _ in range(2)]
    xo = [k.sb([128, 1024], F32) for _ in range(2)]
    k.dma(dww[:].rearrange("p a c -> p (a c)"), dww_d[:, :], w=[dww])
    k.dma(vec[:], vec_d[:, :], w=[vec])
    k.dma(fg[:], fg_d[:, :], w=[fg])
    k.op("pool", lambda e: e.memset(onesM[:], 1.0 / 512.0), w=[onesM])
    for c in range(12):
        st = xt[c % 2]
        k.dma(st[:], wo_d[c * 128:(c + 1) * 128, :], w=[st])
        k.op("act" if c % 2 == 0 else "dve",
             (lambda e: e.copy(out=Wo[:, c, :], in_=st[:])) if c % 2 == 0 else
             (lambda e: e.tensor_copy(out=Wo[:, c, :], in_=st[:])), r=[st], w=[Wo])
    for c in range(4):
        st = xt[c % 2]
        k.dma(st[:, 0:512], wp_d[c * 128:(c + 1) * 128, :], w=[st])
        k.op("dve", lambda e: e.tensor_copy(out=Wp[:, c, :], in_=st[:, 0:512]), r=[st], w=[Wp])

    ga = k.sb([128, 542], F32)
    gb = k.sb([128, 542], F32)
    uu = k.sb([128, 542], F32)
    cv = k.sb([128, 4, 512], F32)
    cvsq = k.sb([128, 4, 512], F32)
    mean = k.sb([128, 512], F32)
    m2 = k.sb([128, 512], F32)
    rstd = k.sb([128, 512], F32)
    un = k.sb([128, 512], F32)
    sT = k.sb([128, 4, 512], BF16)
    zt = k.sb([128, 12, 512], F32)
    yt = k.sb([128, 8, 512], F32)
    mixT = k.sb([128, 12, 512], BF16)
    sq = k.sb([128, 1024], F32)
    ss = k.sb([128, 1], F32)
    PM = k.ps([128, 512], F32)
    PV = k.ps([128, 512], F32)
    PW = [k.ps([128, 512], F32) for _ in range(2)]
    PO = [k.ps([128, 512], F32) for _ in range(2)]
    cnt = 0
    for ti in range(TA // 512):
        t0 = ti * 512
        k.dma(zt[:], zT_d[:, t0:t0 + 512].rearrange("(c p) t -> p c t", p=128), w=[zt])
        k.dma(yt[:], yT_d[:, t0:t0 + 512].rearrange("(c p) t -> p c t", p=128), w=[yt])
        for ch in range(4):
            k.dma(ga[:], glu_d[ch * 128:(ch + 1) * 128, t0:t0 + 542], w=[ga])
            k.dma(gb[:], glu_d[512 + ch * 128:512 + (ch + 1) * 128, t0:t0 + 542], w=[gb])
            k.op("act", lambda e: e.activation(out=gb[:], in_=gb[:], func=AF.Sigmoid), r=[gb], w=[gb])
            k.op("pool", lambda e: e.tensor_tensor(out=uu[:], in0=ga[:], in1=gb[:], op=ALU.mult),
                 r=[ga, gb], w=[uu])
            k.op("dve", lambda e: e.tensor_scalar(out=cv[:, ch, :], in0=uu[:, 0:512], scalar1=dww[:, ch, 0:1],
                                                  scalar2=vec[:, ch:ch + 1], op0=ALU.mult, op1=ALU.add),
                 r=[uu, dww, vec], w=[cv])
            for kk in range(1, 31):
                k.op("dve", lambda e: e.scalar_tensor_tensor(out=cv[:, ch, :], in0=uu[:, kk:kk + 512],
                                                             scalar=dww[:, ch, kk:kk + 1], in1=cv[:, ch, :],
                                                             op0=ALU.mult, op1=ALU.add), r=[uu, dww, cv], w=[cv])
        k.op("pool", lambda e: e.tensor_tensor(out=cvsq[:], in0=cv[:], in1=cv[:], op=ALU.mult), r=[cv], w=[cvsq])
        for ch in range(4):
            k.op("pe", lambda e: e.matmul(PM[:], lhsT=onesM[:], rhs=cv[:, ch, :], start=(ch == 0), stop=(ch == 3)),
                 r=[onesM, cv], w=[PM])
        for ch in range(4):
            k.op("pe", lambda e: e.matmul(PV[:], lhsT=onesM[:], rhs=cvsq[:, ch, :], start=(ch == 0), stop=(ch == 3)),
                 r=[onesM, cvsq], w=[PV])
        k.op("act", lambda e: e.copy(out=mean[:], in_=PM[:]), r=[PM], w=[mean])
        k.op("pool", lambda e: e.tensor_tensor(out=m2[:], in0=mean[:], in1=mean[:], op=ALU.mult), r=[mean], w=[m2])
        k.op("dve", lambda e: e.tensor_tensor(out=rstd[:], in0=PV[:], in1=m2[:], op=ALU.subtract),
             r=[PV, m2], w=[rstd])
        k.op("dve", lambda e: e.tensor_scalar(out=rstd[:], in0=rstd[:], scalar1=0.0, scalar2=EPS, op0=ALU.max,
                                              op1=ALU.add), r=[rstd], w=[rstd])
        k.op("act", lambda e: e.activation(out=rstd[:], in_=rstd[:], func=AF.Ln), r=[rstd], w=[rstd])
        k.op("act", lambda e: e.activation(out=rstd[:], in_=rstd[:], func=AF.Exp, scale=-0.5), r=[rstd], w=[rstd])
        for ch in range(4):
            k.op("pool", lambda e: e.tensor_tensor(out=un[:], in0=cv[:, ch, :], in1=mean[:], op=ALU.subtract),
                 r=[cv, mean], w=[un])
            k.op("dve", lambda e: e.tensor_tensor(out=un[:], in0=un[:], in1=rstd[:], op=ALU.mult),
                 r=[un, rstd], w=[un])
            k.op("act", lambda e: e.activation(out=sT[:, ch, :], in_=un[:], func=AF.Silu,
                                               scale=vec[:, 4 + ch:5 + ch], bias=vec[:, 8 + ch:9 + ch]),
                 r=[un, vec], w=[sT])
        k.op("act", lambda e: e.activation(out=zt[:], in_=zt[:], func=AF.Silu), r=[zt], w=[zt])
        k.op("dve", lambda e: e.tensor_tensor(out=mixT[:, 0:8, :], in0=yt[:], in1=zt[:, 0:8, :], op=ALU.mult),
             r=[yt, zt], w=[mixT])
        for jc in range(4):
            pw = PW[jc % 2]
            for ic in range(4):
                k.op("pe", lambda e: e.matmul(pw[:], lhsT=Wp[:, ic, jc * 128:(jc + 1) * 128], rhs=sT[:, ic, :],
                                              start=(ic == 0), stop=(ic == 3)), r=[Wp, sT], w=[pw])
            k.op("dve", lambda e: e.tensor_tensor(out=mixT[:, 8 + jc, :], in0=pw[:], in1=zt[:, 8 + jc, :],
                                                  op=ALU.mult), r=[pw, zt], w=[mixT])
        for sub in range(4):
            r0 = t0 + sub * 128
            xb = xt[cnt % 2]
            ob = xo[cnt % 2]
            cnt += 1
            k.dma(xb[:], x_d[r0:r0 + 128, :], w=[xb])
            for nh in range(2):
                po = PO[nh]
                for kc in range(12):
                    k.op("pe", lambda e: e.matmul(po[:], lhsT=mixT[:, kc, sub * 128:(sub + 1) * 128],
                                                  rhs=Wo[:, kc, nh * 512:(nh + 1) * 512],
                                                  start=(kc == 0), stop=(kc == 11)), r=[mixT, Wo], w=[po])
                k.op("dve", lambda e: e.tensor_tensor(out=ob[:, nh * 512:(nh + 1) * 512], in0=po[:],
                                                      in1=xb[:, nh * 512:(nh + 1) * 512], op=ALU.add),
                     r=[po, xb], w=[ob])
            if final:
                k.op("pool", lambda e: e.tensor_tensor(out=sq[:], in0=ob[:], in1=ob[:], op=ALU.mult), r=[ob], w=[sq])
                k.op("dve", lambda e: e.tensor_reduce(out=ss[:], in_=sq[:], axis=AX.X, op=ALU.add), r=[sq], w=[ss])
                k.op("dve", lambda e: e.tensor_scalar(out=ss[:], in0=ss[:], scalar1=1.0 / D_MODEL, scalar2=EPS,
                                                      op0=ALU.mult, op1=ALU.add), r=[ss], w=[ss])
                k.op("act", lambda e: e.activation(out=ss[:], in_=ss[:], func=AF.Ln), r=[ss], w=[ss])
                k.op("act", lambda e: e.activation(out=ss[:], in_=ss[:], func=AF.Exp, scale=-0.5), r=[ss], w=[ss])
                k.op("dve", lambda e: e.scalar_tensor_tensor(out=ob[:], in0=ob[:], scalar=ss[:, 0:1], in1=fg[:],
                                                             op0=ALU.mult, op1=ALU.mult), r=[ob, ss, fg], w=[ob])
            k.dma(o_d[r0:r0 + 128, :], ob[:], r=[ob])
    k.finish()
    return k.nc


def run_E(final, rr, y_nsa, y_lru, x_l, dw_w, dw_b, ln_g, ln_b, w_pw2, w_out, final_g):
    nc = build_E(final)
    rr = rr.reshape(BATCH, SEQ, 3072)
    y_nsa = y_nsa.reshape(BATCH, SEQ, 512)
    y_lru = y_lru.reshape(BATCH, SEQ, 512)
    xf = x_l.reshape(BATCH, SEQ, D_MODEL)
    dww = np.ascontiguousarray(dw_w.T.reshape(4, 128, 31).transpose(1, 0, 2)).reshape(128, 124)
    v3 = lambda a: a.reshape(4, 128).T
    vec = np.ascontiguousarray(np.concatenate([v3(dw_b), v3(ln_g), v3(ln_b)], axis=1)).astype(np.float32)
    fg = np.ascontiguousarray(np.broadcast_to(final_g[None, :], (128, D_MODEL))).astype(np.float32)
    in_maps = []
    for core in range(NCORES):
        b, c = core // 4, core % 4
        sl = slice(c * TA, (c + 1) * TA)
        glu = rr[b][:, 1536:2560]
        gp = np.concatenate([np.zeros((30, 1024), np.float32), glu], axis=0)[c * TA: c * TA + TA + 30]
        z = np.concatenate([rr[b][sl, 0:512], rr[b][sl, 1024:1536], rr[b][sl, 2560:3072]], axis=1)
        yy = np.concatenate([y_nsa[b][sl], y_lru[b][sl]], axis=1)
        in_maps.append({"gluT": np.ascontiguousarray(gp.T), "dww": dww, "vec": vec,
                        "wp": np.ascontiguousarray(w_pw2), "yT": np.ascontiguousarray(yy.T),
                        "zT": np.ascontiguousarray(z.T), "wo": np.ascontiguousarray(w_out),
                        "x": np.ascontiguousarray(xf[b][sl]), "fg": fg})
    res = _run(nc, in_maps, 'E')
    out = np.concatenate([r["o"] for r in res.results], axis=0)
    return out.reshape(BATCH, SEQ, D_MODEL)


def kernel(x, positions, norm_g, w_in, b_gate, cmp_pos, cmp_w1, cmp_w2, lru_conv_w, lru_conv_b,
           lru_w_a, lru_b_a, lru_w_i, lru_b_i, lru_lam, conv_dw_w, conv_dw_b, conv_ln_g, conv_ln_b,
           conv_w_pw2, w_out, final_g):
    f = lambda a: np.asarray(a)
    x = f(x).astype(np.float32, copy=False)
    positions = f(positions)
    for l in range(DEPTH):
        qkv, glin, rr = run_A(x, positions, f(norm_g)[l], f(w_in)[l])
        y_nsa = run_B(qkv, glin, f(b_gate)[l], f(cmp_pos)[l], f(cmp_w1)[l], f(cmp_w2)[l])
        y_lru = run_C(np.ascontiguousarray(rr[:, 512:1024]), f(lru_conv_w)[l], f(lru_conv_b)[l], f(lru_w_a)[l],
                      f(lru_b_a)[l], f(lru_w_i)[l], f(lru_b_i)[l], f(lru_lam)[l])
        x = run_E(l == DEPTH - 1, rr, y_nsa, y_lru, x, f(conv_dw_w)[l], f(conv_dw_b)[l], f(conv_ln_g)[l],
                  f(conv_ln_b)[l], f(conv_w_pw2)[l], f(w_out)[l], f(final_g))
    return x
```

```python
import numpy as np
import ml_dtypes
from contextlib import ExitStack
import concourse.bass as bass
import concourse.mybir as mybir
from concourse.bass_utils import run_bass_kernel_spmd

F32 = mybir.dt.float32
BF16 = mybir.dt.bfloat16
I32 = mybir.dt.int32
AF = mybir.ActivationFunctionType
ALU = mybir.AluOpType
AX = mybir.AxisListType
NPBF = ml_dtypes.bfloat16

D_MODEL = 1024
BATCH = 2
SEQ = 16384
DEPTH = 2
N_IN = 4376
EPS = 1e-6
NCORES = 8
TRACE = False


def _run(nc, in_maps, tag):
    res = run_bass_kernel_spmd(nc, in_maps, core_ids=list(range(NCORES)), trace=TRACE)
    if TRACE:
        print("EXEC", tag, res.exec_time_ns)
        global LAST_RES
        LAST_RES = res
    return res
TWO_PI = 6.283185307179586


class TT:
    def __init__(self, h):
        self.h = h
        self.lw = None
        self.rd = {}

    def __getitem__(self, idx):
        return self.h[idx]


class K:
    NDS = 12

    def __init__(self):
        self.nc = bass.Bass("TRN2", target_bir_lowering=False)
        self.es = ExitStack()
        nc = self.nc
        self.eng = {"pe": nc.tensor, "act": nc.scalar, "dve": nc.vector,
                    "pool": nc.gpsimd, "sp": nc.sync}
        self.sem = {}
        self.cnt = {}
        for e in self.eng:
            self.sem[e] = self.es.enter_context(nc.semaphore("s_" + e))
            self.cnt[e] = 0
        for i in range(self.NDS):
            k = ("d", i)
            self.sem[k] = self.es.enter_context(nc.semaphore("s_d%d" % i))
            self.cnt[k] = 0
        self.waited = {e: {} for e in self.eng}
        self.dma_rr = 0
        self.n_alloc = 0
        self.inherit = {}
        self.scope_tts = None

    def sb(self, shape, dt, name=None):
        self.n_alloc += 1
        name = name or ("t%d" % self.n_alloc)
        t = TT(self.es.enter_context(self.nc.sbuf_tensor(name, list(shape), dt)))
        t.rd = dict(self.inherit)
        if self.scope_tts is not None:
            self.scope_tts.append(t)
        return t

    def scope_begin(self):
        self.outer_es = self.es
        self.es = ExitStack()
        self.scope_tts = []

    def scope_end(self):
        for t in self.scope_tts:
            for (kk, v) in ([t.lw] if t.lw else []) + list(t.rd.items()):
                if self.inherit.get(kk, 0) < v:
                    self.inherit[kk] = v
        self.scope_tts = None
        self.es.close()
        self.es = self.outer_es

    def ps(self, shape, dt=F32, name=None):
        self.n_alloc += 1
        name = name or ("p%d" % self.n_alloc)
        return TT(self.es.enter_context(self.nc.psum_tensor(name, list(shape), dt)))

    def din(self, name, shape, dt):
        return self.nc.dram_tensor(name, list(shape), dt, kind="ExternalInput").ap()

    def dout(self, name, shape, dt):
        return self.nc.dram_tensor(name, list(shape), dt, kind="ExternalOutput").ap()

    def _wait(self, e, deps):
        for (k, v) in deps:
            if k == e and e == "pe":
                continue
            if self.waited[e].get(k, 0) >= v:
                continue
            self.eng[e].wait_ge(self.sem[k], v)
            self.waited[e][k] = v

    def _deps(self, r, w):
        deps = []
        for t in r:
            if t.lw:
                deps.append(t.lw)
        for t in w:
            if t.lw:
                deps.append(t.lw)
            deps += list(t.rd.items())
        return deps

    def op(self, e, fn, r=(), w=()):
        self._wait(e, self._deps(r, w))
        inst = fn(self.eng[e])
        self.cnt[e] += 1
        inst.then_inc(self.sem[e], 1)
        c = self.cnt[e]
        for t in r:
            t.rd[e] = c
        for t in w:
            t.lw = (e, c)
            t.rd = {}

    def dma(self, out_ap, in_ap, r=(), w=(), q="sp"):
        k = ("d", self.dma_rr % self.NDS)
        self.dma_rr += 1
        deps = self._deps(r, w)
        if self.cnt[k]:
            deps.append((k, self.cnt[k]))
        self._wait(q, deps)
        self.eng[q].dma_start(out=out_ap, in_=in_ap).then_inc(self.sem[k], 16)
        self.cnt[k] += 16
        c = self.cnt[k]
        for t in r:
            t.rd[k] = c
        for t in w:
            t.lw = (k, c)
            t.rd = {}

    def finish(self):
        for i in range(self.NDS):
            k = ("d", i)
            if self.cnt[k]:
                self.eng["sp"].wait_ge(self.sem[k], self.cnt[k])
        self.es.close()


TA = 4096


def build_A():
    k = K()
    xT = k.din("xT", [D_MODEL, TA], F32)
    w_in = k.din("w_in", [D_MODEL, N_IN], F32)
    g8 = k.din("g8", [128, 8], F32)
    pos = k.din("pos", [128, TA // 128], I32)
    invf = k.din("invf", [128, 32], F32)
    o_qkv = k.dout("o_qkv", [TA, 1280], BF16)
    o_g = k.dout("o_g", [TA, 24], F32)
    o_r = k.dout("o_r", [TA, 3072], F32)
    NT = TA // 128

    Wb = k.sb([128, 8, N_IN], BF16)
    S = [k.sb([128, N_IN], F32) for _ in range(2)]
    xg = k.sb([128, 8, 512], F32)
    xsq = k.sb([128, 8, 512], F32)
    hT = [k.sb([128, 8, 512], BF16) for _ in range(2)]
    Q = [k.sb([128, 1280], BF16) for _ in range(2)]
    g_sb = k.sb([128, 8], F32)
    pos_i = k.sb([128, NT], I32)
    pos_f = k.sb([128, NT], F32)
    invf_sb = k.sb([128, 32], F32)
    ang = k.sb([128, NT, 32], F32)
    ki = k.sb([128, NT, 32], I32)
    kf = k.sb([128, NT, 32], F32)
    red = k.sb([128, NT, 32], F32)
    cos_t = k.sb([128, NT, 32], F32)
    sin_t = k.sb([128, NT, 32], F32)
    ones = k.sb([128, 1], F32)
    rstd = k.sb([128, 4], F32)
    lnv = k.sb([128, 4], F32)
    tmp = [k.sb([128, 8, 32], F32) for _ in range(4)]
    pss = k.ps([128, 4], F32)
    PB = [k.ps([128, 512], F32) for _ in range(6)]

    k.dma(g_sb[:], g8[:, :], w=[g_sb])
    k.dma(pos_i[:], pos[:, :], w=[pos_i])
    k.dma(invf_sb[:], invf[:, :], w=[invf_sb])
    k.op("pool", lambda e: e.memset(ones[:], 1.0), w=[ones])
    for c in range(8):
        st = S[c % 2]
        k.dma(st[:], w_in[c * 128:(c + 1) * 128, :], w=[st])
        if c % 2 == 0:
            k.op("act", lambda e: e.copy(out=Wb[:, c, :], in_=st[:]), r=[st], w=[Wb])
        else:
            k.op("dve", lambda e: e.tensor_copy(out=Wb[:, c, :], in_=st[:]), r=[st], w=[Wb])
    k.op("dve", lambda e: e.tensor_copy(out=pos_f[:], in_=pos_i[:]), r=[pos_i], w=[pos_f])
    k.op("dve", lambda e: e.tensor_tensor(
        out=ang[:], in0=pos_f[:].unsqueeze(2).to_broadcast([128, NT, 32]),
        in1=invf_sb[:].unsqueeze(1).to_broadcast([128, NT, 32]), op=ALU.mult),
        r=[pos_f, invf_sb], w=[ang])
    for which, dst in ((0, sin_t), (1, cos_t)):
        if which == 1:
            k.op("dve", lambda e: e.tensor_scalar(out=ang[:], in0=ang[:], scalar1=float(np.pi / 2),
                                                  scalar2=None, op0=ALU.add), r=[ang], w=[ang])
        k.op("dve", lambda e: e.tensor_scalar(out=ki[:], in0=ang[:], scalar1=float(1.0 / TWO_PI),
                                              scalar2=None, op0=ALU.mult), r=[ang], w=[ki])
        k.op("dve", lambda e: e.tensor_copy(out=kf[:], in_=ki[:]), r=[ki], w=[kf])
        k.op("dve", lambda e: e.scalar_tensor_tensor(out=red[:], in0=kf[:], scalar=float(-TWO_PI),
                                                     in1=ang[:], op0=ALU.mult, op1=ALU.add),
             r=[kf, ang], w=[red])
        k.op("dve", lambda e: e.tensor_scalar(out=red[:], in0=red[:], scalar1=-3.1415925,
                                              scalar2=3.1415925, op0=ALU.max, op1=ALU.min),
             r=[red], w=[red])
        k.op("act", lambda e: e.activation(out=dst[:], in_=red[:], func=AF.Sin), r=[red], w=[dst])

    pb_i = 0
    segs = [(0, 8), (512, 2), (768, 2), (1024, 2)]
    for gi in range(TA // 512):
        h = hT[gi % 2]
        k.dma(xg[:], xT[:, gi * 512:(gi + 1) * 512].rearrange("(c p) t -> p c t", p=128), w=[xg])
        k.op("act", lambda e: e.activation(out=xsq[:], in_=xg[:], func=AF.Square), r=[xg], w=[xsq])
        for c in range(8):
            k.op("dve", lambda e: e.tensor_scalar(out=h[:, c, :], in0=xg[:, c, :],
                                                  scalar1=g_sb[:, c:c + 1], scalar2=None,
                                                  op0=ALU.mult), r=[xg, g_sb], w=[h])
        for sub in range(4):
            for c in range(8):
                k.op("pe", lambda e: e.matmul(pss[:, sub:sub + 1],
                                              lhsT=xsq[:, c, sub * 128:(sub + 1) * 128],
                                              rhs=ones[:, 0:1], start=(c == 0), stop=(c == 7)),
                     r=[xsq, ones], w=[pss])
        k.op("dve", lambda e: e.tensor_scalar(out=lnv[:], in0=pss[:], scalar1=1.0 / D_MODEL,
                                              scalar2=EPS, op0=ALU.mult, op1=ALU.add),
             r=[pss], w=[lnv])
        k.op("act", lambda e: e.activation(out=lnv[:], in_=lnv[:], func=AF.Ln), r=[lnv], w=[lnv])
        k.op("act", lambda e: e.activation(out=rstd[:], in_=lnv[:], func=AF.Exp, scale=-0.5),
             r=[lnv], w=[rstd])
        for sub in range(4):
            ti = gi * 4 + sub
            st = S[ti % 2]
            qb = Q[ti % 2]
            for nt in range(9):
                n0 = nt * 512
                nw = min(512, N_IN - n0)
                pb = PB[pb_i % 6]
                pb_i += 1
                for c in range(8):
                    k.op("pe", lambda e: e.matmul(pb[:, 0:nw],
                                                  lhsT=h[:, c, sub * 128:(sub + 1) * 128],
                                                  rhs=Wb[:, c, n0:n0 + nw],
                                                  start=(c == 0), stop=(c == 7)),
                         r=[h, Wb], w=[pb])
                if nt % 2 == 0:
                    k.op("act", lambda e: e.activation(out=st[:, n0:n0 + nw], in_=pb[:, 0:nw],
                                                       func=AF.Copy, scale=rstd[:, sub:sub + 1]),
                         r=[pb, rstd], w=[st])
                else:
                    k.op("dve", lambda e: e.tensor_scalar(out=st[:, n0:n0 + nw], in0=pb[:, 0:nw],
                                                          scalar1=rstd[:, sub:sub + 1], scalar2=None,
                                                          op0=ALU.mult), r=[pb, rstd], w=[st])
            for si, (o, H) in enumerate(segs):
                eng = "dve"
                u = st[:, o:o + 64 * H].rearrange("p (h two d) -> p h two d", h=H, two=2)
                qv = qb[:, o:o + 64 * H].rearrange("p (h two d) -> p h two d", h=H, two=2)
                u1 = u[:, :, 0, :]
                u2 = u[:, :, 1, :]
                cb = cos_t[:, ti, :].unsqueeze(1).to_broadcast([128, H, 32])
                sb_ = sin_t[:, ti, :].unsqueeze(1).to_broadcast([128, H, 32])
                t = [tt_[:, 0:H, :] for tt_ in tmp]
                k.op(eng, lambda e: e.tensor_tensor(out=t[0], in0=u1, in1=cb, op=ALU.mult),
                     r=[st, cos_t], w=[tmp[0]])
                k.op(eng, lambda e: e.tensor_tensor(out=t[1], in0=u2, in1=sb_, op=ALU.mult),
                     r=[st, sin_t], w=[tmp[1]])
                k.op(eng, lambda e: e.tensor_tensor(out=qv[:, :, 0, :], in0=t[0], in1=t[1],
                                                    op=ALU.subtract), r=[tmp[0], tmp[1]], w=[qb])
                k.op(eng, lambda e: e.tensor_tensor(out=t[2], in0=u2, in1=cb, op=ALU.mult),
                     r=[st, cos_t], w=[tmp[2]])
                k.op(eng, lambda e: e.tensor_tensor(out=t[3], in0=u1, in1=sb_, op=ALU.mult),
                     r=[st, sin_t], w=[tmp[3]])
                k.op(eng, lambda e: e.tensor_tensor(out=qv[:, :, 1, :], in0=t[2], in1=t[3],
                                                    op=ALU.add), r=[tmp[2], tmp[3]], w=[qb])
            vsrc = st[:, 512:1280].rearrange("p (a two d) -> p a two d", a=3, two=2)[:, :, 1, :]
            vdst = qb[:, 512:1280].rearrange("p (a two d) -> p a two d", a=3, two=2)[:, :, 1, :]
            k.op("act", lambda e: e.copy(out=vdst, in_=vsrc), r=[st], w=[qb])
            r0 = ti * 128
            k.dma(o_qkv[r0:r0 + 128, :], qb[:], r=[qb])
            k.dma(o_g[r0:r0 + 128, :], st[:, 1280:1304], r=[st])
            k.dma(o_r[r0:r0 + 128, :], st[:, 1304:N_IN], r=[st])
    k.finish()
    return k.nc


def run_A(x_l, positions, norm_g_l, w_in_l):
    nc = build_A()
    xf = x_l.reshape(BATCH * SEQ, D_MODEL)
    pf = positions.reshape(BATCH * SEQ)
    invf = (10000.0 ** (-np.arange(0, 64, 2, dtype=np.float32) / 64)).astype(np.float32)
    invf = np.ascontiguousarray(np.broadcast_to(invf[None, :], (128, 32)))
    g8 = np.ascontiguousarray(norm_g_l.reshape(8, 128).T)
    in_maps = []
    for c in range(NCORES):
        sl = slice(c * TA, (c + 1) * TA)
        in_maps.append({
            "xT": np.ascontiguousarray(xf[sl].T),
            "w_in": np.ascontiguousarray(w_in_l),
            "g8": g8,
            "pos": np.ascontiguousarray(pf[sl].reshape(TA // 128, 128).T),
            "invf": invf,
        })
    res = _run(nc, in_maps, 'A')
    qkv = np.concatenate([r["o_qkv"] for r in res.results], axis=0)
    g = np.concatenate([r["o_g"] for r in res.results], axis=0)
    rr = np.concatenate([r["o_r"] for r in res.results], axis=0)
    return qkv, g, rr


NJ = 16
NEGM = -30000.0
SCALE = 0.125


def build_B(debug=False):
    k = K()
    qT_d = k.din("qT", [NJ, 128, 1024], BF16)
    ksT_d = k.din("ksT", [64, SEQ], BF16)
    vs1_d = k.din("vs1", [128, 128 * 65], BF16)
    kwT_d = k.din("kwT", [NJ, 64, 1024], BF16)
    vw1_d = k.din("vw1", [NJ, 128, 8 * 65], BF16)
    kcr_d = k.din("kcr", [64, SEQ + 16], BF16)
    vcr_d = k.din("vcr", [64, SEQ + 16], BF16)
    posT_d = k.din("posT", [64, 64], F32)
    w1_d = k.din("w1", [2, 64, 32 * 256], F32)
    w2_d = k.din("w2", [128, 256], F32)
    glin_d = k.din("glin", [NJ, 128, 48], F32)
    bg_d = k.din("bg", [128, 12], F32)
    G_d = k.din("G", [128, 8192], BF16)
    id_d = k.din("ident", [128, 128], BF16)
    idf_d = k.din("identf", [128, 128], F32)
    diag_d = k.din("diagm", [128, 8 * 512], BF16)
    win_d = k.din("winm", [128, 8 * 512], BF16)
    cmT_d = k.din("cmT", [NJ, 128, 1024], BF16)
    cmtok_d = k.din("cmtok", [NJ, 128, 1024], BF16)
    sA_d = k.din("sA", [NJ, 128, 1024], F32)
    sB_d = k.din("sB", [NJ, 128, 1024], F32)
    y_d = k.dout("y", [NJ, 128, 1024], F32)
    ydbg_d = k.dout("ydbg", [NJ, 128, 3072], F32) if debug else None

    ksT = k.sb([128, SEQ], BF16)
    vs1 = k.sb([128, 128, 65], BF16)
    G = k.sb([128, 8192], BF16)
    ident = k.sb([128, 128], BF16)
    diagm = k.sb([128, 8, 512], BF16)
    winm = k.sb([128, 8, 512], BF16)
    kcT = k.sb([128, 1024], BF16)
    vc1 = k.sb([128, 8, 65], BF16)
    bg = k.sb([128, 12], F32)
    k.dma(ksT[0:64, :], ksT_d[:, :], w=[ksT])
    k.dma(ksT[64:128, :], ksT_d[:, :], w=[ksT])
    k.dma(vs1[:].rearrange("p a c -> p (a c)"), vs1_d[:, :], w=[vs1])
    k.dma(G[:], G_d[:, :], w=[G])
    k.dma(ident[:], id_d[:, :], w=[ident])
    identf = k.sb([128, 128], F32)
    k.dma(identf[:], idf_d[:, :], w=[identf])
    k.dma(diagm[:].rearrange("p a c -> p (a c)"), diag_d[:, :], w=[diagm])
    k.dma(winm[:].rearrange("p a c -> p (a c)"), win_d[:, :], w=[winm])
    k.dma(bg[:], bg_d[:, :], w=[bg])

    ST = [k.ps([128, 512], F32) for _ in range(4)]
    MS = k.ps([128, 1024], F32)
    MS2 = k.ps([128, 1024], F32)
    TP = ST[0]

    k.scope_begin()
    raw = k.sb([64, SEQ + 16], BF16)
    w1f = k.sb([64, 32 * 256], F32)
    w1b = k.sb([64, 32, 256], BF16)
    w2f = k.sb([128, 256], F32)
    w2b = k.sb([128, 2, 2, 64], BF16)
    posT = k.sb([64, 64], F32)
    tl = [k.sb([64, 512], BF16) for _ in range(3)]
    h1s = k.sb([128, 2, 512], BF16)
    k.dma(w2f[:], w2_d[:, :], w=[w2f])
    k.dma(posT[:], posT_d[:, :], w=[posT])
    k.op("dve", lambda e: e.tensor_copy(out=w2b[:].rearrange("p a b c -> p (a b c)"), in_=w2f[:]),
         r=[w2f], w=[w2b])
    k.op("pool", lambda e: e.memset(vc1[:, :, 64:65], 1.0), w=[vc1])
    for kv in range(2):
        k.dma(raw[:], (kcr_d if kv == 0 else vcr_d)[:, :], w=[raw])
        k.dma(w1f[:], w1_d[kv, :, :], w=[w1f])
        k.op("act", lambda e: e.copy(out=w1b[:].rearrange("p a b -> p (a b)"), in_=w1f[:]),
             r=[w1f], w=[w1b])
        for nt in range(2):
            for l in range(32):
                t = tl[l % 3]
                src = raw[:, l + 16 * 512 * nt: l + 16 * 512 * nt + 16 * 511 + 1: 16]
                k.op("dve" if l % 2 == 0 else "pool",
                     lambda e: e.tensor_scalar(out=t[:], in0=src,
                                               scalar1=posT[:, kv * 32 + l: kv * 32 + l + 1],
                                               scalar2=None, op0=ALU.add),
                     r=[raw, posT], w=[t])
                for hh in range(2):
                    k.op("pe", lambda e: e.matmul(MS[:, hh * 512:(hh + 1) * 512],
                                                  lhsT=w1b[:, l, hh * 128:(hh + 1) * 128], rhs=t[:],
                                                  start=(l == 0), stop=(l == 31)),
                         r=[w1b, t], w=[MS])
            for hh in range(2):
                k.op("act", lambda e: e.activation(out=h1s[:, hh, :], in_=MS[:, hh * 512:(hh + 1) * 512],
                                                   func=AF.Silu), r=[MS], w=[h1s])
            if kv == 0:
                for hh in range(2):
                    k.op("pe", lambda e: e.matmul(ST[0][0:64, :], lhsT=w2b[:, 0, hh, :], rhs=h1s[:, hh, :],
                                                  start=(hh == 0), stop=(hh == 1)),
                         r=[w2b, h1s], w=[ST[0]])
                k.op("dve", lambda e: e.tensor_copy(out=kcT[0:64, nt * 512:(nt + 1) * 512], in_=ST[0][0:64, :]),
                     r=[ST[0]], w=[kcT])
                k.op("act", lambda e: e.copy(out=kcT[64:128, nt * 512:(nt + 1) * 512], in_=ST[0][0:64, :]),
                     r=[ST[0]], w=[kcT])
            else:
                for ns in range(4):
                    for hh in range(2):
                        k.op("pe", lambda e: e.matmul(ST[0][:, ns * 64:(ns + 1) * 64],
                                                      lhsT=h1s[:, hh, ns * 128:(ns + 1) * 128],
                                                      rhs=w2b[:, 1, hh, :],
                                                      start=(hh == 0 and ns == 0), stop=(hh == 1),
                                                      skip_group_check=True),
                             r=[h1s, w2b], w=[ST[0]])
                k.op("dve", lambda e: e.tensor_copy(
                    out=vc1[:, nt * 4:(nt + 1) * 4, 0:64],
                    in_=ST[0][:, 0:256].rearrange("p (a c) -> p a c", a=4)), r=[ST[0]], w=[vc1])
    k.scope_end()

    q_j = [k.sb([128, 2, 512], BF16) for _ in range(2)]
    kw_j = [k.sb([128, 1024], BF16) for _ in range(2)]
    vw_j = [k.sb([128, 8, 65], BF16) for _ in range(2)]
    cmT_j = [k.sb([128, 2, 512], BF16) for _ in range(2)]
    cmtok_j = [k.sb([128, 4, 256], BF16) for _ in range(2)]
    sA_j = [k.sb([128, 4, 256], F32) for _ in range(2)]
    sB_j = [k.sb([128, 4, 256], F32) for _ in range(2)]
    gl_j = [k.sb([128, 4, 12], F32) for _ in range(2)]
    y_j = [k.sb([128, 4, 4, 64], F32) for _ in range(2)]
    gate = k.sb([128, 4, 12], F32)
    ydbg = k.sb([128, 3, 4, 4, 64], F32) if debug else None
    biasT = k.sb([128, 2, 512], BF16)
    PT = [k.sb([128, 512], BF16) for _ in range(6)]
    MSB = [k.sb([128, 512], BF16) for _ in range(2)]
    pcs = k.sb([128, 1032], F32)
    ee = k.sb([128, 1024], F32)
    Zc = k.sb([128, 4], F32)
    imp = k.sb([128, 256], F32)
    score = k.sb([128, 256], F32)
    sc2 = k.sb([128, 256], F32)
    m8 = k.sb([128, 8], F32)
    thr = k.sb([128, 1], F32)
    bq = k.sb([128, 256], BF16)
    coef = k.sb([128, 16], F32)
    zt = k.sb([128, 16], F32)
    cnt = {"st": 0, "pt": 0, "mk": 0}

    oT = k.sb([65, 512], F32)

    def acc_bank(r):
        return (MS if r < 2 else MS2), (r % 2) * 512

    def load_j(j):
        b = j % 2
        k.dma(q_j[b][:].rearrange("p a c -> p (a c)"), qT_d[j, :, :], w=[q_j[b]])
        k.dma(kw_j[b][0:64, :], kwT_d[j, :, :], w=[kw_j[b]])
        k.dma(kw_j[b][64:128, :], kwT_d[j, :, :], w=[kw_j[b]])
        k.dma(vw_j[b][:].rearrange("p a c -> p (a c)"), vw1_d[j, :, :], w=[vw_j[b]])
        k.dma(cmT_j[b][:].rearrange("p a c -> p (a c)"), cmT_d[j, :, :], w=[cmT_j[b]])
        k.dma(cmtok_j[b][:].rearrange("p a c -> p (a c)"), cmtok_d[j, :, :], w=[cmtok_j[b]])
        k.dma(sA_j[b][:].rearrange("p a c -> p (a c)"), sA_d[j, :, :], w=[sA_j[b]])
        k.dma(sB_j[b][:].rearrange("p a c -> p (a c)"), sB_d[j, :, :], w=[sB_j[b]])
        k.dma(gl_j[b][:].rearrange("p a c -> p (a c)"), glin_d[j, :, :], w=[gl_j[b]])

    def branch(j, br, items):
        b = j % 2
        yj = y_j[b]
        units = [(it, r) for it in items for r in range(4)]
        bufs = {}
        mkb = {}
        first = {}
        last = {}
        for ui, (it, r) in enumerate(units):
            first.setdefault(r, ui)
            last[r] = ui

        def score_pair(pi):
            us = [u for u in (2 * pi, 2 * pi + 1) if u < len(units)]
            sts = {}
            for ui in us:
                (keys_fn, keys_tt, v_ap, v_tt, slo, shi, masks, mm), r = units[ui]
                if mm is not None and r == 0:
                    (g_ap, s_ap, mtts) = mm
                    mb = MSB[cnt["mk"] % 2]
                    cnt["mk"] += 1
                    tp = ST[cnt["st"] % 4]
                    cnt["st"] += 1
                    k.op("pe", lambda e: e.matmul(tp[:, 0:512], lhsT=g_ap, rhs=s_ap, start=True, stop=True),
                         r=mtts, w=[tp])
                    k.op("dve", lambda e: e.tensor_copy(out=mb[:], in_=tp[:, 0:512]), r=[tp], w=[mb])
                    mkb[id(units[ui][0])] = mb
            for ui in us:
                (keys_fn, keys_tt, v_ap, v_tt, slo, shi, masks, mm), r = units[ui]
                c0, c1 = slo * 128, (shi + 1) * 128
                st = ST[cnt["st"] % 4]
                cnt["st"] += 1
                sts[ui] = st
                pt = PT[cnt["pt"] % 6]
                cnt["pt"] += 1
                bufs[ui] = pt
                h0 = (r % 2) * 64
                k.op("pe", lambda e: e.matmul(st[:, c0:c1], lhsT=keys_fn(h0), rhs=q_j[b][h0:h0 + 64, r // 2, c0:c1],
                                              start=True, stop=(len(masks) == 0)),
                     r=[keys_tt, q_j[b]], w=[st])
            for ui in us:
                (keys_fn, keys_tt, v_ap, v_tt, slo, shi, masks, mm), r = units[ui]
                c0, c1 = slo * 128, (shi + 1) * 128
                st = sts[ui]
                for mi, (ml, mr, mtts) in enumerate(masks):
                    k.op("pe", lambda e: e.matmul(st[:, c0:c1], lhsT=ml, rhs=mr[:, c0:c1],
                                                  start=False, stop=(mi == len(masks) - 1)),
                         r=mtts, w=[st])
            for ui in us:
                (keys_fn, keys_tt, v_ap, v_tt, slo, shi, masks, mm), r = units[ui]
                c0, c1 = slo * 128, (shi + 1) * 128
                pt = bufs[ui]
                st = sts[ui]
                k.op("act", lambda e: e.activation(out=pt[:, c0:c1], in_=st[:, c0:c1], func=AF.Exp,
                                                   scale=SCALE), r=[st], w=[pt])

        def mul(ui):
            (keys_fn, keys_tt, v_ap, v_tt, slo, shi, masks, mm), r = units[ui]
            if mm is None:
                return
            pt = bufs[ui]
            mb = mkb[id(units[ui][0])]
            k.op("dve" if r % 2 == 0 else "pool",
                 lambda e: e.tensor_tensor(out=pt[:], in0=pt[:], in1=mb[:], op=ALU.mult), r=[pt, mb], w=[pt])

        def pv(ui):
            (keys_fn, keys_tt, v_ap, v_tt, slo, shi, masks, mm), r = units[ui]
            c0, c1 = slo * 128, (shi + 1) * 128
            pt = bufs.pop(ui)
            a, off = acc_bank(r)
            k.op("pe", lambda e: e.matmul(a[0:65, off + c0:off + c1], lhsT=v_ap, rhs=pt[:, c0:c1],
                                          start=(ui == first[r]), stop=(ui == last[r])),
                 r=[pt, v_tt], w=[a])

        npairs = (len(units) + 1) // 2
        LAG = 2
        for pi in range(npairs + LAG):
            if pi < npairs:
                score_pair(pi)
            if 0 <= pi - 1 < npairs:
                for ui in (2 * (pi - 1), 2 * (pi - 1) + 1):
                    if ui < len(units):
                        mul(ui)
            if pi - LAG >= 0:
                for ui in (2 * (pi - LAG), 2 * (pi - LAG) + 1):
                    if ui < len(units):
                        pv(ui)

        for r in range(4):
            a, off = acc_bank(r)
            k.op("act", lambda e: e.copy(out=oT[:], in_=a[0:65, off:off + 512]), r=[a], w=[oT])
            for sub in range(4):
                k.op("pe", lambda e: e.transpose(TP[:, sub * 65:(sub + 1) * 65], oT[:, sub * 128:(sub + 1) * 128],
                                                 identf[0:65, 0:65]), r=[oT, identf], w=[TP])
            k.op("dve", lambda e: e.tensor_scalar(
                out=zt[:, 0:4], in0=TP[:, 0:260].rearrange("p (s c) -> p s c", c=65)[:, :, 64],
                scalar1=1e-30, scalar2=None, op0=ALU.max), r=[TP], w=[zt])
            k.op("dve", lambda e: e.reciprocal(out=zt[:, 0:4], in_=zt[:, 0:4]), r=[zt], w=[zt])
            k.op("dve", lambda e: e.tensor_tensor(out=coef[:, 0:4], in0=zt[:, 0:4],
                                                  in1=gate[:, :, r * 3 + br], op=ALU.mult),
                 r=[zt, gate], w=[coef])
            for sub in range(4):
                src = TP[:, sub * 65:sub * 65 + 64]
                if debug:
                    k.op("dve", lambda e: e.tensor_scalar(out=ydbg[:, br, sub, r, :], in0=src,
                                                          scalar1=zt[:, sub:sub + 1], scalar2=None,
                                                          op0=ALU.mult), r=[TP, zt], w=[ydbg])
                if br == 0:
                    k.op("dve", lambda e: e.tensor_scalar(out=yj[:, sub, r, :], in0=src,
                                                          scalar1=coef[:, sub:sub + 1], scalar2=None,
                                                          op0=ALU.mult), r=[TP, coef], w=[yj])
                else:
                    k.op("dve", lambda e: e.scalar_tensor_tensor(out=yj[:, sub, r, :], in0=src,
                                                                 scalar=coef[:, sub:sub + 1],
                                                                 in1=yj[:, sub, r, :], op0=ALU.mult,
                                                                 op1=ALU.add), r=[TP, coef, yj], w=[yj])

    load_j(0)
    for j in range(NJ):
        b = j % 2
        if j + 1 < NJ:
            load_j(j + 1)
        qj = q_j[b]
        k.op("dve", lambda e: e.tensor_tensor(out=gate[:], in0=gl_j[b][:],
                                              in1=bg[:].unsqueeze(1).to_broadcast([128, 4, 12]), op=ALU.add),
             r=[gl_j[b], bg], w=[gate])
        k.op("act", lambda e: e.activation(out=gate[:], in_=gate[:], func=AF.Exp, scale=-1.0),
             r=[gate], w=[gate])
        k.op("dve", lambda e: e.tensor_scalar(out=gate[:], in0=gate[:], scalar1=1.0, scalar2=None,
                                              op0=ALU.add), r=[gate], w=[gate])
        k.op("dve", lambda e: e.reciprocal(out=gate[:], in_=gate[:]), r=[gate], w=[gate])

        W = 128 * (j // 2 + 1)
        mw = min(256, W)
        for sub in range(4):
            k.op("pool", lambda e: e.memset(pcs[:], 0.0), w=[pcs])
            for r in range(4):
                for c0 in range(0, W, 512):
                    c1 = min(W, c0 + 512)
                    m0 = max(c0, W - mw)
                    has_mask = m0 < c1
                    k.op("pe", lambda e: e.matmul(MS[:, c0:c1], lhsT=qj[(r % 2) * 64:(r % 2) * 64 + 64, r // 2, sub * 128:(sub + 1) * 128],
                                                  rhs=kcT[(r % 2) * 64:(r % 2) * 64 + 64, c0:c1], start=True, stop=not has_mask),
                         r=[qj, kcT], w=[MS])
                    if has_mask:
                        k.op("pe", lambda e: e.matmul(MS[:, m0:c1], lhsT=ident[:],
                                                      rhs=cmtok_j[b][:, sub, m0 - (W - mw):c1 - (W - mw)],
                                                      start=False, stop=True),
                             r=[ident, cmtok_j[b]], w=[MS])
                k.op("act", lambda e: e.activation(out=ee[:, 0:W], in_=MS[:, 0:W], func=AF.Exp, scale=SCALE,
                                                   accum_out=Zc[:, r:r + 1]), r=[MS], w=[ee, Zc])
                k.op("dve", lambda e: e.tensor_scalar(out=Zc[:, r:r + 1], in0=Zc[:, r:r + 1], scalar1=1e-30,
                                                      scalar2=None, op0=ALU.max), r=[Zc], w=[Zc])
                k.op("dve", lambda e: e.reciprocal(out=Zc[:, r:r + 1], in_=Zc[:, r:r + 1]), r=[Zc], w=[Zc])
                k.op("dve", lambda e: e.scalar_tensor_tensor(out=pcs[:, 1:1 + W], in0=ee[:, 0:W],
                                                             scalar=Zc[:, r:r + 1], in1=pcs[:, 1:1 + W],
                                                             op0=ALU.mult, op1=ALU.add),
                     r=[ee, Zc, pcs], w=[pcs])
            k.op("dve", lambda e: e.tensor_reduce(out=imp[:], in_=pcs[:, 0:1024].rearrange("p (s c) -> p s c", c=4),
                                                  axis=AX.X, op=ALU.add), r=[pcs], w=[imp])
            k.op("dve", lambda e: e.tensor_tensor(out=imp[:], in0=imp[:], in1=pcs[:, 4:1028:4], op=ALU.add),
                 r=[imp, pcs], w=[imp])
            k.op("dve", lambda e: e.tensor_tensor(out=score[:], in0=imp[:], in1=sA_j[b][:, sub, :], op=ALU.mult),
                 r=[imp, sA_j[b]], w=[score])
            k.op("dve", lambda e: e.tensor_tensor(out=score[:], in0=score[:], in1=sB_j[b][:, sub, :], op=ALU.add),
                 r=[score, sB_j[b]], w=[score])
            k.op("dve", lambda e: e.max(out=m8[:], in_=score[:]), r=[score], w=[m8])
            k.op("dve", lambda e: e.match_replace(out=sc2[:], in_to_replace=m8[:], in_values=score[:],
                                                  imm_value=-2.0), r=[m8, score], w=[sc2])
            k.op("dve", lambda e: e.max(out=m8[:], in_=sc2[:]), r=[sc2], w=[m8])
            k.op("dve", lambda e: e.tensor_scalar(out=thr[:], in0=m8[:, 7:8], scalar1=0.0, scalar2=None,
                                                  op0=ALU.max), r=[m8], w=[thr])
            k.op("dve", lambda e: e.tensor_scalar(out=bq[:], in0=score[:], scalar1=thr[:, 0:1], scalar2=None,
                                                  op0=ALU.is_ge), r=[score, thr], w=[bq])
            MSb = MS2[:].bitcast(BF16)
            for ch in range(2):
                k.op("pe", lambda e: e.transpose(MSb[:, ch * 128:(ch + 1) * 128], bq[:, ch * 128:(ch + 1) * 128],
                                                 ident[:]), r=[bq, ident], w=[MS2])
            k.op("act", lambda e: e.activation(
                out=biasT[:, :, sub * 128:(sub + 1) * 128],
                in_=MSb[:, 0:256].rearrange("p (a c) -> p a c", a=2), func=AF.Copy),
                r=[MS2], w=[biasT])

        items = []
        for m in range(j // 2 + 1):
            masks = []
            if m >= j // 2 - 1:
                masks.append((ident[:], cmT_j[b][:, m - (j // 2 - 1), :], [ident, cmT_j[b]]))
            items.append(((lambda h0, m=m: kcT[h0:h0 + 64, m * 128:(m + 1) * 128]), kcT, vc1[:, m, :], vc1, 0, 3, masks, None))
        branch(j, 0, items)
        items = []
        for kt in range(8 * j + 8):
            masks = []
            if kt >= 8 * j:
                masks.append((ident[:], diagm[:, kt - 8 * j, :], [ident, diagm]))
            mm = (G[:, (kt % 64) * 128:(kt % 64 + 1) * 128], biasT[:, kt // 64, :], [G, biasT])
            items.append(((lambda h0, kt=kt: ksT[h0:h0 + 64, kt * 128:(kt + 1) * 128]), ksT, vs1[:, kt, :], vs1, 0, 3, masks, mm))
        branch(j, 1, items)
        items = []
        for kt in range(8):
            masks = [(ident[:], winm[:, kt, :], [ident, winm])]
            items.append(((lambda h0, kt=kt, b=b: kw_j[b][h0:h0 + 64, kt * 128:(kt + 1) * 128]), kw_j[b], vw_j[b][:, kt, :], vw_j[b],
                          (0 if kt == 0 else max(0, kt - 4)), (3 if kt == 0 else min(3, kt)), masks, None))
        branch(j, 2, items)
        k.dma(y_d[j, :, :], y_j[b][:].rearrange("p a b c -> p (a b c)"), r=[y_j[b]])
        if debug:
            k.dma(ydbg_d[j, :, :], ydbg[:].rearrange("p e a b c -> p (e a b c)"), r=[ydbg])
    k.finish()
    return k.nc


def _b_consts(p):
    c = {}
    c["G"] = (np.arange(8192)[None, :] // 64 == np.arange(128)[:, None]).astype(NPBF)
    c["ident"] = np.eye(128, dtype=np.float32).astype(NPBF)
    c["identf"] = np.eye(128, dtype=np.float32)
    kk = np.arange(128)[:, None, None]
    ii = np.arange(8)[None, :, None]
    qq = np.arange(512)[None, None, :]
    c["diagm"] = np.where(128 * ii + kk <= 512 * p + qq, 0.0, NEGM).astype(NPBF).reshape(128, 4096)
    kp = 128 * ii + kk
    qp = 512 + qq
    c["winm"] = np.where((kp <= qp) & (kp > qp - 512), 0.0, NEGM).astype(NPBF).reshape(128, 4096)
    cmT = np.zeros((NJ, 128, 2, 512), np.float32)
    cmtok = np.zeros((NJ, 128, 4, 256), np.float32)
    sA = np.zeros((NJ, 128, 4, 256), np.float32)
    sB = np.zeros((NJ, 128, 4, 256), np.float32)
    for j in range(NJ):
        gq = 2 * j + p
        for i in range(2):
            mt = j // 2 - 1 + i
            if mt < 0:
                continue
            n = 128 * mt + np.arange(128)[:, None]
            t = 512 * gq + np.arange(512)[None, :]
            cmT[j, :, i, :] = np.where((16 * n + 31 <= t) & (n <= 1022), 0.0, NEGM)
        W = 128 * (j // 2 + 1)
        mw = min(256, W)
        t = 512 * gq + 128 * np.arange(4)[None, :, None] + np.arange(128)[:, None, None]
        n = (W - mw) + np.arange(mw)[None, None, :]
        cmtok[j, :, :, 0:mw] = np.where((16 * n + 31 <= t) & (n <= 1022), 0.0, NEGM)
        s = np.arange(256)[None, None, :]
        cur = t // 64
        forced = (s == 0) | (s == cur) | (s == cur - 1)
        valid = 64 * s <= t
        sA[j] = (valid & ~forced).astype(np.float32)
        sB[j] = np.where(forced, 1e4, np.where(valid, 0.0, -1.0))
    c["cmT"] = cmT.astype(NPBF).reshape(NJ, 128, 1024)
    c["cmtok"] = cmtok.astype(NPBF).reshape(NJ, 128, 1024)
    c["sA"] = sA.reshape(NJ, 128, 1024)
    c["sB"] = sB.reshape(NJ, 128, 1024)
    return c


def run_B(qkv, glin, b_gate_l, cmp_pos_l, cmp_w1_l, cmp_w2_l, debug=False):
    nc = build_B(debug)
    qkv = qkv.reshape(BATCH, SEQ, 1280)
    glin = glin.reshape(BATCH, SEQ, 8, 3)
    consts = [_b_consts(0), _b_consts(1)]
    posT = np.ascontiguousarray(cmp_pos_l.transpose(2, 0, 1).reshape(64, 64)).astype(np.float32)
    w1 = np.ascontiguousarray(cmp_w1_l.reshape(2, 32, 64, 256).transpose(0, 2, 1, 3).reshape(2, 64, 32 * 256))
    w2 = np.ascontiguousarray(cmp_w2_l.reshape(2, 2, 128, 64).transpose(2, 0, 1, 3).reshape(128, 256))
    ones = np.ones((SEQ, 1), NPBF)
    in_maps = []
    for core in range(NCORES):
        b, g, p = core // 4, (core // 2) % 2, core % 2
        qk = qkv[b]
        tiles = [2 * j + p for j in range(NJ)]
        q = qk[:, g * 256:(g + 1) * 256].reshape(32, 512, 4, 64)[tiles]
        q5 = q.reshape(NJ, 512, 2, 2, 64)
        qT = np.ascontiguousarray(q5.transpose(0, 3, 4, 2, 1)).reshape(NJ, 128, 1024)
        sl = lambda o: qk[:, o + g * 64: o + (g + 1) * 64]
        kc, vc, ks, vs, kw, vw = sl(512), sl(640), sl(768), sl(896), sl(1024), sl(1152)
        pad16 = np.zeros((64, 16), NPBF)
        kcr = np.concatenate([np.ascontiguousarray(kc.T), pad16], axis=1)
        vcr = np.concatenate([np.ascontiguousarray(vc.T), pad16], axis=1)
        ksT = np.ascontiguousarray(ks.T)
        vs1 = np.concatenate([vs, ones], axis=1).reshape(128, 128, 65).transpose(1, 0, 2)
        vs1 = np.ascontiguousarray(vs1).reshape(128, 128 * 65)
        kwp = np.concatenate([np.zeros((512, 64), NPBF), kw], axis=0)
        vwp = np.concatenate([np.zeros((512, 65), NPBF), np.concatenate([vw, ones], axis=1)], axis=0)
        kwT = np.stack([kwp[512 * gq: 512 * gq + 1024].T for gq in tiles])
        vw1 = np.stack([vwp[512 * gq: 512 * gq + 1024].reshape(8, 128, 65).transpose(1, 0, 2).reshape(128, 520)
                        for gq in tiles])
        gl = glin[b][:, g * 4:(g + 1) * 4, :].reshape(32, 4, 128, 12)[tiles]
        gl = np.ascontiguousarray(gl.transpose(0, 2, 1, 3)).reshape(NJ, 128, 48)
        bg = np.ascontiguousarray(np.broadcast_to(b_gate_l.reshape(8, 3)[g * 4:(g + 1) * 4].reshape(1, 12), (128, 12)))
        m = {"qT": qT, "ksT": ksT, "vs1": vs1, "kwT": np.ascontiguousarray(kwT), "vw1": np.ascontiguousarray(vw1),
             "kcr": np.ascontiguousarray(kcr), "vcr": np.ascontiguousarray(vcr), "posT": posT, "w1": w1, "w2": w2,
             "glin": gl.astype(np.float32), "bg": bg.astype(np.float32)}
        m.update(consts[p])
        in_maps.append(m)
    res = _run(nc, in_maps, 'B')
    y = np.zeros((BATCH, 32, 512, 2, 256), np.float32)
    for core in range(NCORES):
        b, g, p = core // 4, (core // 2) % 2, core % 2
        yc = res.results[core]["y"].reshape(NJ, 128, 4, 256).transpose(0, 2, 1, 3).reshape(NJ, 512, 256)
        y[b, p::2, :, g, :] = yc
    if debug:
        yd = np.zeros((BATCH, 32, 512, 3, 2, 256), np.float32)
        for core in range(NCORES):
            b, g, p = core // 4, (core // 2) % 2, core % 2
            yc = res.results[core]["ydbg"].reshape(NJ, 128, 3, 4, 256).transpose(0, 3, 1, 2, 4).reshape(NJ, 512, 3, 256)
            yd[b, p::2, :, :, g, :] = yc
        return y.reshape(BATCH * SEQ, 512), yd.reshape(BATCH * SEQ, 3, 512)
    return y.reshape(BATCH * SEQ, 512)


def build_C():
    k = K()
    uT_d = k.din("uT", [128, SEQ + 3], F32)
    cw_d = k.din("cw", [128, 4], F32)
    vec_d = k.din("vec", [128, 4], F32)
    wa_d = k.din("wa", [128, 128], F32)
    wi_d = k.din("wi", [128, 128], F32)
    y_d = k.dout("y", [128, SEQ], F32)
    cw = k.sb([128, 4], F32)
    vec = k.sb([128, 4], F32)
    wf = k.sb([128, 256], F32)
    wb = k.sb([128, 256], BF16)
    cs = k.sb([128, 4], F32)
    t1 = k.sb([128, 1], F32)
    k.dma(cw[:], cw_d[:, :], w=[cw])
    k.dma(vec[:], vec_d[:, :], w=[vec])
    k.dma(wf[:, 0:128], wa_d[:, :], w=[wf])
    k.dma(wf[:, 128:256], wi_d[:, :], w=[wf])
    k.op("dve", lambda e: e.tensor_copy(out=wb[:], in_=wf[:]), r=[wf], w=[wb])
    k.op("act", lambda e: e.activation(out=t1[:], in_=vec[:, 3:4], func=AF.Exp, scale=-1.0), r=[vec], w=[t1])
    k.op("dve", lambda e: e.tensor_scalar(out=t1[:], in0=t1[:], scalar1=1.0, scalar2=None, op0=ALU.add),
         r=[t1], w=[t1])
    k.op("act", lambda e: e.activation(out=t1[:], in_=t1[:], func=AF.Ln), r=[t1], w=[t1])
    k.op("dve", lambda e: e.tensor_scalar(out=cs[:, 0:1], in0=t1[:], scalar1=-8.0, scalar2=None, op0=ALU.mult),
         r=[t1], w=[cs])
    k.op("dve", lambda e: e.tensor_scalar(out=cs[:, 1:2], in0=t1[:], scalar1=-16.0, scalar2=None, op0=ALU.mult),
         r=[t1], w=[cs])
    k.op("dve", lambda e: e.tensor_scalar(out=cs[:, 2:4], in0=vec[:, 1:3], scalar1=-1.0, scalar2=None,
                                          op0=ALU.mult), r=[vec], w=[cs])
    NTL = SEQ // 512
    u = [k.sb([128, 515], F32) for _ in range(2)]
    xc_l = [k.sb([128, 512], F32) for _ in range(2)]
    xcb_l = [k.sb([128, 512], BF16) for _ in range(2)]
    rr_l = [k.sb([128, 512], F32) for _ in range(2)]
    ig_l = [k.sb([128, 512], F32) for _ in range(2)]
    aa_l = [k.sb([128, 512], F32) for _ in range(2)]
    om_l = [k.sb([128, 512], F32) for _ in range(2)]
    bt_l = [k.sb([128, 512], F32) for _ in range(2)]
    hh = [k.sb([128, 512], F32) for _ in range(2)]
    P4 = [k.ps([128, 512], F32) for _ in range(4)]
    for ti in range(NTL):
        ub = u[ti % 2]
        h = hh[ti % 2]
        xc, xcb, rr, ig = xc_l[ti % 2], xcb_l[ti % 2], rr_l[ti % 2], ig_l[ti % 2]
        aa, om, bt = aa_l[ti % 2], om_l[ti % 2], bt_l[ti % 2]
        P = P4[(ti % 2) * 2:(ti % 2) * 2 + 2]
        k.dma(ub[:], uT_d[:, ti * 512: ti * 512 + 515], w=[ub])
        k.op("dve", lambda e: e.tensor_scalar(out=xc[:], in0=ub[:, 0:512], scalar1=cw[:, 0:1], scalar2=vec[:, 0:1],
                                              op0=ALU.mult, op1=ALU.add), r=[ub, cw, vec], w=[xc])
        for kk in range(1, 4):
            k.op("dve", lambda e: e.scalar_tensor_tensor(out=xc[:], in0=ub[:, kk:kk + 512], scalar=cw[:, kk:kk + 1],
                                                         in1=xc[:], op0=ALU.mult, op1=ALU.add),
                 r=[ub, cw, xc], w=[xc])
        k.op("dve", lambda e: e.tensor_copy(out=xcb[:], in_=xc[:]), r=[xc], w=[xcb])
        for gi, dst in enumerate((rr, ig)):
            k.op("pe", lambda e: e.matmul(P[gi][:], lhsT=wb[:, gi * 128:(gi + 1) * 128], rhs=xcb[:],
                                          start=True, stop=True), r=[wb, xcb], w=[P[gi]])
            k.op("act", lambda e: e.activation(out=dst[:], in_=P[gi][:], func=AF.Exp, scale=-1.0,
                                               bias=cs[:, 2 + gi:3 + gi]), r=[P[gi], cs], w=[dst])
            k.op("dve", lambda e: e.tensor_scalar(out=dst[:], in0=dst[:], scalar1=1.0, scalar2=None,
                                                   op0=ALU.add), r=[dst], w=[dst])
            k.op("dve", lambda e: e.reciprocal(out=dst[:], in_=dst[:]), r=[dst], w=[dst])
        k.op("act", lambda e: e.activation(out=aa[:], in_=rr[:], func=AF.Exp, scale=cs[:, 0:1]),
             r=[rr, cs], w=[aa])
        k.op("act", lambda e: e.activation(out=om[:], in_=rr[:], func=AF.Exp, scale=cs[:, 1:2]),
             r=[rr, cs], w=[om])
        k.op("dve", lambda e: e.tensor_scalar(out=om[:], in0=om[:], scalar1=-1.0, scalar2=1.0,
                                               op0=ALU.mult, op1=ALU.add), r=[om], w=[om])
        k.op("dve", lambda e: e.tensor_scalar(out=om[:], in0=om[:], scalar1=1e-18, scalar2=None,
                                               op0=ALU.max), r=[om], w=[om])
        k.op("act", lambda e: e.activation(out=om[:], in_=om[:], func=AF.Ln), r=[om], w=[om])
        k.op("act", lambda e: e.activation(out=om[:], in_=om[:], func=AF.Exp, scale=0.5), r=[om], w=[om])
        k.op("dve", lambda e: e.tensor_tensor(out=bt[:], in0=ig[:], in1=xc[:], op=ALU.mult), r=[ig, xc], w=[bt])
        k.op("dve", lambda e: e.tensor_tensor(out=bt[:], in0=bt[:], in1=om[:], op=ALU.mult), r=[bt, om], w=[bt])
        if ti == 0:
            k.op("dve", lambda e: e.tensor_tensor_scan(out=h[:], data0=aa[:], data1=bt[:], initial=0.0,
                                                       op0=ALU.mult, op1=ALU.add), r=[aa, bt], w=[h])
        else:
            hp = hh[(ti - 1) % 2]
            k.op("dve", lambda e: e.tensor_tensor_scan(out=h[:], data0=aa[:], data1=bt[:], initial=hp[:, 511:512],
                                                       op0=ALU.mult, op1=ALU.add), r=[aa, bt, hp], w=[h])
        k.dma(y_d[:, ti * 512:(ti + 1) * 512], h[:], r=[h])
    k.finish()
    return k.nc


def run_C(u_lru, conv_w, conv_b, w_a, b_a, w_i, b_i, lam):
    nc = build_C()
    u = u_lru.reshape(BATCH, SEQ, 512)
    in_maps = []
    for core in range(NCORES):
        b, cq = core // 4, core % 4
        cs_ = slice(cq * 128, (cq + 1) * 128)
        uT = np.concatenate([np.zeros((128, 3), np.float32), np.ascontiguousarray(u[b][:, cs_].T)], axis=1)
        wa = np.zeros((128, 128), np.float32)
        wi = np.zeros((128, 128), np.float32)
        for hb in range(2):
            wa[hb * 64:(hb + 1) * 64, hb * 64:(hb + 1) * 64] = w_a[cq * 2 + hb]
            wi[hb * 64:(hb + 1) * 64, hb * 64:(hb + 1) * 64] = w_i[cq * 2 + hb]
        vec = np.stack([conv_b[cs_], b_a[cs_], b_i[cs_], lam[cs_]], axis=1).astype(np.float32)
        in_maps.append({"uT": np.ascontiguousarray(uT), "cw": np.ascontiguousarray(conv_w[:, cs_].T),
                        "vec": np.ascontiguousarray(vec), "wa": wa, "wi": wi})
    res = _run(nc, in_maps, 'C')
    y = np.zeros((BATCH, SEQ, 512), np.float32)
    for core in range(NCORES):
        b, cq = core // 4, core % 4
        y[b][:, cq * 128:(cq + 1) * 128] = res.results[core]["y"].T
    return y.reshape(BATCH * SEQ, 512)


def build_E(final):
    k = K()
    glu_d = k.din("gluT", [1024, TA + 30], F32)
    dww_d = k.din("dww", [128, 4 * 31], F32)
    vec_d = k.din("vec", [128, 12], F32)
    wp_d = k.din("wp", [512, 512], F32)
    yT_d = k.din("yT", [1024, TA], F32)
    zT_d = k.din("zT", [1536, TA], F32)
    wo_d = k.din("wo", [1536, 1024], F32)
    x_d = k.din("x", [TA, 1024], F32)
    fg_d = k.din("fg", [128, 1024], F32)
    o_d = k.dout("o", [TA, 1024], F32)

    Wo = k.sb([128, 12, 1024], BF16)
    Wp = k.sb([128, 4, 512], BF16)
    dww = k.sb([128, 4, 31], F32)
    vec = k.sb([128, 12], F32)
    fg = k.sb([128, 1024], F32)
    onesM = k.sb([128, 128], F32)
    xt = [k.sb([128, 1024], F32) for _ in range(2)]
    xo = [k.sb([128, 1024], F32) for _ in range(2)]
    k.dma(dww[:].rearrange("p a c -> p (a c)"), dww_d[:, :], w=[dww])
    k.dma(vec[:], vec_d[:, :], w=[vec])
    k.dma(fg[:], fg_d[:, :], w=[fg])
    k.op("dve", lambda e: e.memset(onesM[:], 1.0 / 512.0), w=[onesM])
    for c in range(12):
        st = xt[c % 2]
        k.dma(st[:], wo_d[c * 128:(c + 1) * 128, :], w=[st])
        k.op("act" if c % 2 == 0 else "dve",
             (lambda e: e.copy(out=Wo[:, c, :], in_=st[:])) if c % 2 == 0 else
             (lambda e: e.tensor_copy(out=Wo[:, c, :], in_=st[:])), r=[st], w=[Wo])
    for c in range(4):
        st = xt[c % 2]
        k.dma(st[:, 0:512], wp_d[c * 128:(c + 1) * 128, :], w=[st])
        k.op("dve", lambda e: e.tensor_copy(out=Wp[:, c, :], in_=st[:, 0:512]), r=[st], w=[Wp])

    ga_l = [k.sb([128, 542], F32) for _ in range(2)]
    gb_l = [k.sb([128, 542], F32) for _ in range(2)]
    uu_l = [k.sb([128, 542], F32) for _ in range(2)]
    cv_l = [k.sb([128, 4, 512], F32) for _ in range(2)]
    cvsq = k.sb([128, 4, 512], F32)
    mean = k.sb([128, 512], F32)
    m2 = k.sb([128, 512], F32)
    rstd = k.sb([128, 512], F32)
    un = k.sb([128, 512], F32)
    sT = k.sb([128, 4, 512], BF16)
    zt = k.sb([128, 12, 512], F32)
    yt = k.sb([128, 8, 512], F32)
    mixT_l = [k.sb([128, 12, 512], BF16) for _ in range(2)]
    sq = k.sb([128, 1024], F32)
    ss = k.sb([128, 1], F32)
    PM = k.ps([128, 512], F32)
    PV = k.ps([128, 512], F32)
    PW = [k.ps([128, 512], F32) for _ in range(2)]
    PO = [k.ps([128, 512], F32) for _ in range(2)]
    cnt = 0
    for ti in range(TA // 512):
        t0 = ti * 512
        cv = cv_l[ti % 2]
        mixT = mixT_l[ti % 2]
        k.dma(zt[:], zT_d[:, t0:t0 + 512].rearrange("(c p) t -> p c t", p=128), w=[zt])
        k.dma(yt[:], yT_d[:, t0:t0 + 512].rearrange("(c p) t -> p c t", p=128), w=[yt])
        for ch in range(4):
            ga, gb, uu = ga_l[ch % 2], gb_l[ch % 2], uu_l[ch % 2]
            k.dma(ga[:], glu_d[ch * 128:(ch + 1) * 128, t0:t0 + 542], w=[ga])
            k.dma(gb[:], glu_d[512 + ch * 128:512 + (ch + 1) * 128, t0:t0 + 542], w=[gb])
            k.op("act", lambda e: e.activation(out=gb[:], in_=gb[:], func=AF.Sigmoid), r=[gb], w=[gb])
            k.op("dve", lambda e: e.tensor_tensor(out=uu[:], in0=ga[:], in1=gb[:], op=ALU.mult),
                 r=[ga, gb], w=[uu])
            k.op("dve", lambda e: e.tensor_scalar(out=cv[:, ch, :], in0=uu[:, 0:512], scalar1=dww[:, ch, 0:1],
                                                  scalar2=vec[:, ch:ch + 1], op0=ALU.mult, op1=ALU.add),
                 r=[uu, dww, vec], w=[cv])
            for kk in range(1, 31):
                k.op("dve", lambda e: e.scalar_tensor_tensor(out=cv[:, ch, :], in0=uu[:, kk:kk + 512],
                                                             scalar=dww[:, ch, kk:kk + 1], in1=cv[:, ch, :],
                                                             op0=ALU.mult, op1=ALU.add), r=[uu, dww, cv], w=[cv])
        k.op("act", lambda e: e.activation(out=cvsq[:], in_=cv[:], func=AF.Square), r=[cv], w=[cvsq])
        for ch in range(4):
            k.op("pe", lambda e: e.matmul(PM[:], lhsT=onesM[:], rhs=cv[:, ch, :], start=(ch == 0), stop=(ch == 3)),
                 r=[onesM, cv], w=[PM])
        for ch in range(4):
            k.op("pe", lambda e: e.matmul(PV[:], lhsT=onesM[:], rhs=cvsq[:, ch, :], start=(ch == 0), stop=(ch == 3)),
                 r=[onesM, cvsq], w=[PV])
        k.op("act", lambda e: e.copy(out=mean[:], in_=PM[:]), r=[PM], w=[mean])
        k.op("dve", lambda e: e.tensor_tensor(out=m2[:], in0=mean[:], in1=mean[:], op=ALU.mult), r=[mean], w=[m2])
        k.op("dve", lambda e: e.tensor_tensor(out=rstd[:], in0=PV[:], in1=m2[:], op=ALU.subtract),
             r=[PV, m2], w=[rstd])
        k.op("dve", lambda e: e.tensor_scalar(out=rstd[:], in0=rstd[:], scalar1=0.0, scalar2=EPS, op0=ALU.max,
                                              op1=ALU.add), r=[rstd], w=[rstd])
        k.op("act", lambda e: e.activation(out=rstd[:], in_=rstd[:], func=AF.Ln), r=[rstd], w=[rstd])
        k.op("act", lambda e: e.activation(out=rstd[:], in_=rstd[:], func=AF.Exp, scale=-0.5), r=[rstd], w=[rstd])
        for ch in range(4):
            k.op("dve", lambda e: e.tensor_tensor(out=un[:], in0=cv[:, ch, :], in1=mean[:], op=ALU.subtract),
                 r=[cv, mean], w=[un])
            k.op("dve", lambda e: e.tensor_tensor(out=un[:], in0=un[:], in1=rstd[:], op=ALU.mult),
                 r=[un, rstd], w=[un])
            k.op("act", lambda e: e.activation(out=sT[:, ch, :], in_=un[:], func=AF.Silu,
                                               scale=vec[:, 4 + ch:5 + ch], bias=vec[:, 8 + ch:9 + ch]),
                 r=[un, vec], w=[sT])
        k.op("act", lambda e: e.activation(out=zt[:], in_=zt[:], func=AF.Silu), r=[zt], w=[zt])
        k.op("dve", lambda e: e.tensor_tensor(out=mixT[:, 0:8, :], in0=yt[:], in1=zt[:, 0:8, :], op=ALU.mult),
             r=[yt, zt], w=[mixT])
        for jc in range(4):
            pw = PW[jc % 2]
            for ic in range(4):
                k.op("pe", lambda e: e.matmul(pw[:], lhsT=Wp[:, ic, jc * 128:(jc + 1) * 128], rhs=sT[:, ic, :],
                                              start=(ic == 0), stop=(ic == 3)), r=[Wp, sT], w=[pw])
            k.op("dve", lambda e: e.tensor_tensor(out=mixT[:, 8 + jc, :], in0=pw[:], in1=zt[:, 8 + jc, :],
                                                  op=ALU.mult), r=[pw, zt], w=[mixT])
        for sub in range(4):
            r0 = t0 + sub * 128
            xb = xt[cnt % 2]
            ob = xo[cnt % 2]
            cnt += 1
            k.dma(xb[:], x_d[r0:r0 + 128, :], w=[xb])
            for nh in range(2):
                po = PO[nh]
                for kc in range(12):
                    k.op("pe", lambda e: e.matmul(po[:], lhsT=mixT[:, kc, sub * 128:(sub + 1) * 128],
                                                  rhs=Wo[:, kc, nh * 512:(nh + 1) * 512],
                                                  start=(kc == 0), stop=(kc == 11)), r=[mixT, Wo], w=[po])
                k.op("dve", lambda e: e.tensor_tensor(out=ob[:, nh * 512:(nh + 1) * 512], in0=po[:],
                                                      in1=xb[:, nh * 512:(nh + 1) * 512], op=ALU.add),
                     r=[po, xb], w=[ob])
            if final:
                k.op("dve", lambda e: e.tensor_tensor(out=sq[:], in0=ob[:], in1=ob[:], op=ALU.mult), r=[ob], w=[sq])
                k.op("dve", lambda e: e.tensor_reduce(out=ss[:], in_=sq[:], axis=AX.X, op=ALU.add), r=[sq], w=[ss])
                k.op("dve", lambda e: e.tensor_scalar(out=ss[:], in0=ss[:], scalar1=1.0 / D_MODEL, scalar2=EPS,
                                                      op0=ALU.mult, op1=ALU.add), r=[ss], w=[ss])
                k.op("act", lambda e: e.activation(out=ss[:], in_=ss[:], func=AF.Ln), r=[ss], w=[ss])
                k.op("act", lambda e: e.activation(out=ss[:], in_=ss[:], func=AF.Exp, scale=-0.5), r=[ss], w=[ss])
                k.op("dve", lambda e: e.scalar_tensor_tensor(out=ob[:], in0=ob[:], scalar=ss[:, 0:1], in1=fg[:],
                                                             op0=ALU.mult, op1=ALU.mult), r=[ob, ss, fg], w=[ob])
            k.dma(o_d[r0:r0 + 128, :], ob[:], r=[ob])
    k.finish()
    return k.nc


def run_E(final, rr, y_nsa, y_lru, x_l, dw_w, dw_b, ln_g, ln_b, w_pw2, w_out, final_g):
    nc = build_E(final)
    rr = rr.reshape(BATCH, SEQ, 3072)
    y_nsa = y_nsa.reshape(BATCH, SEQ, 512)
    y_lru = y_lru.reshape(BATCH, SEQ, 512)
    xf = x_l.reshape(BATCH, SEQ, D_MODEL)
    dww = np.ascontiguousarray(dw_w.T.reshape(4, 128, 31).transpose(1, 0, 2)).reshape(128, 124)
    v3 = lambda a: a.reshape(4, 128).T
    vec = np.ascontiguousarray(np.concatenate([v3(dw_b), v3(ln_g), v3(ln_b)], axis=1)).astype(np.float32)
    fg = np.ascontiguousarray(np.broadcast_to(final_g[None, :], (128, D_MODEL))).astype(np.float32)
    in_maps = []
    for core in range(NCORES):
        b, c = core // 4, core % 4
        sl = slice(c * TA, (c + 1) * TA)
        glu = rr[b][:, 1536:2560]
        gp = np.concatenate([np.zeros((30, 1024), np.float32), glu], axis=0)[c * TA: c * TA + TA + 30]
        z = np.concatenate([rr[b][sl, 0:512], rr[b][sl, 1024:1536], rr[b][sl, 2560:3072]], axis=1)
        yy = np.concatenate([y_nsa[b][sl], y_lru[b][sl]], axis=1)
        in_maps.append({"gluT": np.ascontiguousarray(gp.T), "dww": dww, "vec": vec,
                        "wp": np.ascontiguousarray(w_pw2), "yT": np.ascontiguousarray(yy.T),
                        "zT": np.ascontiguousarray(z.T), "wo": np.ascontiguousarray(w_out),
                        "x": np.ascontiguousarray(xf[b][sl]), "fg": fg})
    res = _run(nc, in_maps, 'E')
    out = np.concatenate([r["o"] for r in res.results], axis=0)
    return out.reshape(BATCH, SEQ, D_MODEL)


def kernel(x, positions, norm_g, w_in, b_gate, cmp_pos, cmp_w1, cmp_w2, lru_conv_w, lru_conv_b,
           lru_w_a, lru_b_a, lru_w_i, lru_b_i, lru_lam, conv_dw_w, conv_dw_b, conv_ln_g, conv_ln_b,
           conv_w_pw2, w_out, final_g):
    f = lambda a: np.asarray(a)
    x = f(x).astype(np.float32, copy=False)
    positions = f(positions)
    for l in range(DEPTH):
        qkv, glin, rr = run_A(x, positions, f(norm_g)[l], f(w_in)[l])
        y_nsa = run_B(qkv, glin, f(b_gate)[l], f(cmp_pos)[l], f(cmp_w1)[l], f(cmp_w2)[l])
        y_lru = run_C(np.ascontiguousarray(rr[:, 512:1024]), f(lru_conv_w)[l], f(lru_conv_b)[l], f(lru_w_a)[l],
                      f(lru_b_a)[l], f(lru_w_i)[l], f(lru_b_i)[l], f(lru_lam)[l])
        x = run_E(l == DEPTH - 1, rr, y_nsa, y_lru, x, f(conv_dw_w)[l], f(conv_dw_b)[l], f(conv_ln_g)[l],
                  f(conv_ln_b)[l], f(conv_w_pw2)[l], f(w_out)[l], f(final_g))
    return x
```
